# Optimizing a Trainium2 kernel written in Bass

```python
import math
import jax, jax.numpy as jnp
from jax import lax
import numpy as np


D_MODEL = 1024
BATCH = 8
SEQ = 2048
DEPTH = 2
DEC_BATCH = 128
DEC_SEQ = 8
PAST_LEN = 16384
PAGE_SIZE = 128

N_MIXERS = 2
N_A = (DEPTH + 1) // 2
N_B = DEPTH // 2
A_WIDTH = 2 * D_MODEL
A_CONV = 3
B_WIDTH = 2 * D_MODEL
B_CONV = 4
B_HEADS = 4
B_DK = B_WIDTH // B_HEADS
B_DV = B_WIDTH // B_HEADS
B_CHUNK = 64
RMS_EPS = 1e-6
LN_EPS = 1e-5

kernel_name = "hybrid_shortconv_mlstm_step"


def rmsnorm(x, w):
    xf = x.astype(jnp.float32)
    y = xf * lax.rsqrt(jnp.mean(xf * xf, axis=-1, keepdims=True) + RMS_EPS)
    return (y * w.astype(jnp.float32)).astype(x.dtype)


def causal_dwconv(u, buf, w):
    K = w.shape[0]
    T = u.shape[1]
    full = jnp.concatenate([buf.astype(u.dtype), u], axis=1)
    out = full[:, 0:T] * w[0]
    for j in range(1, K):
        out = out + full[:, j:j + T] * w[j]
    return out, full[:, full.shape[1] - (K - 1):]


def short_conv_mixer(h, buf, w_in, conv_w, w_out):
    b_gate, c_gate, xa, z = jnp.split(h @ w_in, 4, axis=-1)
    u = c_gate * xa
    conv, new_buf = causal_dwconv(u, buf, conv_w)
    y = b_gate * conv * jax.nn.silu(z)
    return y @ w_out, new_buf


def mlstm_chunkwise(q, k, v, ig, logf, C0, n0, m0):
    bsz, nh, t, _ = q.shape
    dv = v.shape[-1]
    L = math.gcd(t, B_CHUNK)
    nc = t // L

    def to_chunks(a):
        a = a.reshape((bsz, nh, nc, L) + a.shape[3:])
        return jnp.moveaxis(a, 2, 0)

    causal = jnp.tril(jnp.ones((L, L), dtype=bool))

    def step(carry, inp):
        C, n, m = carry
        qj, kj, vj, ij, fj = inp
        b = jnp.cumsum(fj, axis=-1)
        g = b[..., -1]
        logD = jnp.where(causal, b[..., :, None] - b[..., None, :] + ij[..., None, :], -jnp.inf)
        inter = b + m[..., None]
        m_row = jnp.maximum(inter, jnp.max(logD, axis=-1))
        w_inter = jnp.exp(inter - m_row)
        S = jnp.einsum('bhid,bhjd->bhij', qj, kj) * jnp.exp(logD - m_row[..., None])
        num = w_inter[..., None] * jnp.einsum('bhid,bhde->bhie', qj, C) + jnp.einsum('bhij,bhje->bhie', S, vj)
        den = w_inter * jnp.einsum('bhid,bhd->bhi', qj, n) + jnp.sum(S, axis=-1)
        h = num / jnp.maximum(jnp.abs(den), jnp.exp(-m_row))[..., None]
        logw = g[..., None] - b + ij
        m_new = jnp.maximum(g + m, jnp.max(logw, axis=-1))
        decay = jnp.exp(g + m - m_new)
        kw = kj * jnp.exp(logw - m_new[..., None])[..., None]
        C_new = decay[..., None, None] * C + jnp.einsum('bhjd,bhje->bhde', kw, vj)
        n_new = decay[..., None] * n + jnp.sum(kw, axis=2)
        return (C_new, n_new, m_new), h

    xs = (to_chunks(q), to_chunks(k), to_chunks(v), to_chunks(ig), to_chunks(logf))
    (C, n, m), hs = lax.scan(step, (C0, n0, m0), xs)
    hs = jnp.moveaxis(hs, 0, 2).reshape(bsz, nh, t, dv)
    return hs, C, n, m


def mlstm_mixer(h, conv_buf, C0, n0, m0, w_in, conv_w, conv_b, w_q, w_k, w_v, w_if, b_if, skip, onorm_w, w_out):
    bsz, t, _ = h.shape
    f32 = jnp.float32
    xm, z = jnp.split(h @ w_in, 2, axis=-1)
    xc, new_buf = causal_dwconv(xm, conv_buf, conv_w)
    xc = jax.nn.silu(xc + conv_b)
    q = xc @ w_q
    k = xc @ w_k
    v = xm @ w_v
    gates = (jnp.concatenate([q, k, v], axis=-1) @ w_if + b_if).astype(f32)
    ig = gates[..., :B_HEADS].transpose(0, 2, 1)
    logf = jax.nn.log_sigmoid(gates[..., B_HEADS:]).transpose(0, 2, 1)

    def heads(a, d):
        return a.reshape(bsz, t, B_HEADS, d).transpose(0, 2, 1, 3).astype(f32)

    hh, C, n, m = mlstm_chunkwise(heads(q, B_DK), heads(k, B_DK) * (B_DK ** -0.5), heads(v, B_DV),
                                  ig, logf, C0.astype(f32), n0.astype(f32), m0.astype(f32))
    mu = jnp.mean(hh, axis=-1, keepdims=True)
    var = jnp.mean(jnp.square(hh - mu), axis=-1, keepdims=True)
    hn = ((hh - mu) * lax.rsqrt(var + LN_EPS)).transpose(0, 2, 1, 3).reshape(bsz, t, B_WIDTH)
    out = ((hn * onorm_w.astype(f32)).astype(h.dtype) + skip * xc) * jax.nn.silu(z)
    return out @ w_out, new_buf, C, n, m


def run_trunk(x, conv_a, conv_b, C, n, m, norm_w, final_norm_w, a_w_in, a_conv_w, a_w_out,
              b_w_in, b_conv_w, b_conv_b, b_w_q, b_w_k, b_w_v, b_w_if, b_b_if, b_skip, b_onorm_w, b_w_out):
    new_a, new_b, new_C, new_n, new_m = [], [], [], [], []
    for layer in range(DEPTH):
        hnorm = rmsnorm(x, norm_w[layer])
        j = layer // N_MIXERS
        if layer % N_MIXERS == 0:
            out, buf = short_conv_mixer(hnorm, conv_a[j], a_w_in[j], a_conv_w[j], a_w_out[j])
            new_a.append(buf)
        else:
            out, buf, Cj, nj, mj = mlstm_mixer(hnorm, conv_b[j], C[j], n[j], m[j], b_w_in[j], b_conv_w[j],
                                               b_conv_b[j], b_w_q[j], b_w_k[j], b_w_v[j], b_w_if[j], b_b_if[j],
                                               b_skip[j], b_onorm_w[j], b_w_out[j])
            new_b.append(buf)
            new_C.append(Cj)
            new_n.append(nj)
            new_m.append(mj)
        x = x + out
    y = rmsnorm(x, final_norm_w)
    return y, jnp.stack(new_a), jnp.stack(new_b), jnp.stack(new_C), jnp.stack(new_n), jnp.stack(new_m)


def setup_inputs(seed: int = 0) -> dict:
    key = jax.random.key(seed)
    ks = jax.random.split(key, 24)
    f32 = jnp.float32

    def nrm(k, shape, scale):
        return jax.random.normal(k, shape, f32) * scale

    ig_bias = nrm(ks[19], (N_B, B_HEADS), 0.1)
    fg_bias = jnp.linspace(3.0, 6.0, B_HEADS, dtype=f32)[None, :] + nrm(ks[20], (N_B, B_HEADS), 0.01)
    return {
        'x_prompt': nrm(ks[0], (BATCH, SEQ, D_MODEL), 1.0),
        'x_sample': nrm(ks[1], (DEC_BATCH, DEC_SEQ, D_MODEL), 1.0),
        'state_conv_a': nrm(ks[2], (N_A, DEC_BATCH, A_CONV - 1, A_WIDTH), 1.0),
        'state_conv_b': nrm(ks[3], (N_B, DEC_BATCH, B_CONV - 1, B_WIDTH), 1.0),
        'state_C': nrm(ks[4], (N_B, DEC_BATCH, B_HEADS, B_DK, B_DV), B_DK ** -0.5),
        'state_n': nrm(ks[5], (N_B, DEC_BATCH, B_HEADS, B_DK), B_DK ** -0.5),
        'state_m': nrm(ks[6], (N_B, DEC_BATCH, B_HEADS), 1.0),
        'norm_w': 1.0 + nrm(ks[7], (DEPTH, D_MODEL), 0.02),
        'final_norm_w': 1.0 + nrm(ks[8], (D_MODEL,), 0.02),
        'a_w_in': nrm(ks[9], (N_A, D_MODEL, 4 * A_WIDTH), D_MODEL ** -0.5),
        'a_conv_w': nrm(ks[10], (N_A, A_CONV, A_WIDTH), A_CONV ** -0.5),
        'a_w_out': nrm(ks[11], (N_A, A_WIDTH, D_MODEL), A_WIDTH ** -0.5),
        'b_w_in': nrm(ks[12], (N_B, D_MODEL, 2 * B_WIDTH), D_MODEL ** -0.5),
        'b_conv_w': nrm(ks[13], (N_B, B_CONV, B_WIDTH), B_CONV ** -0.5),
        'b_conv_b': nrm(ks[14], (N_B, B_WIDTH), 0.02),
        'b_w_q': nrm(ks[15], (N_B, B_WIDTH, B_HEADS * B_DK), B_WIDTH ** -0.5),
        'b_w_k': nrm(ks[16], (N_B, B_WIDTH, B_HEADS * B_DK), B_WIDTH ** -0.5),
        'b_w_v': nrm(ks[17], (N_B, B_WIDTH, B_HEADS * B_DV), B_WIDTH ** -0.5),
        'b_w_if': nrm(ks[18], (N_B, 3 * B_WIDTH, 2 * B_HEADS), (3 * B_WIDTH) ** -0.5),
        'b_b_if': jnp.concatenate([ig_bias, fg_bias], axis=-1),
        'b_skip': 1.0 + nrm(ks[21], (N_B, B_WIDTH), 0.02),
        'b_onorm_w': 1.0 + nrm(ks[22], (N_B, B_WIDTH), 0.02),
        'b_w_out': nrm(ks[23], (N_B, B_WIDTH, D_MODEL), B_WIDTH ** -0.5),
    }


def reference(x_prompt, x_sample, state_conv_a, state_conv_b, state_C, state_n, state_m,
              norm_w, final_norm_w, a_w_in, a_conv_w, a_w_out,
              b_w_in, b_conv_w, b_conv_b, b_w_q, b_w_k, b_w_v, b_w_if, b_b_if, b_skip, b_onorm_w, b_w_out):
    f32 = jnp.float32
    pb = x_prompt.shape[0]
    zc_a = jnp.zeros((N_A, pb, A_CONV - 1, A_WIDTH), x_prompt.dtype)
    zc_b = jnp.zeros((N_B, pb, B_CONV - 1, B_WIDTH), x_prompt.dtype)
    zC = jnp.zeros((N_B, pb, B_HEADS, B_DK, B_DV), f32)
    zn = jnp.zeros((N_B, pb, B_HEADS, B_DK), f32)
    zm = jnp.zeros((N_B, pb, B_HEADS), f32)
    y_prompt, ca_p, cb_p, C_p, n_p, m_p = run_trunk(
        x_prompt, zc_a, zc_b, zC, zn, zm, norm_w, final_norm_w, a_w_in, a_conv_w, a_w_out,
        b_w_in, b_conv_w, b_conv_b, b_w_q, b_w_k, b_w_v, b_w_if, b_b_if, b_skip, b_onorm_w, b_w_out)
    y_sample, ca_s, cb_s, C_s, n_s, m_s = run_trunk(
        x_sample, state_conv_a, state_conv_b, state_C, state_n, state_m, norm_w, final_norm_w,
        a_w_in, a_conv_w, a_w_out, b_w_in, b_conv_w, b_conv_b, b_w_q, b_w_k, b_w_v, b_w_if, b_b_if,
        b_skip, b_onorm_w, b_w_out)
    return (y_prompt, y_sample, ca_p, ca_s, cb_p, cb_s, C_p, C_s, n_p, n_s, m_p, m_s)
```

```python
import math
import contextlib
import numpy as np
import concourse.bass as bass
import concourse.mybir as mybir
from concourse.bass_utils import run_bass_kernel_spmd

F32 = mybir.dt.float32
BF16 = mybir.dt.bfloat16
AF = mybir.ActivationFunctionType
ALU = mybir.AluOpType

NCORES = 8
DM = 1024
AW = 2048
NH = 4
DK = 512
RMS_EPS = 1e-6
LN_EPS = 1e-5
KSCALE = DK ** -0.5
NITEMS = 112
NSLOTS = 4
ITEM = 2048

PC_CONVA, PC_CONVB, PC_CB, PC_SKIP, PC_ONORM = 0, 48, 112, 128, 144
PC_IDENT = 160
PC_NEG = 288
PC_EH = 352
PC_EYE4 = 608
PC_ONES = 612
PC_SEL = 740
PC_BIF = 744
PC_RMASK = 745
PC_AMASK = 1257
PC_RMASK_S = 1769
PC_AMASK_S = 1897
NPAR = 2025
CB_IDENT, CB_ONES, CB_WIF = 0, 128, 136
NCB = 520


class Op:
    __slots__ = ("eng", "fn", "deps", "needs_inc", "val", "is_dma", "sem", "dma_val")

    def __init__(self, eng, fn, is_dma=False):
        self.eng = eng
        self.fn = fn
        self.deps = []
        self.needs_inc = False
        self.val = None
        self.is_dma = is_dma
        self.sem = None
        self.dma_val = None


class Prog:
    ENGS = ("pe", "act", "dve", "pool", "sp")

    def __init__(self, nc, n_dma_sems=32):
        self.nc = nc
        self.eng = {"pe": nc.tensor, "act": nc.scalar, "dve": nc.vector, "pool": nc.gpsimd, "sp": nc.sync}
        self.ops = []
        self.last_w = {}
        self.readers = {}
        self.n_dma_sems = n_dma_sems
        self.dma_rr = 0
        self.dma_rr_sw = 0
        self.dma_sem_last = [None] * n_dma_sems
        self.dma_sem_count = [0] * n_dma_sems
        self.out_dmas = []

    def _add_dep(self, op, d):
        if d is None or d is op:
            return
        if d.eng == op.eng and op.eng == "pe" and not d.is_dma and not op.is_dma:
            return
        op.deps.append(d)
        if not d.is_dma:
            d.needs_inc = True

    def op(self, eng, fn, reads=(), writes=(), is_dma=False, is_out=False):
        o = Op(eng, fn, is_dma)
        for t in reads:
            self._add_dep(o, self.last_w.get(t))
        for t in writes:
            self._add_dep(o, self.last_w.get(t))
            for r in self.readers.get(t, ()):
                if r.eng == eng and not r.is_dma and not is_dma:
                    continue
                self._add_dep(o, r)
        for t in reads:
            self.readers.setdefault(t, []).append(o)
        for t in writes:
            self.last_w[t] = o
            self.readers[t] = []
        if is_dma:
            if eng == "pool":
                s = self.dma_rr_sw
                self.dma_rr_sw = (self.dma_rr_sw + 1) % 8
            else:
                s = 8 + self.dma_rr
                self.dma_rr = (self.dma_rr + 1) % (self.n_dma_sems - 8)
            prev = self.dma_sem_last[s]
            if prev is not None:
                o.deps.append(prev)
            self.dma_sem_count[s] += 16
            o.sem = s
            o.dma_val = self.dma_sem_count[s]
            self.dma_sem_last[s] = o
            if is_out:
                self.out_dmas.append(o)
        self.ops.append(o)
        return o

    def dma(self, eng, out, in_, reads=(), writes=(), is_out=False):
        return self.op(eng, lambda e: e.dma_start(out=out, in_=in_), reads, writes, is_dma=True, is_out=is_out)

    def emit(self, sems, dma_sems):
        cnt = {e: 0 for e in self.ENGS}
        for o in self.ops:
            if o.needs_inc and not o.is_dma:
                cnt[o.eng] += 1
                o.val = cnt[o.eng]
        waited = {e: {} for e in self.ENGS}
        import os
        maxops = int(os.environ.get("MK_MAXOPS", "0"))
        if maxops:
            self.ops = self.ops[:maxops]
            self.out_dmas = [o for o in self.out_dmas if o in set(self.ops)]
        for o in self.ops:
            e = self.eng[o.eng]
            w = waited[o.eng]
            need = {}
            for d in o.deps:
                if d.is_dma:
                    key, v = ("d", d.sem), d.dma_val
                else:
                    key, v = ("e", d.eng), d.val
                if need.get(key, 0) < v:
                    need[key] = v
            for key, v in need.items():
                if w.get(key, 0) >= v:
                    continue
                w[key] = v
                e.wait_ge(dma_sems[key[1]] if key[0] == "d" else sems[key[1]], v)
            inst = o.fn(e)
            if o.is_dma:
                inst.then_inc(dma_sems[o.sem], 16)
            elif o.needs_inc:
                inst.then_inc(sems[o.eng], 1)
        e = self.eng["sp"]
        fin = {}
        for o in self.out_dmas:
            fin[o.sem] = max(fin.get(o.sem, 0), o.dma_val)
        for s, v in fin.items():
            e.wait_ge(dma_sems[s], v)


def build_program(n_ptiles=4, do_sample=True, dbg=False):
    nc = bass.Bass("TRN2", target_bir_lowering=False)

    def din(name, shape):
        return nc.dram_tensor(name, shape, F32, kind="ExternalInput").ap()

    def dout(name, shape):
        return nc.dram_tensor(name, shape, F32, kind="ExternalOutput").ap()

    xp = din("xp", [2048, DM])
    xs = din("xs", [128, DM])
    sca = din("sca", [32, AW])
    scb = din("scb", [48, AW])
    sC = din("sC", [64, 512, 512])
    sn = din("sn", [256, 128])
    sm = din("sm", [4, 16])
    wst = din("wst", [NITEMS, 128, ITEM])
    par_d = din("par", [128, NPAR])
    cbf_d = din("cbf", [128, NCB])
    bc_d = din("bc", [3, 128, DM])

    yp = dout("yp", [2048, DM])
    ys = dout("ys", [128, DM])
    cap = dout("cap", [2, AW])
    cas = dout("cas", [32, AW])
    cbp = dout("cbp", [3, AW])
    cbs = dout("cbs", [48, AW])
    Cp = dout("Cp", [4, 512, 512])
    Cs = dout("Cs", [64, 512, 512])
    np_o = dout("np", [16, 128])
    ns_o = dout("ns", [256, 128])
    mp_o = dout("mp", [4, 1])
    ms_o = dout("ms", [4, 16])

    es = contextlib.ExitStack()
    with es:
        AR_WORDS = 52600
        arena = es.enter_context(nc.sbuf_tensor("arena", [128, AR_WORDS], F32))
        psb = [es.enter_context(nc.psum_tensor(f"psb{i}", [128, 512], F32)) for i in range(8)]
        sems = {e: es.enter_context(nc.semaphore(f"sem_{e}")) for e in Prog.ENGS}
        dsems = [es.enter_context(nc.semaphore(f"dsem{i}")) for i in range(32)]
        p = Prog(nc)
        A = p.op

        cur = [0]
        PAGE = 32

        class V:
            __slots__ = ("ap", "tok")

            def __init__(self, ap, tok):
                self.ap = ap
                self.tok = tok

        def alloc(words):
            words = (words + PAGE - 1) // PAGE * PAGE
            o = cur[0]
            cur[0] += words
            assert cur[0] <= AR_WORDS, f"arena overflow {cur[0]}"
            return o

        def view(off, words, dtype=F32, parts=128):
            ap = arena[:parts, off:off + words]
            if dtype != F32:
                ap = ap.bitcast(dtype)
            toks = [("a", pg) for pg in range(off // PAGE, (off + words - 1) // PAGE + 1)]
            return V(ap, toks)

        o_w = alloc(NSLOTS * 1024)
        wslot = [view(o_w + i * 1024, 1024, BF16) for i in range(NSLOTS)]
        o_x = alloc(4096)
        Xv = view(o_x, 4096)
        o_ht = alloc(2048)
        HTv = view(o_ht, 2048, BF16)
        o_B = [alloc(4096) for _ in range(5)]
        o_cst = alloc(4 * 2048)
        CSTv = [view(o_cst + i * 2048, 2048) for i in range(4)]
        o_cbf = alloc(2 * 1024)
        CBFv = [view(o_cbf + i * 1024, 1024, BF16) for i in range(2)]
        o_par = alloc(NPAR)
        PARv = view(o_par, NPAR)
        PAR = PARv.ap
        o_cb = alloc(NCB // 2)
        CBv = view(o_cb, NCB // 2, BF16)
        CB = CBv.ap
        o_hist = alloc(416)
        HISTAv = view(o_hist, 32)
        HISTBv = view(o_hist + 32, 48)
        MALLv = view(o_hist + 96, 24, parts=4)
        NSTh = [view(o_hist + 128 + h * 32, 4) for h in range(4)]
        NBFh = [view(o_hist + 256 + h * 32, 2, BF16) for h in range(4)]
        NSTC = view(o_hist + 384, 16)
        o_g = alloc(512 * 2 + 192 + 64)
        CCv = view(o_g, 512, parts=4)
        NEGAv = view(o_g + 512, 512, parts=4)
        COLQv = view(o_g + 1024, 192, parts=64)
        DECBv = view(o_g + 1216, 64)
        o_scr = cur[0]
        SCR_WORDS = AR_WORDS - o_scr

        def scr(off, words, dtype=F32, parts=128):
            assert off + words <= SCR_WORDS, f"scratch overflow {off + words} > {SCR_WORDS}"
            return view(o_scr + off, words, dtype, parts)

        def dump(name, v, dtype):
            if not dbg:
                return
            shp = list(v.ap.shape)
            d = nc.dram_tensor(name, shp, dtype, kind="ExternalOutput").ap()
            p.dma("sp", d, v.ap, reads=v.tok, is_out=True)

        psrr = [0]

        def newps():
            i = psrr[0]
            psrr[0] = (i + 1) % 8
            return i

        def PT(i):
            return [("ps", i)]

        def psf(i):
            return psb[i][:]

        def psh(i):
            return psb[i][:].bitcast(BF16)

        wg = [0]
        wissued = [0]
        total_items = NITEMS * (n_ptiles + (1 if do_sample else 0))

        wbf = nc.dram_tensor("wbf_cache", [NITEMS, 128, ITEM], BF16, kind=("ExternalOutput" if dbg else "Internal")).ap()
        use_cache = total_items > NITEMS

        def w_issue_upto(g):
            while wissued[0] <= g and wissued[0] < total_items:
                gi = wissued[0]
                s = gi % NSLOTS
                if gi < NITEMS or not use_cache:
                    p.dma("pool", wslot[s].ap, wst[gi % NITEMS], writes=wslot[s].tok)
                else:
                    p.dma("pool", wslot[s].ap, wbf[gi % NITEMS], reads=[("wd", gi % NITEMS)], writes=wslot[s].tok)
                wissued[0] += 1

        def wget():
            g = wg[0]
            wg[0] += 1
            w_issue_upto(g + NSLOTS - 1)
            if use_cache and g < NITEMS:
                p.dma("sp", wbf[g], wslot[g % NSLOTS].ap, reads=wslot[g % NSLOTS].tok, writes=[("wd", g)])
            return wslot[g % NSLOTS]

        p.dma("sp", PAR, par_d, writes=PARv.tok)
        p.dma("pool", CB, cbf_d, writes=CBv.tok)
        ident_bf = CB[:, CB_IDENT:CB_IDENT + 128]
        ones_bf = CB[:, CB_ONES:CB_ONES + 8]
        wif_bf = CB[:, CB_WIF:CB_WIF + 384].rearrange("p (k g) -> p k g", g=8)
        ident_f = PAR[:, PC_IDENT:PC_IDENT + 128]
        CT = PARv.tok
        CBT = CBv.tok

        for h in range(4):
            A("pool", lambda e, h=h: e.memset(CSTv[h].ap, 0.0), writes=CSTv[h].tok)
        hist_all = view(o_hist, 416)
        A("pool", lambda e: e.memset(hist_all.ap, 0.0), writes=hist_all.tok)

        def run_tile(kind, ti):
            samp = kind == "s"
            T = 128 if samp else 512
            NB = T // 128
            L = 8 if samp else 64
            NCH = T // L
            HBA, HBB = 2, 3
            last_p = (not samp) and ti == n_ptiles - 1

            def Bview(i, words=None, dtype=BF16, off=0):
                return view(o_B[i] + off, words if words is not None else 16 * T // 2, dtype)

            def ctok(i, fc):
                cw = T // 2
                o0 = o_B[i] + fc * cw
                return [("a", pg) for pg in range(o0 // PAGE, (o0 + cw - 1) // PAGE + 1)]

            def fm(v):
                return v.ap.rearrange("p (c t) -> p c t", t=T)

            B1, B2, B3, B4, B5 = [Bview(i) for i in range(5)]
            HT = view(o_ht, 8 * T // 2, BF16)
            hT = HT.ap.rearrange("p (c t) -> p c t", t=T)
            X = view(o_x, NB * 1024)
            Xt = X.ap.rearrange("p (b f) -> p b f", f=DM)

            s_junk = scr(0, 512, BF16)
            s_hb = [scr(512, 512, BF16), scr(1024, 512, BF16)]
            s_ss = [scr(1536, 4), scr(1568, 4), scr(6784, 4), scr(6816, 4)]
            s_nw = scr(1600, 1024)
            s_xe = [scr(2624, 520), scr(3168, 520)]
            s_xa = [scr(3712, 512), scr(4224, 512)]
            s_sz = [scr(4736, 512), scr(5248, 512)]
            s_tt = scr(5760, 512)
            s_acc = scr(6272, 512)
            s_ost = [scr(2624, 1024), scr(3648, 1024)]

            if samp:
                p.dma("sp", Xt, xs.rearrange("(b p) f -> p b f", p=128), writes=X.tok)
            else:
                p.dma("sp", Xt, xp[ti * 512:(ti + 1) * 512, :].rearrange("(b p) f -> p b f", p=128), writes=X.tok)

            def load_nw(i):
                p.dma("sp", s_nw.ap, bc_d[i], writes=s_nw.tok)

            def rms_stats(b, k):
                ss = s_ss[k]
                A("act", lambda e: e.activation(out=s_junk.ap, in_=Xt[:, b, :], func=AF.Square, accum_out=ss.ap[:, 0:1]),
                  reads=X.tok, writes=s_junk.tok + ss.tok)
                A("act", lambda e: e.activation(out=ss.ap[:, 1:2], in_=ss.ap[:, 0:1], func=AF.Ln, scale=1.0 / DM, bias=RMS_EPS),
                  reads=ss.tok, writes=ss.tok)
                A("act", lambda e: e.activation(out=ss.ap[:, 2:3], in_=ss.ap[:, 1:2], func=AF.Exp, scale=-0.5),
                  reads=ss.tok, writes=ss.tok)

            def norm_to_hT():
                for b in range(NB):
                    rms_stats(b, b)
                for b in range(NB):
                    k = b % 2
                    hb = s_hb[k]
                    A("dve", lambda e, b=b, k=k, hb=hb: e.scalar_tensor_tensor(
                        out=hb.ap, in0=Xt[:, b, :], scalar=s_ss[b].ap[:, 2:3], in1=s_nw.ap, op0=ALU.mult, op1=ALU.mult),
                      reads=X.tok + s_ss[b].tok + s_nw.tok, writes=hb.tok)
                    pi = newps()
                    for c in range(8):
                        A("pe", lambda e, c=c, pi=pi, hb=hb: e.transpose(
                            out=psh(pi)[:, c * 128:(c + 1) * 128], in_=hb.ap[:, c * 128:(c + 1) * 128], identity=ident_bf),
                          reads=hb.tok + CBT, writes=PT(pi))
                    A("act", lambda e, b=b, pi=pi: e.copy(
                        out=hT[:, :, b * 128:(b + 1) * 128], in_=psh(pi)[:, 0:1024].rearrange("p (c t) -> p c t", t=128)),
                      reads=PT(pi), writes=HT.tok)

            print("mark L0 start", len(p.ops))
            load_nw(0)
            norm_to_hT()
            if ti == 0 and not samp:
                dump("d_hT", HT, BF16)

            yT = fm(B5)
            if samp:
                s_rows = Bview(1, 2048, F32, off=1024)
                HAS = Bview(0, 512, F32, off=1024)
                hist_s = HAS.ap.rearrange("p (c r) -> p c r", r=32)
                p.dma("sp", s_rows.ap[:32, :], sca, writes=s_rows.tok)
                for g in range(4):
                    pi = newps()
                    for c4 in range(4):
                        c = g * 4 + c4
                        A("pe", lambda e, c=c, c4=c4, pi=pi: e.transpose(
                            out=psf(pi)[:, c4 * 32:(c4 + 1) * 32], in_=s_rows.ap[:32, c * 128:(c + 1) * 128], identity=ident_f[:32, :32]),
                          reads=s_rows.tok + CT, writes=PT(pi))
                    A("dve", lambda e, g=g, pi=pi: e.tensor_copy(
                        out=hist_s[:, g * 4:(g + 1) * 4, :], in_=psf(pi)[:, 0:128].rearrange("p (c r) -> p c r", r=32)),
                      reads=PT(pi), writes=HAS.tok)
                NHA = view(o_B[0] + 1024 + 512, 512, F32)
                nhist_s = NHA.ap.rearrange("p (c r) -> p c r", r=32)

            for j in range(16):
                pis = [newps() for _ in range(4)]
                for bi in range(4):
                    bl = bi % 2
                    if bl == 0:
                        wt = wget()
                        wv = wt.ap.rearrange("p (b k n) -> p b k n", b=2, k=8)
                    for k in range(8):
                        A("pe", lambda e, wv=wv, bl=bl, k=k, pi=pis[bi]: e.matmul(
                            psf(pi)[:, :T], lhsT=wv[:, bl, k, :], rhs=hT[:, k, :], start=(k == 0), stop=(k == 7)),
                          reads=wt.tok + HT.tok, writes=PT(pis[bi]))
                pb, pc, pxa, pz = pis
                k2 = j % 2
                xe, xa, sz = s_xe[k2], s_xa[k2], s_sz[k2]
                A("act", lambda e, xa=xa, pxa=pxa: e.copy(out=xa.ap[:, :T], in_=psf(pxa)[:, :T]), reads=PT(pxa), writes=xa.tok)
                A("act", lambda e, sz=sz, pz=pz: e.activation(out=sz.ap[:, :T], in_=psf(pz)[:, :T], func=AF.Silu),
                  reads=PT(pz), writes=sz.tok)
                if ti == 0 and not samp and j == 0:
                    dump("d_xa0", xa, F32)
                    dump("d_sz0", sz, F32)
                if samp:
                    xe3 = xe.ap[:, 0:160].rearrange("p (s t) -> p s t", t=10)
                    A("dve", lambda e, xe3=xe3, j=j: e.tensor_copy(
                        out=xe3[:, :, 0:2], in_=hist_s[:, j, :].rearrange("p (s r) -> p s r", r=2)),
                      reads=HAS.tok, writes=xe.tok)
                    A("dve", lambda e, xe3=xe3, xa=xa, pc=pc: e.tensor_tensor(
                        out=xe3[:, :, 2:10], in0=psf(pc)[:, :128].rearrange("p (s t) -> p s t", t=8),
                        in1=xa.ap[:, :128].rearrange("p (s t) -> p s t", t=8), op=ALU.mult),
                      reads=PT(pc) + xa.tok, writes=xe.tok)
                    acc3 = s_acc.ap[:, :128].rearrange("p (s t) -> p s t", t=8)
                    win = [xe3[:, :, d:d + 8] for d in range(3)]
                    accv = acc3
                else:
                    A("dve", lambda e, xe=xe, j=j: e.tensor_copy(out=xe.ap[:, 0:2], in_=HISTAv.ap[:, j * 2:j * 2 + 2]),
                      reads=HISTAv.tok, writes=xe.tok)
                    A("dve", lambda e, xe=xe, xa=xa, pc=pc: e.tensor_tensor(
                        out=xe.ap[:, 2:2 + T], in0=psf(pc)[:, :T], in1=xa.ap[:, :T], op=ALU.mult),
                      reads=PT(pc) + xa.tok, writes=xe.tok)
                    win = [xe.ap[:, d:d + T] for d in range(3)]
                    accv = s_acc.ap[:, :T]
                A("dve", lambda e, sz=sz, pb=pb: e.tensor_tensor(out=s_tt.ap[:, :T], in0=psf(pb)[:, :T], in1=sz.ap[:, :T], op=ALU.mult),
                  reads=PT(pb) + sz.tok, writes=s_tt.tok)
                cw = [PAR[:, PC_CONVA + d * 16 + j:PC_CONVA + d * 16 + j + 1] for d in range(3)]
                A("dve", lambda e, accv=accv, win=win, cw=cw: e.tensor_scalar(
                    out=accv, in0=win[0], scalar1=cw[0], scalar2=None, op0=ALU.mult),
                  reads=xe.tok + CT, writes=s_acc.tok)
                for d in (1, 2):
                    A("dve", lambda e, accv=accv, win=win, cw=cw, d=d: e.scalar_tensor_tensor(
                        out=accv, in0=win[d], scalar=cw[d], in1=accv, op0=ALU.mult, op1=ALU.add),
                      reads=xe.tok + CT + s_acc.tok, writes=s_acc.tok)
                A("dve", lambda e, j=j: e.tensor_tensor(out=yT[:, j, :], in0=s_tt.ap[:, :T], in1=s_acc.ap[:, :T], op=ALU.mult),
                  reads=s_tt.tok + s_acc.tok, writes=ctok(4, j))
                if samp:
                    A("act", lambda e, xe3=xe3, j=j: e.copy(
                        out=nhist_s[:, j, :].rearrange("p (s r) -> p s r", r=2), in_=xe3[:, :, 8:10]),
                      reads=xe.tok, writes=NHA.tok)
                else:
                    A("act", lambda e, xe=xe, j=j: e.copy(out=HISTAv.ap[:, j * 2:j * 2 + 2], in_=xe.ap[:, T:T + 2]),
                      reads=xe.tok, writes=HISTAv.tok)

            if samp or last_p:
                nr = 32 if samp else 2
                src3 = nhist_s if samp else HISTAv.ap.rearrange("p (c r) -> p c r", r=2)
                srct = NHA.tok if samp else HISTAv.tok
                stg = scr(0, 2048, F32)
                for g in range(4):
                    pi = newps()
                    for c4 in range(4):
                        c = g * 4 + c4
                        A("pe", lambda e, c=c, c4=c4, pi=pi, src3=src3, nr=nr: e.matmul(
                            psf(pi)[:nr, c4 * 128:(c4 + 1) * 128], lhsT=src3[:, c, :], rhs=ident_f, start=True, stop=True),
                          reads=srct + CT, writes=PT(pi))
                    A("act", lambda e, g=g, pi=pi: e.copy(out=stg.ap[:nr, g * 512:(g + 1) * 512], in_=psf(pi)[:nr, :]),
                      reads=PT(pi), writes=stg.tok)
                p.dma("sp", cas if samp else cap, stg.ap[:nr, :], reads=stg.tok, is_out=True)

            def outproj(srcT, srcbuf):
                pss = [[newps(), newps()] for _ in range(NB)]
                for it in range(8):
                    w = wget()
                    wv = w.ap.rearrange("p (k n) -> p k n", k=2)
                    for k2 in range(2):
                        kc = it * 2 + k2
                        for b in range(NB):
                            for hf in range(2):
                                A("pe", lambda e, wv=wv, k2=k2, kc=kc, b=b, hf=hf, pi=pss[b][hf]: e.matmul(
                                    psf(pi)[:, :512], lhsT=srcT[:, kc, b * 128:(b + 1) * 128],
                                    rhs=wv[:, k2, hf * 512:(hf + 1) * 512], start=(kc == 0), stop=(kc == 15)),
                                  reads=w.tok + ctok(srcbuf, kc), writes=PT(pss[b][hf]))
                for b in range(NB):
                    for hf in range(2):
                        A("dve", lambda e, b=b, hf=hf, pi=pss[b][hf]: e.tensor_tensor(
                            out=Xt[:, b, hf * 512:(hf + 1) * 512], in0=psf(pi)[:, :512],
                            in1=Xt[:, b, hf * 512:(hf + 1) * 512], op=ALU.add),
                          reads=PT(pss[b][hf]) + X.tok, writes=X.tok)

            print("mark L0 outproj", len(p.ops))
            if ti == 0 and not samp:
                dump("d_yT", B5, BF16)
            outproj(yT, 4)
            if ti == 0 and not samp:
                dump("d_x1", X, F32)
            print("mark L1 start", len(p.ops))

            load_nw(1)
            norm_to_hT()
            xmT, xcT, vT, qT, kT = fm(B1), fm(B2), fm(B3), fm(B4), fm(B5)
            ogT, szT = xmT, qT

            if samp:
                s_rows_b = Bview(2, 2048, F32, off=1024)
                HBS = view(o_B[0] + 2048, 768, F32)
                NHB = view(o_B[0] + 2816, 768, F32)
                histb_s = HBS.ap.rearrange("p (c r) -> p c r", r=48)
                nhistb_s = NHB.ap.rearrange("p (c r) -> p c r", r=48)
                A("pool", lambda e: e.memset(s_rows_b.ap[:64, :], 0.0), writes=s_rows_b.tok)
                p.dma("sp", s_rows_b.ap[:48, :], scb, writes=s_rows_b.tok)
                for g in range(2):
                    pi = newps()
                    for c8 in range(8):
                        c = g * 8 + c8
                        A("pe", lambda e, c=c, c8=c8, pi=pi: e.transpose(
                            out=psf(pi)[:, c8 * 64:(c8 + 1) * 64], in_=s_rows_b.ap[:64, c * 128:(c + 1) * 128], identity=ident_f[:64, :64]),
                          reads=s_rows_b.tok + CT, writes=PT(pi))
                    A("dve", lambda e, g=g, pi=pi: e.tensor_copy(
                        out=histb_s[:, g * 8:(g + 1) * 8, :], in_=psf(pi)[:, 0:512].rearrange("p (c r) -> p c r", r=64)[:, :, 0:48]),
                      reads=PT(pi), writes=HBS.tok)

            for it in range(8):
                w = wget()
                wv = w.ap.rearrange("p (o k n) -> p o k n", o=2, k=8)
                for o2 in range(2):
                    oc = it * 2 + o2
                    pi = newps()
                    for k in range(8):
                        A("pe", lambda e, wv=wv, o2=o2, k=k, pi=pi: e.matmul(
                            psf(pi)[:, :T], lhsT=wv[:, o2, k, :], rhs=hT[:, k, :], start=(k == 0), stop=(k == 7)),
                          reads=w.tok + HT.tok, writes=PT(pi))
                    xe = s_xe[oc % 2]
                    if samp:
                        xe3 = xe.ap[:, 0:176].rearrange("p (s t) -> p s t", t=11)
                        A("dve", lambda e, xe3=xe3, oc=oc: e.tensor_copy(
                            out=xe3[:, :, 0:3], in_=histb_s[:, oc, :].rearrange("p (s r) -> p s r", r=3)),
                          reads=HBS.tok, writes=xe.tok)
                        A("act", lambda e, xe3=xe3, pi=pi: e.copy(
                            out=xe3[:, :, 3:11], in_=psf(pi)[:, :128].rearrange("p (s t) -> p s t", t=8)),
                          reads=PT(pi), writes=xe.tok)
                        A("act", lambda e, oc=oc, pi=pi: e.copy(out=xmT[:, oc, :], in_=psf(pi)[:, :T]),
                          reads=PT(pi), writes=B1.tok)
                        win = [xe3[:, :, d:d + 8] for d in range(4)]
                        accv = s_acc.ap[:, :128].rearrange("p (s t) -> p s t", t=8)
                    else:
                        A("dve", lambda e, xe=xe, oc=oc: e.tensor_copy(out=xe.ap[:, 0:3], in_=HISTBv.ap[:, oc * 3:oc * 3 + 3]),
                          reads=HISTBv.tok, writes=xe.tok)
                        A("act", lambda e, xe=xe, pi=pi: e.copy(out=xe.ap[:, 3:3 + T], in_=psf(pi)[:, :T]),
                          reads=PT(pi), writes=xe.tok)
                        A("act", lambda e, oc=oc, pi=pi: e.copy(out=xmT[:, oc, :], in_=psf(pi)[:, :T]),
                          reads=PT(pi), writes=B1.tok)
                        win = [xe.ap[:, d:d + T] for d in range(4)]
                        accv = s_acc.ap[:, :T]
                    cw = [PAR[:, PC_CONVB + d * 16 + oc:PC_CONVB + d * 16 + oc + 1] for d in range(4)]
                    A("act", lambda e, accv=accv, win=win, cw=cw: e.activation(out=accv, in_=win[0], func=AF.Copy, scale=cw[0]),
                      reads=xe.tok + CT, writes=s_acc.tok)
                    for d in (1, 2, 3):
                        A("dve", lambda e, accv=accv, win=win, cw=cw, d=d: e.scalar_tensor_tensor(
                            out=accv, in0=win[d], scalar=cw[d], in1=accv, op0=ALU.mult, op1=ALU.add),
                          reads=xe.tok + CT + s_acc.tok, writes=s_acc.tok)
                    A("act", lambda e, oc=oc: e.activation(out=xcT[:, oc, :], in_=s_acc.ap[:, :T], func=AF.Silu,
                                                           bias=PAR[:, PC_CB + oc:PC_CB + oc + 1]),
                      reads=s_acc.tok + CT, writes=B2.tok)
                    if samp:
                        A("act", lambda e, xe3=xe3, oc=oc: e.copy(
                            out=nhistb_s[:, oc, :].rearrange("p (s r) -> p s r", r=3), in_=xe3[:, :, 8:11]),
                          reads=xe.tok, writes=NHB.tok)
                    else:
                        A("act", lambda e, xe=xe, oc=oc: e.copy(out=HISTBv.ap[:, oc * 3:oc * 3 + 3], in_=xe.ap[:, T:T + 3]),
                          reads=xe.tok, writes=HISTBv.tok)

            if samp or last_p:
                nr = 48 if samp else 3
                src3 = nhistb_s if samp else HISTBv.ap.rearrange("p (c r) -> p c r", r=3)
                srct = NHB.tok if samp else HISTBv.tok
                stg = scr(0, 2048, F32)
                for g in range(4):
                    pi = newps()
                    for c4 in range(4):
                        c = g * 4 + c4
                        A("pe", lambda e, c=c, c4=c4, pi=pi, src3=src3, nr=nr: e.matmul(
                            psf(pi)[:nr, c4 * 128:(c4 + 1) * 128], lhsT=src3[:, c, :], rhs=ident_f, start=True, stop=True),
                          reads=srct + CT, writes=PT(pi))
                    A("act", lambda e, g=g, pi=pi: e.copy(out=stg.ap[:nr, g * 512:(g + 1) * 512], in_=psf(pi)[:nr, :]),
                      reads=PT(pi), writes=stg.tok)
                p.dma("sp", cbs if samp else cbp, stg.ap[:nr, :], reads=stg.tok, is_out=True)

            print("mark vqk", len(p.ops))
            cnt_ev = [0]
            for (dst, dv_, src, srct) in [(vT, B3, xmT, B1.tok), (qT, B4, xcT, B2.tok), (kT, B5, xcT, B2.tok)]:
                for oc in range(16):
                    w = wget()
                    wv = w.ap.rearrange("p (k n) -> p k n", k=16)
                    pi = newps()
                    for k in range(16):
                        A("pe", lambda e, wv=wv, k=k, pi=pi, src=src: e.matmul(
                            psf(pi)[:, :T], lhsT=wv[:, k, :], rhs=src[:, k, :], start=(k == 0), stop=(k == 15)),
                          reads=w.tok + srct, writes=PT(pi))
                    if cnt_ev[0] % 2 == 0:
                        A("act", lambda e, dst=dst, oc=oc, pi=pi: e.copy(out=dst[:, oc, :], in_=psf(pi)[:, :T]),
                          reads=PT(pi), writes=dv_.tok)
                    else:
                        A("dve", lambda e, dst=dst, oc=oc, pi=pi: e.tensor_copy(out=dst[:, oc, :], in_=psf(pi)[:, :T]),
                          reads=PT(pi), writes=dv_.tok)
                    cnt_ev[0] += 1

            print("mark gates", len(p.ops))
            pg = newps()
            gsrc = [(qT, B4.tok)] * 16 + [(kT, B5.tok)] * 16 + [(vT, B3.tok)] * 16
            for kc in range(48):
                A("pe", lambda e, kc=kc, pg=pg: e.matmul(
                    psf(pg)[:8, :T], lhsT=wif_bf[:, kc, :], rhs=gsrc[kc][0][:, kc % 16, :], start=(kc == 0), stop=(kc == 47)),
                  reads=CBT + gsrc[kc][1], writes=PT(pg))
            GSB = scr(3072, 512, parts=8)
            A("act", lambda e: e.activation(out=GSB.ap[:8, :T], in_=psf(pg)[:8, :T], func=AF.Identity,
                                            bias=PAR[:8, PC_BIF:PC_BIF + 1]),
              reads=PT(pg) + CT, writes=GSB.tok)
            pf = newps()
            A("pe", lambda e: e.matmul(psf(pf)[:4, :T], lhsT=PAR[:8, PC_SEL:PC_SEL + 4], rhs=GSB.ap[:8, :T], start=True, stop=True),
              reads=GSB.tok + CT, writes=PT(pf))
            G = [scr(i * 512, 512, parts=4) for i in range(6)]
            g_ = [g.ap[:4, :T] for g in G]
            CC = CCv.ap[:4, :T]
            NEGA = NEGAv.ap[:4, :T]
            rmask = PAR[:4, (PC_RMASK_S if samp else PC_RMASK):(PC_RMASK_S if samp else PC_RMASK) + T]
            amask = PAR[:4, (PC_AMASK_S if samp else PC_AMASK):(PC_AMASK_S if samp else PC_AMASK) + T]
            A("act", lambda e: e.copy(out=g_[2], in_=psf(pf)[:4, :T]), reads=PT(pf), writes=G[2].tok)
            A("dve", lambda e: e.scalar_tensor_tensor(out=g_[0], in0=g_[2], scalar=-1.0, in1=g_[2], op0=ALU.mult, op1=ALU.max),
              reads=G[2].tok, writes=G[0].tok)
            A("act", lambda e: e.activation(out=g_[1], in_=g_[0], func=AF.Exp, scale=-1.0), reads=G[0].tok, writes=G[1].tok)
            A("act", lambda e: e.activation(out=g_[1], in_=g_[1], func=AF.Ln, bias=1.0), reads=G[1].tok, writes=G[1].tok)
            A("dve", lambda e: e.tensor_scalar_min(out=g_[0], in0=g_[2], scalar1=0.0), reads=G[2].tok, writes=G[0].tok)
            A("dve", lambda e: e.tensor_sub(out=g_[0], in0=g_[0], in1=g_[1]), reads=G[0].tok + G[1].tok, writes=G[0].tok)
            A("dve", lambda e: e.tensor_tensor_scan(out=g_[5], data0=rmask, data1=g_[0], initial=0.0, op0=ALU.mult, op1=ALU.add),
              reads=G[0].tok + CT, writes=G[5].tok)
            A("dve", lambda e: e.tensor_sub(out=CC, in0=GSB.ap[:4, :T], in1=g_[5]), reads=GSB.tok + G[5].tok, writes=CCv.tok)
            A("dve", lambda e: e.tensor_tensor_scan(out=g_[1], data0=amask, data1=CC, initial=0.0, op0=ALU.add, op1=ALU.max),
              reads=CCv.tok + CT, writes=G[1].tok)
            MALL = MALLv.ap
            MT = scr(3584 + 64, 32, parts=4)
            if samp:
                p.dma("sp", MALL[:4, 0:16], sm, writes=MALLv.tok)
            else:
                for c in range(NCH):
                    le = c * L + L - 1
                    A("dve", lambda e, c=c, le=le: e.tensor_tensor(out=MT.ap[:4, 0:1], in0=g_[1][:, le:le + 1], in1=MALL[:4, c:c + 1], op=ALU.max),
                      reads=G[1].tok + MALLv.tok, writes=MT.tok)
                    A("dve", lambda e, c=c, le=le: e.tensor_tensor(out=MALL[:4, c + 1:c + 2], in0=MT.ap[:4, 0:1], in1=g_[5][:, le:le + 1], op=ALU.add),
                      reads=MT.tok + G[5].tok, writes=MALLv.tok)

            def v3(ap):
                return ap.rearrange("p (c l) -> p c l", l=L)

            mprev_bc = MALL[:4, 0:NCH].unsqueeze(2).to_broadcast([4, NCH, L])
            A("dve", lambda e: e.tensor_tensor(out=v3(g_[2]), in0=v3(g_[1]), in1=mprev_bc, op=ALU.max),
              reads=G[1].tok + MALLv.tok, writes=G[2].tok)
            A("dve", lambda e: e.tensor_scalar(out=NEGA, in0=g_[2], scalar1=-1.0, scalar2=None, op0=ALU.mult),
              reads=G[2].tok, writes=NEGAv.tok)
            A("dve", lambda e: e.tensor_tensor(out=g_[3], in0=g_[5], in1=g_[2], op=ALU.add), reads=G[5].tok + G[2].tok, writes=G[3].tok)
            A("act", lambda e: e.activation(out=g_[3], in_=g_[3], func=AF.Exp, scale=-1.0), reads=G[3].tok, writes=G[3].tok)
            A("dve", lambda e: e.tensor_tensor(out=v3(g_[4]), in0=mprev_bc, in1=v3(g_[2]), op=ALU.subtract),
              reads=G[2].tok + MALLv.tok, writes=G[4].tok)
            A("act", lambda e: e.activation(out=g_[4], in_=g_[4], func=AF.Exp), reads=G[4].tok, writes=G[4].tok)
            alast_bc = v3(g_[2])[:, :, L - 1:L].to_broadcast([4, NCH, L])
            A("dve", lambda e: e.tensor_tensor(out=v3(g_[0]), in0=v3(CC), in1=alast_bc, op=ALU.subtract),
              reads=CCv.tok + G[2].tok, writes=G[0].tok)
            A("act", lambda e: e.activation(out=g_[0], in_=g_[0], func=AF.Exp, bias=math.log(KSCALE)), reads=G[0].tok, writes=G[0].tok)
            if samp:
                A("dve", lambda e: e.tensor_tensor(out=MT.ap[:4, 0:16], in0=v3(g_[5])[:, :, L - 1], in1=v3(g_[2])[:, :, L - 1], op=ALU.add),
                  reads=G[5].tok + G[2].tok, writes=MT.tok)
                p.dma("sp", ms_o, MT.ap[:4, 0:16], reads=MT.tok, is_out=True)
            else:
                if last_p:
                    p.dma("sp", mp_o, MALL[:4, NCH:NCH + 1], reads=MALLv.tok, is_out=True)
            pq = newps()
            for c in range(NCH):
                for qi, gi in enumerate((0, 4, 3)):
                    A("pe", lambda e, c=c, qi=qi, gi=gi: e.matmul(
                        psf(pq)[:L, (c * 3 + qi) * 4:(c * 3 + qi) * 4 + 4], lhsT=g_[gi][:, c * L:(c + 1) * L],
                        rhs=PAR[:4, PC_EYE4:PC_EYE4 + 4], start=True, stop=True),
                      reads=G[gi].tok + CT, writes=PT(pq))
            COLQ = COLQv.ap
            A("act", lambda e: e.copy(out=COLQ[:L, :NCH * 12], in_=psf(pq)[:L, :NCH * 12]), reads=PT(pq), writes=COLQv.tok)
            DECD = scr(3584, 64, parts=4)
            A("dve", lambda e: e.tensor_tensor(
                out=DECD.ap[:4, :NCH * 4].rearrange("p (c h) -> p c h", h=4),
                in0=v3(g_[4])[:, :, L - 1:L].to_broadcast([4, NCH, 4]),
                in1=PAR[:4, PC_EYE4:PC_EYE4 + 4].unsqueeze(1).to_broadcast([4, NCH, 4]), op=ALU.mult),
              reads=G[4].tok + CT, writes=DECD.tok)
            pd = newps()
            A("pe", lambda e: e.matmul(psf(pd)[:, :NCH * 4], lhsT=PAR[:4, PC_ONES:PC_ONES + 128], rhs=DECD.ap[:4, :NCH * 4],
                                       start=True, stop=True),
              reads=DECD.tok + CT, writes=PT(pd))
            DECB = DECBv.ap
            A("act", lambda e: e.copy(out=DECB[:, :NCH * 4], in_=psf(pd)[:, :NCH * 4]), reads=PT(pd), writes=DECBv.tok)
            if not samp:
                A("dve", lambda e: e.tensor_copy(out=MALL[:4, 0:1], in_=MALL[:4, NCH:NCH + 1]), reads=MALLv.tok, writes=MALLv.tok)

            if samp:
                NROWS = Bview(3, 256, F32, off=1024)
                NALL = view(o_B[0] + 3584, 256, F32)
                NBFA = view(o_B[0] + 3840, 128, BF16)
                nrows3 = NROWS.ap.rearrange("p (a d) -> p a d", a=2)
                p.dma("sp", nrows3, sn.rearrange("(a p) d -> p a d", p=128), writes=NROWS.tok)
                pi = newps()
                for a in range(2):
                    A("pe", lambda e, a=a, pi=pi: e.transpose(out=psf(pi)[:, a * 128:(a + 1) * 128], in_=nrows3[:, a, :], identity=ident_f),
                      reads=NROWS.tok + CT, writes=PT(pi))
                A("dve", lambda e, pi=pi: e.tensor_copy(out=NALL.ap, in_=psf(pi)[:, :256]), reads=PT(pi), writes=NALL.tok)
                A("act", lambda e: e.copy(out=NBFA.ap, in_=NALL.ap), reads=NALL.tok, writes=NBFA.tok)

            print("mark recurrence", len(p.ops))
            s_D = [scr(3712, 64, parts=64), scr(3776, 64, parts=64)]
            s_Sd = [scr(3840, 32, BF16, parts=64), scr(3872, 32, BF16, parts=64)]
            s_kw = [scr(3904, 256, BF16, parts=64), scr(4160, 256, BF16, parts=64)]
            s_vb = [scr(4416, 256, BF16, parts=64), scr(4672, 256, BF16, parts=64)]
            s_t1 = scr(4928, 512, parts=64)
            s_hh = [scr(5440, 512, parts=64), scr(5952, 512, parts=64)]
            s_hn = scr(6464, 1024, BF16, parts=64)
            s_sm = [scr(7488, 32, parts=64), scr(7520, 32, parts=64)]
            NIT = NCH * 4
            PF = 3

            def st_of(idx):
                c, h = idx // 4, idx % 4
                if samp:
                    Cst = CSTv[idx % 4]
                    return (Cst, NALL.ap[:, idx * 4:idx * 4 + 4], NALL.tok, NBFA.ap[:, idx * 4:idx * 4 + 4], NBFA.tok)
                return (CSTv[h], NSTh[h].ap, NSTh[h].tok, NBFh[h].ap, NBFh[h].tok)

            def c3(Cst):
                return Cst.ap.rearrange("p (k e) -> p k e", k=4)

            def load_C(idx):
                Cst = CSTv[idx % 4]
                p.dma("sp", c3(Cst), sC[idx].rearrange("(k p) e -> p k e", p=128), writes=Cst.tok)

            def castC(idx):
                Cst = st_of(idx)[0]
                Cbf = CBFv[idx % 2]
                A("act", lambda e, Cbf=Cbf, Cst=Cst: e.copy(out=Cbf.ap, in_=Cst.ap), reads=Cst.tok, writes=Cbf.tok)

            def front(idx):
                c, h = idx // 4, idx % 4
                t0 = c * L
                k2 = idx % 2
                pa, pbk, pvv = k2, 2, 3
                D, Sd, kw, vb = s_D[k2], s_Sd[k2], s_kw[k2], s_vb[k2]
                qs = [qT[:, h * 4 + dk, t0:t0 + L] for dk in range(4)]
                ks_ = [kT[:, h * 4 + dk, t0:t0 + L] for dk in range(4)]
                vs = [vT[:, h * 4 + dk, t0:t0 + L] for dk in range(4)]
                for dk in range(4):
                    A("pe", lambda e, dk=dk, pa=pa, ks_=ks_, qs=qs: e.matmul(
                        psf(pa)[:L, 0:L], lhsT=ks_[dk], rhs=qs[dk], start=(dk == 0), stop=(dk == 3)),
                      reads=B4.tok + B5.tok, writes=PT(pa))
                eh = PAR[:4, PC_EH + h * 64:PC_EH + h * 64 + L]
                A("pe", lambda e, pa=pa, eh=eh, t0=t0: e.matmul(psf(pa)[:L, 64:64 + L], lhsT=eh, rhs=NEGA[:, t0:t0 + L], start=True, stop=False),
                  reads=NEGAv.tok + CT, writes=PT(pa))
                A("pe", lambda e, pa=pa, eh=eh, t0=t0: e.matmul(psf(pa)[:L, 64:64 + L], lhsT=CC[:, t0:t0 + L], rhs=eh, start=False, stop=False),
                  reads=CCv.tok + CT, writes=PT(pa))
                A("pe", lambda e, pa=pa: e.matmul(psf(pa)[:L, 64:64 + L], lhsT=ident_f[:L, :L], rhs=PAR[:L, PC_NEG:PC_NEG + L], start=False, stop=True),
                  reads=CT, writes=PT(pa))
                for dk in range(4):
                    A("pe", lambda e, dk=dk, pbk=pbk, ks_=ks_: e.matmul(psf(pbk)[:L, dk * 128:(dk + 1) * 128], lhsT=ks_[dk], rhs=ident_bf, start=True, stop=True),
                      reads=B5.tok + CBT, writes=PT(pbk))
                for dk in range(4):
                    A("pe", lambda e, dk=dk, pvv=pvv, vs=vs: e.matmul(psf(pvv)[:L, dk * 128:(dk + 1) * 128], lhsT=vs[dk], rhs=ident_bf, start=True, stop=True),
                      reads=B3.tok + CBT, writes=PT(pvv))
                A("act", lambda e, pa=pa, D=D: e.activation(out=D.ap[:L, :L], in_=psf(pa)[:L, 64:64 + L], func=AF.Exp),
                  reads=PT(pa), writes=D.tok)
                A("dve", lambda e, pa=pa, D=D, Sd=Sd: e.scalar_tensor_tensor(
                    out=Sd.ap[:L, :L], in0=psf(pa)[:L, 0:L], scalar=KSCALE, in1=D.ap[:L, :L], op0=ALU.mult, op1=ALU.mult),
                  reads=PT(pa) + D.tok, writes=Sd.tok)
                cq = (c * 3) * 4 + h
                A("act", lambda e, pbk=pbk, kw=kw, cq=cq: e.activation(
                    out=kw.ap[:L, :512], in_=psf(pbk)[:L, 0:512], func=AF.Copy, scale=COLQ[:L, cq:cq + 1]),
                  reads=PT(pbk) + COLQv.tok, writes=kw.tok)
                A("act", lambda e, pvv=pvv, vb=vb: e.copy(out=vb.ap[:L, :512], in_=psf(pvv)[:L, 0:512]), reads=PT(pvv), writes=vb.tok)

            def mid(idx, part):
                c, h = idx // 4, idx % 4
                t0 = c * L
                k2 = idx % 2
                Cst, nst, nstt, nbf, nbft = st_of(idx)
                Cbf = CBFv[k2]
                Cb3 = Cbf.ap.rearrange("p (k e) -> p k e", k=4)
                pa, phq, phs = k2, 4, 5
                Sd, vb, hh, sm_ = s_Sd[k2], s_vb[k2], s_hh[k2], s_sm[k2]
                qs = [qT[:, h * 4 + dk, t0:t0 + L] for dk in range(4)]
                s_ = sm_.ap
                if part == "b":
                    A("dve", lambda e, phs=phs, s_=s_, hh=hh: e.scalar_tensor_tensor(
                        out=hh.ap[:L, :512], in0=psf(phs)[:L, :512], scalar=s_[:L, 4:5], in1=s_t1.ap[:L, :512], op0=ALU.mult, op1=ALU.add),
                      reads=PT(phs) + sm_.tok + s_t1.tok, writes=hh.tok)
                    A("dve", lambda e, s_=s_, hh=hh: e.bn_stats(out=s_[:L, 8:14], in_=hh.ap[:L, :512]), reads=hh.tok, writes=sm_.tok)
                    A("dve", lambda e, s_=s_: e.bn_aggr(out=s_[:L, 14:16], in_=s_[:L, 8:14]), reads=sm_.tok, writes=sm_.tok)
                    A("act", lambda e, s_=s_: e.activation(out=s_[:L, 16:17], in_=s_[:L, 15:16], func=AF.Ln, bias=LN_EPS), reads=sm_.tok, writes=sm_.tok)
                    A("act", lambda e, s_=s_: e.activation(out=s_[:L, 17:18], in_=s_[:L, 16:17], func=AF.Exp, scale=-0.5), reads=sm_.tok, writes=sm_.tok)
                    return
                if part == "c":
                    A("dve", lambda e, s_=s_, hh=hh, h=h: e.tensor_scalar(
                        out=s_hn.ap[:L, h * 512:(h + 1) * 512], in0=hh.ap[:L, :512], scalar1=s_[:L, 14:15], scalar2=s_[:L, 17:18],
                        op0=ALU.subtract, op1=ALU.mult),
                      reads=hh.tok + sm_.tok, writes=s_hn.tok)
                    return
                for dk in range(4):
                    A("pe", lambda e, dk=dk, phq=phq, qs=qs, Cb3=Cb3: e.matmul(
                        psf(phq)[:L, :512], lhsT=qs[dk], rhs=Cb3[:, dk, :], start=(dk == 0), stop=(dk == 3)),
                      reads=B4.tok + Cbf.tok, writes=PT(phq))
                for dk in range(4):
                    A("pe", lambda e, dk=dk, pa=pa, qs=qs, nbf=nbf: e.matmul(
                        psf(pa)[:L, 128:129], lhsT=qs[dk], rhs=nbf[:, dk:dk + 1], start=(dk == 0), stop=(dk == 3)),
                      reads=B4.tok + nbft, writes=PT(pa))
                A("pe", lambda e, phs=phs, Sd=Sd, vb=vb: e.matmul(psf(phs)[:L, :512], lhsT=Sd.ap[:L, :L], rhs=vb.ap[:L, :512], start=True, stop=True),
                  reads=Sd.tok + vb.tok, writes=PT(phs))
                A("pe", lambda e, pa=pa, Sd=Sd: e.matmul(psf(pa)[:L, 129:130], lhsT=Sd.ap[:L, :L], rhs=ones_bf[:L, 0:1], start=True, stop=True),
                  reads=Sd.tok + CBT, writes=PT(pa))
                s_ = sm_.ap
                cq = (c * 3) * 4 + h
                wi = COLQ[:L, cq + 4:cq + 5]
                em = COLQ[:L, cq + 8:cq + 9]
                A("act", lambda e, pa=pa, s_=s_: e.copy(out=s_[:L, 0:2], in_=psf(pa)[:L, 128:130]), reads=PT(pa), writes=sm_.tok)
                A("dve", lambda e, s_=s_, wi=wi: e.scalar_tensor_tensor(out=s_[:L, 2:3], in0=s_[:L, 0:1], scalar=wi, in1=s_[:L, 1:2],
                                                                      op0=ALU.mult, op1=ALU.add),
                  reads=sm_.tok + COLQv.tok, writes=sm_.tok)
                A("dve", lambda e, s_=s_: e.scalar_tensor_tensor(out=s_[:L, 3:4], in0=s_[:L, 2:3], scalar=-1.0, in1=s_[:L, 2:3],
                                                                 op0=ALU.mult, op1=ALU.max),
                  reads=sm_.tok, writes=sm_.tok)
                A("dve", lambda e, s_=s_, em=em: e.tensor_tensor(out=s_[:L, 3:4], in0=s_[:L, 3:4], in1=em, op=ALU.max),
                  reads=sm_.tok + COLQv.tok, writes=sm_.tok)
                A("dve", lambda e, s_=s_: e.reciprocal(out=s_[:L, 4:5], in_=s_[:L, 3:4]), reads=sm_.tok, writes=sm_.tok)
                A("dve", lambda e, s_=s_, wi=wi: e.tensor_tensor(out=s_[:L, 5:6], in0=s_[:L, 4:5], in1=wi, op=ALU.mult),
                  reads=sm_.tok + COLQv.tok, writes=sm_.tok)
                A("act", lambda e, phq=phq, s_=s_: e.activation(out=s_t1.ap[:L, :512], in_=psf(phq)[:L, :512], func=AF.Copy, scale=s_[:L, 5:6]),
                  reads=PT(phq) + sm_.tok, writes=s_t1.tok)

            def back(idx, part):
                c, h = idx // 4, idx % 4
                k2 = idx % 2
                Cst, nst, nstt, nbf, nbft = st_of(idx)
                C3 = c3(Cst)
                kw, vb = s_kw[k2], s_vb[k2]
                pa = (idx + 1) % 2
                dec = DECB[:, idx:idx + 1]
                for dk in ((0, 1) if part == "a" else (2, 3)):
                    pk = 6 + dk % 2
                    A("pe", lambda e, dk=dk, pk=pk, kw=kw, vb=vb: e.matmul(
                        psf(pk)[:, :512], lhsT=kw.ap[:L, dk * 128:(dk + 1) * 128], rhs=vb.ap[:L, :512], start=True, stop=True),
                      reads=kw.tok + vb.tok, writes=PT(pk))
                    A("dve", lambda e, dk=dk, pk=pk, C3=C3, dec=dec: e.scalar_tensor_tensor(
                        out=C3[:, dk, :], in0=C3[:, dk, :], scalar=dec, in1=psf(pk)[:, :512], op0=ALU.mult, op1=ALU.add),
                      reads=PT(pk) + Cst.tok + DECBv.tok, writes=Cst.tok)
                if part == "a":
                    return
                for dk in range(4):
                    A("pe", lambda e, dk=dk, pa=pa, kw=kw: e.matmul(
                        psf(pa)[:, 132 + dk:133 + dk], lhsT=kw.ap[:L, dk * 128:(dk + 1) * 128], rhs=ones_bf[:L, 0:1], start=True, stop=True),
                      reads=kw.tok + CBT, writes=PT(pa))
                A("dve", lambda e, pa=pa, nst=nst, dec=dec: e.scalar_tensor_tensor(
                    out=nst, in0=nst, scalar=dec, in1=psf(pa)[:, 132:136], op0=ALU.mult, op1=ALU.add),
                  reads=PT(pa) + nstt + DECBv.tok, writes=nstt)
                if samp:
                    p.dma("pool", Cs[idx].rearrange("(k p) e -> p k e", p=128), C3, reads=Cst.tok, is_out=True)
                else:
                    A("pool", lambda e, nbf=nbf, nst=nst: e.tensor_copy(out=nbf, in_=nst), reads=nstt, writes=nbft)
                    if last_p and c == NCH - 1:
                        p.dma("sp", Cp[h].rearrange("(k p) e -> p k e", p=128), C3, reads=Cst.tok, is_out=True)

            def outstage(c):
                t0 = c * L
                po = 2
                if samp:
                    for fc in range(16):
                        A("pe", lambda e, fc=fc, po=po: e.matmul(psf(po)[:, fc * L:(fc + 1) * L], lhsT=s_hn.ap[:L, fc * 128:(fc + 1) * 128],
                                                                 rhs=ident_bf[:L, :L], start=True, stop=True),
                          reads=s_hn.tok + CBT, writes=PT(po))
                    A("act", lambda e, po=po, t0=t0: e.copy(out=ogT[:, :, t0:t0 + L], in_=psf(po)[:, :16 * L].rearrange("p (c l) -> p c l", l=L)),
                      reads=PT(po), writes=B1.tok)
                else:
                    for fc in range(16):
                        A("pe", lambda e, fc=fc, po=po: e.transpose(out=psh(po)[:, fc * L:(fc + 1) * L], in_=s_hn.ap[:L, fc * 128:(fc + 1) * 128],
                                                                    identity=ident_bf[:L, :L]),
                          reads=s_hn.tok + CBT, writes=PT(po))
                    A("act", lambda e, po=po, t0=t0: e.copy(out=ogT[:, :, t0:t0 + L], in_=psh(po)[:, :16 * L].rearrange("p (c l) -> p c l", l=L)),
                      reads=PT(po), writes=B1.tok)

            if samp:
                for i in range(min(PF, NIT)):
                    load_C(i)
            castC(0)
            front(0)
            for idx in range(NIT):
                if idx + 1 < NIT:
                    castC(idx + 1)
                mid(idx, "a")
                if idx > 0 and idx % 4 == 0:
                    outstage(idx // 4 - 1)
                if idx > 0:
                    back(idx - 1, "a")
                mid(idx, "b")
                if idx > 0:
                    back(idx - 1, "b")
                if idx + 1 < NIT:
                    front(idx + 1)
                mid(idx, "c")
                if samp and idx + PF < NIT:
                    load_C(idx + PF)
            outstage(NCH - 1)
            back(NIT - 1, "a")
            back(NIT - 1, "b")

            if samp:
                pi = newps()
                for a in range(2):
                    A("pe", lambda e, a=a, pi=pi: e.transpose(out=psf(pi)[:, a * 128:(a + 1) * 128], in_=NALL.ap[:, a * 128:(a + 1) * 128], identity=ident_f),
                      reads=NALL.tok + CT, writes=PT(pi))
                A("act", lambda e, pi=pi: e.copy(out=NROWS.ap, in_=psf(pi)[:, :256]), reads=PT(pi), writes=NROWS.tok)
                p.dma("sp", ns_o.rearrange("(a p) d -> p a d", p=128), nrows3, reads=NROWS.tok, is_out=True)
            elif last_p:
                pi = newps()
                for h in range(4):
                    A("dve", lambda e, h=h: e.tensor_copy(out=NSTC.ap[:, h * 4:h * 4 + 4], in_=NSTh[h].ap), reads=NSTh[h].tok, writes=NSTC.tok)
                A("pe", lambda e, pi=pi: e.matmul(psf(pi)[:16, 0:128], lhsT=NSTC.ap, rhs=ident_f, start=True, stop=True), reads=NSTC.tok + CT, writes=PT(pi))
                NO = scr(0, 128, parts=16)
                A("act", lambda e, pi=pi: e.copy(out=NO.ap[:16, :], in_=psf(pi)[:16, 0:128]), reads=PT(pi), writes=NO.tok)
                p.dma("sp", np_o, NO.ap[:16, :], reads=NO.tok, is_out=True)

            print("mark z", len(p.ops))
            for it in range(8):
                w = wget()
                wv = w.ap.rearrange("p (o k n) -> p o k n", o=2, k=8)
                for o2 in range(2):
                    oc = it * 2 + o2
                    pi = newps()
                    for k in range(8):
                        A("pe", lambda e, wv=wv, o2=o2, k=k, pi=pi: e.matmul(
                            psf(pi)[:, :T], lhsT=wv[:, o2, k, :], rhs=hT[:, k, :], start=(k == 0), stop=(k == 7)),
                          reads=w.tok + HT.tok, writes=PT(pi))
                    A("act", lambda e, oc=oc, pi=pi: e.activation(out=szT[:, oc, :], in_=psf(pi)[:, :T], func=AF.Silu),
                      reads=PT(pi), writes=ctok(3, oc))
            s_o1 = [scr(0, 512), scr(512, 512)]
            for fc in range(16):
                o1 = s_o1[fc % 2]
                A("dve", lambda e, fc=fc, o1=o1: e.tensor_scalar(out=o1.ap[:, :T], in0=xcT[:, fc, :], scalar1=PAR[:, PC_SKIP + fc:PC_SKIP + fc + 1],
                                                                scalar2=None, op0=ALU.mult),
                  reads=ctok(1, fc) + CT, writes=o1.tok)
                A("dve", lambda e, fc=fc, o1=o1: e.scalar_tensor_tensor(out=o1.ap[:, :T], in0=ogT[:, fc, :], scalar=PAR[:, PC_ONORM + fc:PC_ONORM + fc + 1],
                                                                       in1=o1.ap[:, :T], op0=ALU.mult, op1=ALU.add),
                  reads=ctok(0, fc) + CT + o1.tok, writes=o1.tok)
                A("dve", lambda e, fc=fc, o1=o1: e.tensor_tensor(out=ogT[:, fc, :], in0=o1.ap[:, :T], in1=szT[:, fc, :], op=ALU.mult),
                  reads=o1.tok + ctok(3, fc), writes=ctok(0, fc))
            outproj(ogT, 0)

            print("mark final", len(p.ops))
            load_nw(2)
            for b in range(NB):
                k = b % 2
                rms_stats(b, k)
                ost = s_ost[k]
                A("dve", lambda e, b=b, k=k, ost=ost: e.scalar_tensor_tensor(
                    out=ost.ap, in0=Xt[:, b, :], scalar=s_ss[k].ap[:, 2:3], in1=s_nw.ap, op0=ALU.mult, op1=ALU.mult),
                  reads=X.tok + s_ss[k].tok + s_nw.tok, writes=ost.tok)
                if samp:
                    p.dma("sp", ys, ost.ap, reads=ost.tok, is_out=True)
                else:
                    r0 = ti * 512 + b * 128
                    p.dma("sp", yp[r0:r0 + 128, :], ost.ap, reads=ost.tok, is_out=True)

        for ti in range(n_ptiles):
            run_tile("p", ti)
        if do_sample:
            run_tile("s", 0)
        print("ops", len(p.ops), "arena scratch words", SCR_WORDS)
        p.emit(sems, dsems)
    return nc


def _host_prep(inp):
    f = np.float32
    a_w_in = np.asarray(inp["a_w_in"], f)[0]
    a_w_out = np.asarray(inp["a_w_out"], f)[0]
    b_w_in = np.asarray(inp["b_w_in"], f)[0]
    b_w_out = np.asarray(inp["b_w_out"], f)[0]
    items = []
    A5 = a_w_in.reshape(8, 128, 4, 16, 128)
    t = A5.transpose(3, 2, 1, 0, 4)
    for j in range(16):
        for half in range(2):
            blk = t[j, half * 2:half * 2 + 2]
            items.append(blk.transpose(1, 0, 2, 3).reshape(128, ITEM))
    def outw(w):
        W = w.reshape(8, 2, 128, 1024)
        return [W[i].transpose(1, 0, 2).reshape(128, ITEM) for i in range(8)]
    items += outw(a_w_out)
    def inw(wcols):
        W = wcols.reshape(8, 128, 8, 2, 128)
        return [W[:, :, i].transpose(1, 2, 0, 3).reshape(128, ITEM) for i in range(8)]
    items += inw(b_w_in[:, :2048])
    def sqw(w):
        W = w.reshape(16, 128, 16, 128)
        return [W[:, :, oc].transpose(1, 0, 2).reshape(128, ITEM) for oc in range(16)]
    items += sqw(np.asarray(inp["b_w_v"], f)[0])
    items += sqw(np.asarray(inp["b_w_q"], f)[0])
    items += sqw(np.asarray(inp["b_w_k"], f)[0])
    items += inw(b_w_in[:, 2048:])
    items += outw(b_w_out)
    assert len(items) == NITEMS
    wst = np.ascontiguousarray(np.stack(items, 0))

    par = np.zeros((128, NPAR), f)
    def fmaj(v):
        return np.asarray(v, f).reshape(16, 128).T
    ca = np.asarray(inp["a_conv_w"], f)[0]
    cb = np.asarray(inp["b_conv_w"], f)[0]
    for d in range(3):
        par[:, PC_CONVA + d * 16:PC_CONVA + (d + 1) * 16] = fmaj(ca[d])
    for d in range(4):
        par[:, PC_CONVB + d * 16:PC_CONVB + (d + 1) * 16] = fmaj(cb[d])
    par[:, PC_CB:PC_CB + 16] = fmaj(np.asarray(inp["b_conv_b"], f)[0])
    par[:, PC_SKIP:PC_SKIP + 16] = fmaj(np.asarray(inp["b_skip"], f)[0])
    par[:, PC_ONORM:PC_ONORM + 16] = fmaj(np.asarray(inp["b_onorm_w"], f)[0])
    par[:, PC_IDENT:PC_IDENT + 128] = np.eye(128, dtype=f)
    jj, ii = np.meshgrid(np.arange(64), np.arange(64), indexing="ij")
    par[:64, PC_NEG:PC_NEG + 64] = np.where(ii >= jj, 0.0, -30000.0).astype(f)
    for h in range(4):
        par[h, PC_EH + h * 64:PC_EH + (h + 1) * 64] = 1.0
    par[:4, PC_EYE4:PC_EYE4 + 4] = np.eye(4, dtype=f)
    par[:, PC_ONES:PC_ONES + 128] = 1.0
    for h in range(4):
        par[4 + h, PC_SEL + h] = 1.0
    par[:8, PC_BIF] = np.asarray(inp["b_b_if"], f)[0]
    tt = np.arange(512)
    par[:4, PC_RMASK:PC_RMASK + 512] = (tt % 64 != 0).astype(f)[None]
    par[:4, PC_AMASK:PC_AMASK + 512] = np.where(tt % 64 == 0, -1e30, 0.0).astype(f)[None]
    ts = np.arange(128)
    par[:4, PC_RMASK_S:PC_RMASK_S + 128] = (ts % 8 != 0).astype(f)[None]
    par[:4, PC_AMASK_S:PC_AMASK_S + 128] = np.where(ts % 8 == 0, -1e30, 0.0).astype(f)[None]

    cbf = np.zeros((128, NCB), f)
    cbf[:, CB_IDENT:CB_IDENT + 128] = np.eye(128, dtype=f)
    cbf[:, CB_ONES:CB_ONES + 8] = 1.0
    wif = np.asarray(inp["b_w_if"], f)[0]
    cbf[:, CB_WIF:CB_WIF + 384] = wif.reshape(48, 128, 8).transpose(1, 0, 2).reshape(128, 384)

    nw = np.asarray(inp["norm_w"], f)
    bc = np.stack([np.broadcast_to(nw[0], (128, DM)), np.broadcast_to(nw[1], (128, DM)),
                   np.broadcast_to(np.asarray(inp["final_norm_w"], f), (128, DM))], 0)
    bc = np.ascontiguousarray(bc)
    return wst, par, cbf, bc


_CACHE = {}


def kernel(**inp):
    f = np.float32
    wst, par, cbf, bc = _host_prep(inp)
    if "nc" not in _CACHE:
        _CACHE["nc"] = build_program()
    nc = _CACHE["nc"]
    xp = np.asarray(inp["x_prompt"], f)
    xs = np.asarray(inp["x_sample"], f)
    sca = np.asarray(inp["state_conv_a"], f)[0]
    scb = np.asarray(inp["state_conv_b"], f)[0]
    sC = np.asarray(inp["state_C"], f)[0]
    sn = np.asarray(inp["state_n"], f)[0]
    sm = np.asarray(inp["state_m"], f)[0]
    in_maps = []
    for c in range(NCORES):
        s0, s1 = 16 * c, 16 * c + 16
        in_maps.append({
            "xp": np.ascontiguousarray(xp[c]),
            "xs": np.ascontiguousarray(xs[s0:s1].reshape(128, DM)),
            "sca": np.ascontiguousarray(sca[s0:s1].reshape(32, AW)),
            "scb": np.ascontiguousarray(scb[s0:s1].reshape(48, AW)),
            "sC": np.ascontiguousarray(sC[s0:s1].reshape(64, 512, 512)),
            "sn": np.ascontiguousarray(sn[s0:s1].reshape(256, 128)),
            "sm": np.ascontiguousarray(sm[s0:s1].T),
            "wst": wst, "par": par, "cbf": cbf, "bc": bc,
        })
    res = run_bass_kernel_spmd(nc, in_maps, core_ids=list(range(NCORES)))
    R = res.results
    def cat(name, shp):
        return np.stack([np.asarray(R[c][name], f).reshape(shp) for c in range(NCORES)], 0)
    y_p = cat("yp", (2048, DM))
    y_s = cat("ys", (16, 8, DM)).reshape(128, 8, DM)
    ca_p = cat("cap", (2, AW))[None]
    ca_s = cat("cas", (16, 2, AW)).reshape(128, 2, AW)[None]
    cb_p = cat("cbp", (3, AW))[None]
    cb_s = cat("cbs", (16, 3, AW)).reshape(128, 3, AW)[None]
    C_p = cat("Cp", (4, 512, 512))[None]
    C_s = cat("Cs", (16, 4, 512, 512)).reshape(128, 4, 512, 512)[None]
    n_p = cat("np", (4, 512))[None]
    n_s = cat("ns", (16, 4, 512)).reshape(128, 4, 512)[None]
    m_p = cat("mp", (4,))[None]
    m_s = np.stack([np.asarray(R[c]["ms"], f).reshape(4, 16).T for c in range(NCORES)], 0).reshape(128, 4)[None]
    return (y_p, y_s, ca_p, ca_s, cb_p, cb_s, C_p, C_s, n_p, n_s, m_p, m_s)
```

```python
import math
import contextlib
import numpy as np
import concourse.bass as bass
import concourse.mybir as mybir
from concourse.bass_utils import run_bass_kernel_spmd

F32 = mybir.dt.float32
BF16 = mybir.dt.bfloat16
AF = mybir.ActivationFunctionType
ALU = mybir.AluOpType

NCORES = 8
DM = 1024
AW = 2048
NH = 4
DK = 512
RMS_EPS = 1e-6
LN_EPS = 1e-5
KSCALE = DK ** -0.5
NITEMS = 112
NSLOTS = 4
ITEM = 2048

PC_CONVA, PC_CONVB, PC_CB, PC_SKIP, PC_ONORM = 0, 48, 112, 128, 144
PC_IDENT = 160
PC_NEG = 288
PC_EH = 352
PC_EYE4 = 608
PC_ONES = 612
PC_SEL = 740
PC_BIF = 744
PC_RMASK = 745
PC_AMASK = 1257
PC_RMASK_S = 1769
PC_AMASK_S = 1897
NPAR = 2025
CB_IDENT, CB_ONES, CB_WIF = 0, 128, 136
NCB = 520


class Op:
    __slots__ = ("eng", "fn", "deps", "needs_inc", "val", "is_dma", "sem", "dma_val")

    def __init__(self, eng, fn, is_dma=False):
        self.eng = eng
        self.fn = fn
        self.deps = []
        self.needs_inc = False
        self.val = None
        self.is_dma = is_dma
        self.sem = None
        self.dma_val = None


class Prog:
    ENGS = ("pe", "act", "dve", "pool", "sp")

    def __init__(self, nc, n_dma_sems=32):
        self.nc = nc
        self.eng = {"pe": nc.tensor, "act": nc.scalar, "dve": nc.vector, "pool": nc.gpsimd, "sp": nc.sync}
        self.ops = []
        self.last_w = {}
        self.readers = {}
        self.n_dma_sems = n_dma_sems
        self.dma_rr = 0
        self.dma_rr_sw = 0
        self.dma_sem_last = [None] * n_dma_sems
        self.dma_sem_count = [0] * n_dma_sems
        self.out_dmas = []

    def _add_dep(self, op, d):
        if d is None or d is op:
            return
        if d.eng == op.eng and op.eng == "pe" and not d.is_dma and not op.is_dma:
            return
        op.deps.append(d)
        if not d.is_dma:
            d.needs_inc = True

    def op(self, eng, fn, reads=(), writes=(), is_dma=False, is_out=False):
        o = Op(eng, fn, is_dma)
        for t in reads:
            self._add_dep(o, self.last_w.get(t))
        for t in writes:
            self._add_dep(o, self.last_w.get(t))
            for r in self.readers.get(t, ()):
                if r.eng == eng and not r.is_dma and not is_dma:
                    continue
                self._add_dep(o, r)
        for t in reads:
            self.readers.setdefault(t, []).append(o)
        for t in writes:
            self.last_w[t] = o
            self.readers[t] = []
        if is_dma:
            if eng == "pool":
                s = self.dma_rr_sw
                self.dma_rr_sw = (self.dma_rr_sw + 1) % 8
            else:
                s = 8 + self.dma_rr
                self.dma_rr = (self.dma_rr + 1) % (self.n_dma_sems - 8)
            prev = self.dma_sem_last[s]
            if prev is not None:
                o.deps.append(prev)
            self.dma_sem_count[s] += 16
            o.sem = s
            o.dma_val = self.dma_sem_count[s]
            self.dma_sem_last[s] = o
            if is_out:
                self.out_dmas.append(o)
        self.ops.append(o)
        return o

    def dma(self, eng, out, in_, reads=(), writes=(), is_out=False):
        return self.op(eng, lambda e: e.dma_start(out=out, in_=in_), reads, writes, is_dma=True, is_out=is_out)

    def emit(self, sems, dma_sems):
        cnt = {e: 0 for e in self.ENGS}
        for o in self.ops:
            if o.needs_inc and not o.is_dma:
                cnt[o.eng] += 1
                o.val = cnt[o.eng]
        waited = {e: {} for e in self.ENGS}
        import os
        maxops = int(os.environ.get("MK_MAXOPS", "0"))
        if maxops:
            self.ops = self.ops[:maxops]
            self.out_dmas = [o for o in self.out_dmas if o in set(self.ops)]
        for o in self.ops:
            e = self.eng[o.eng]
            w = waited[o.eng]
            need = {}
            for d in o.deps:
                if d.is_dma:
                    key, v = ("d", d.sem), d.dma_val
                else:
                    key, v = ("e", d.eng), d.val
                if need.get(key, 0) < v:
                    need[key] = v
            for key, v in need.items():
                if w.get(key, 0) >= v:
                    continue
                w[key] = v
                e.wait_ge(dma_sems[key[1]] if key[0] == "d" else sems[key[1]], v)
            inst = o.fn(e)
            if o.is_dma:
                inst.then_inc(dma_sems[o.sem], 16)
            elif o.needs_inc:
                inst.then_inc(sems[o.eng], 1)
        e = self.eng["sp"]
        fin = {}
        for o in self.out_dmas:
            fin[o.sem] = max(fin.get(o.sem, 0), o.dma_val)
        for s, v in fin.items():
            e.wait_ge(dma_sems[s], v)


def build_program(n_ptiles=4, do_sample=True, dbg=False):
    nc = bass.Bass("TRN2", target_bir_lowering=False)

    def din(name, shape):
        return nc.dram_tensor(name, shape, F32, kind="ExternalInput").ap()

    def dout(name, shape):
        return nc.dram_tensor(name, shape, F32, kind="ExternalOutput").ap()

    xp = din("xp", [2048, DM])
    xs = din("xs", [128, DM])
    sca = din("sca", [32, AW])
    scb = din("scb", [48, AW])
    sC = din("sC", [64, 512, 512])
    sn = din("sn", [256, 128])
    sm = din("sm", [4, 16])
    wst = din("wst", [NITEMS, 128, ITEM])
    par_d = din("par", [128, NPAR])
    cbf_d = din("cbf", [128, NCB])
    bc_d = din("bc", [3, 128, DM])

    yp = dout("yp", [2048, DM])
    ys = dout("ys", [128, DM])
    cap = dout("cap", [2, AW])
    cas = dout("cas", [32, AW])
    cbp = dout("cbp", [3, AW])
    cbs = dout("cbs", [48, AW])
    Cp = dout("Cp", [4, 512, 512])
    Cs = dout("Cs", [64, 512, 512])
    np_o = dout("np", [16, 128])
    ns_o = dout("ns", [256, 128])
    mp_o = dout("mp", [4, 1])
    ms_o = dout("ms", [4, 16])

    es = contextlib.ExitStack()
    with es:
        AR_WORDS = 52600
        arena = es.enter_context(nc.sbuf_tensor("arena", [128, AR_WORDS], F32))
        psb = [es.enter_context(nc.psum_tensor(f"psb{i}", [128, 512], F32)) for i in range(8)]
        sems = {e: es.enter_context(nc.semaphore(f"sem_{e}")) for e in Prog.ENGS}
        dsems = [es.enter_context(nc.semaphore(f"dsem{i}")) for i in range(32)]
        p = Prog(nc)
        A = p.op

        cur = [0]
        PAGE = 32

        class V:
            __slots__ = ("ap", "tok")

            def __init__(self, ap, tok):
                self.ap = ap
                self.tok = tok

        def alloc(words):
            words = (words + PAGE - 1) // PAGE * PAGE
            o = cur[0]
            cur[0] += words
            assert cur[0] <= AR_WORDS, f"arena overflow {cur[0]}"
            return o

        def view(off, words, dtype=F32, parts=128):
            ap = arena[:parts, off:off + words]
            if dtype != F32:
                ap = ap.bitcast(dtype)
            toks = [("a", pg) for pg in range(off // PAGE, (off + words - 1) // PAGE + 1)]
            return V(ap, toks)

        o_w = alloc(NSLOTS * 1024)
        wslot = [view(o_w + i * 1024, 1024, BF16) for i in range(NSLOTS)]
        o_x = alloc(4096)
        Xv = view(o_x, 4096)
        o_ht = alloc(2048)
        HTv = view(o_ht, 2048, BF16)
        o_B = [alloc(4096) for _ in range(5)]
        o_cst = alloc(4 * 2048)
        CSTv = [view(o_cst + i * 2048, 2048) for i in range(4)]
        o_cbf = alloc(2 * 1024)
        CBFv = [view(o_cbf + i * 1024, 1024, BF16) for i in range(2)]
        o_par = alloc(NPAR)
        PARv = view(o_par, NPAR)
        PAR = PARv.ap
        o_cb = alloc(NCB // 2)
        CBv = view(o_cb, NCB // 2, BF16)
        CB = CBv.ap
        o_hist = alloc(416)
        HISTAv = view(o_hist, 32)
        HISTBv = view(o_hist + 32, 48)
        MALLv = view(o_hist + 96, 24, parts=4)
        NSTh = [view(o_hist + 128 + h * 32, 4) for h in range(4)]
        NBFh = [view(o_hist + 256 + h * 32, 2, BF16) for h in range(4)]
        NSTC = view(o_hist + 384, 16)
        o_g = alloc(512 * 2 + 192 + 64)
        CCv = view(o_g, 512, parts=4)
        NEGAv = view(o_g + 512, 512, parts=4)
        COLQv = view(o_g + 1024, 192, parts=64)
        DECBv = view(o_g + 1216, 64)
        o_scr = cur[0]
        SCR_WORDS = AR_WORDS - o_scr

        def scr(off, words, dtype=F32, parts=128):
            assert off + words <= SCR_WORDS, f"scratch overflow {off + words} > {SCR_WORDS}"
            return view(o_scr + off, words, dtype, parts)

        def dump(name, v, dtype):
            if not dbg:
                return
            shp = list(v.ap.shape)
            d = nc.dram_tensor(name, shp, dtype, kind="ExternalOutput").ap()
            p.dma("sp", d, v.ap, reads=v.tok, is_out=True)

        psrr = [0]

        def newps():
            i = psrr[0]
            psrr[0] = (i + 1) % 8
            return i

        def PT(i):
            return [("ps", i)]

        def psf(i):
            return psb[i][:]

        def psh(i):
            return psb[i][:].bitcast(BF16)

        wg = [0]
        wissued = [0]
        total_items = NITEMS * (n_ptiles + (1 if do_sample else 0))

        wbf = nc.dram_tensor("wbf_cache", [NITEMS, 128, ITEM], BF16, kind=("ExternalOutput" if dbg else "Internal")).ap()
        use_cache = total_items > NITEMS

        def w_issue_upto(g):
            while wissued[0] <= g and wissued[0] < total_items:
                gi = wissued[0]
                s = gi % NSLOTS
                if gi < NITEMS or not use_cache:
                    p.dma("pool", wslot[s].ap, wst[gi % NITEMS], writes=wslot[s].tok)
                else:
                    p.dma("pool", wslot[s].ap, wbf[gi % NITEMS], reads=[("wd", gi % NITEMS)], writes=wslot[s].tok)
                wissued[0] += 1

        def wget():
            g = wg[0]
            wg[0] += 1
            w_issue_upto(g + NSLOTS - 1)
            if use_cache and g < NITEMS:
                p.dma("sp", wbf[g], wslot[g % NSLOTS].ap, reads=wslot[g % NSLOTS].tok, writes=[("wd", g)])
            return wslot[g % NSLOTS]

        p.dma("sp", PAR, par_d, writes=PARv.tok)
        p.dma("pool", CB, cbf_d, writes=CBv.tok)
        ident_bf = CB[:, CB_IDENT:CB_IDENT + 128]
        ones_bf = CB[:, CB_ONES:CB_ONES + 8]
        wif_bf = CB[:, CB_WIF:CB_WIF + 384].rearrange("p (k g) -> p k g", g=8)
        ident_f = PAR[:, PC_IDENT:PC_IDENT + 128]
        CT = PARv.tok
        CBT = CBv.tok

        for h in range(4):
            A("pool", lambda e, h=h: e.memset(CSTv[h].ap, 0.0), writes=CSTv[h].tok)
        hist_all = view(o_hist, 416)
        A("pool", lambda e: e.memset(hist_all.ap, 0.0), writes=hist_all.tok)

        def run_tile(kind, ti):
            samp = kind == "s"
            T = 128 if samp else 512
            NB = T // 128
            L = 8 if samp else 64
            NCH = T // L
            HBA, HBB = 2, 3
            last_p = (not samp) and ti == n_ptiles - 1

            def Bview(i, words=None, dtype=BF16, off=0):
                return view(o_B[i] + off, words if words is not None else 16 * T // 2, dtype)

            def ctok(i, fc):
                cw = T // 2
                o0 = o_B[i] + fc * cw
                return [("a", pg) for pg in range(o0 // PAGE, (o0 + cw - 1) // PAGE + 1)]

            def fm(v):
                return v.ap.rearrange("p (c t) -> p c t", t=T)

            B1, B2, B3, B4, B5 = [Bview(i) for i in range(5)]
            HT = view(o_ht, 8 * T // 2, BF16)
            hT = HT.ap.rearrange("p (c t) -> p c t", t=T)
            X = view(o_x, NB * 1024)
            Xt = X.ap.rearrange("p (b f) -> p b f", f=DM)

            s_junk = scr(0, 512, BF16)
            s_hb = [scr(512, 512, BF16), scr(1024, 512, BF16)]
            s_ss = [scr(1536, 4), scr(1568, 4), scr(6784, 4), scr(6816, 4)]
            s_nw = scr(1600, 1024)
            s_xe = [scr(2624, 520), scr(3168, 520)]
            s_xa = [scr(3712, 512), scr(4224, 512)]
            s_sz = [scr(4736, 512), scr(5248, 512)]
            s_tt = scr(5760, 512)
            s_acc = scr(6272, 512)
            s_ost = [scr(2624, 1024), scr(3648, 1024)]

            if samp:
                p.dma("sp", Xt, xs.rearrange("(b p) f -> p b f", p=128), writes=X.tok)
            else:
                p.dma("sp", Xt, xp[ti * 512:(ti + 1) * 512, :].rearrange("(b p) f -> p b f", p=128), writes=X.tok)

            def load_nw(i):
                p.dma("sp", s_nw.ap, bc_d[i], writes=s_nw.tok)

            def rms_stats(b, k):
                ss = s_ss[k]
                A("act", lambda e: e.activation(out=s_junk.ap, in_=Xt[:, b, :], func=AF.Square, accum_out=ss.ap[:, 0:1]),
                  reads=X.tok, writes=s_junk.tok + ss.tok)
                A("act", lambda e: e.activation(out=ss.ap[:, 1:2], in_=ss.ap[:, 0:1], func=AF.Ln, scale=1.0 / DM, bias=RMS_EPS),
                  reads=ss.tok, writes=ss.tok)
                A("act", lambda e: e.activation(out=ss.ap[:, 2:3], in_=ss.ap[:, 1:2], func=AF.Exp, scale=-0.5),
                  reads=ss.tok, writes=ss.tok)

            def norm_to_hT():
                for b in range(NB):
                    rms_stats(b, b)
                for b in range(NB):
                    k = b % 2
                    hb = s_hb[k]
                    A("dve", lambda e, b=b, k=k, hb=hb: e.scalar_tensor_tensor(
                        out=hb.ap, in0=Xt[:, b, :], scalar=s_ss[b].ap[:, 2:3], in1=s_nw.ap, op0=ALU.mult, op1=ALU.mult),
                      reads=X.tok + s_ss[b].tok + s_nw.tok, writes=hb.tok)
                    pi = newps()
                    for c in range(8):
                        A("pe", lambda e, c=c, pi=pi, hb=hb: e.transpose(
                            out=psh(pi)[:, c * 128:(c + 1) * 128], in_=hb.ap[:, c * 128:(c + 1) * 128], identity=ident_bf),
                          reads=hb.tok + CBT, writes=PT(pi))
                    A("act", lambda e, b=b, pi=pi: e.copy(
                        out=hT[:, :, b * 128:(b + 1) * 128], in_=psh(pi)[:, 0:1024].rearrange("p (c t) -> p c t", t=128)),
                      reads=PT(pi), writes=HT.tok)

            print("mark L0 start", len(p.ops))
            load_nw(0)
            norm_to_hT()
            if ti == 0 and not samp:
                dump("d_hT", HT, BF16)

            yT = fm(B5)
            if samp:
                s_rows = Bview(1, 2048, F32, off=1024)
                HAS = Bview(0, 512, F32, off=1024)
                hist_s = HAS.ap.rearrange("p (c r) -> p c r", r=32)
                p.dma("sp", s_rows.ap[:32, :], sca, writes=s_rows.tok)
                for g in range(4):
                    pi = newps()
                    for c4 in range(4):
                        c = g * 4 + c4
                        A("pe", lambda e, c=c, c4=c4, pi=pi: e.transpose(
                            out=psf(pi)[:, c4 * 32:(c4 + 1) * 32], in_=s_rows.ap[:32, c * 128:(c + 1) * 128], identity=ident_f[:32, :32]),
                          reads=s_rows.tok + CT, writes=PT(pi))
                    A("dve", lambda e, g=g, pi=pi: e.tensor_copy(
                        out=hist_s[:, g * 4:(g + 1) * 4, :], in_=psf(pi)[:, 0:128].rearrange("p (c r) -> p c r", r=32)),
                      reads=PT(pi), writes=HAS.tok)
                NHA = view(o_B[0] + 1024 + 512, 512, F32)
                nhist_s = NHA.ap.rearrange("p (c r) -> p c r", r=32)

            for j in range(16):
                pis = [newps() for _ in range(4)]
                for bi in range(4):
                    bl = bi % 2
                    if bl == 0:
                        wt = wget()
                        wv = wt.ap.rearrange("p (b k n) -> p b k n", b=2, k=8)
                    for k in range(8):
                        A("pe", lambda e, wv=wv, bl=bl, k=k, pi=pis[bi]: e.matmul(
                            psf(pi)[:, :T], lhsT=wv[:, bl, k, :], rhs=hT[:, k, :], start=(k == 0), stop=(k == 7)),
                          reads=wt.tok + HT.tok, writes=PT(pis[bi]))
                pb, pc, pxa, pz = pis
                k2 = j % 2
                xe, xa, sz = s_xe[k2], s_xa[k2], s_sz[k2]
                A("act", lambda e, xa=xa, pxa=pxa: e.copy(out=xa.ap[:, :T], in_=psf(pxa)[:, :T]), reads=PT(pxa), writes=xa.tok)
                A("act", lambda e, sz=sz, pz=pz: e.activation(out=sz.ap[:, :T], in_=psf(pz)[:, :T], func=AF.Silu),
                  reads=PT(pz), writes=sz.tok)
                if ti == 0 and not samp and j == 0:
                    dump("d_xa0", xa, F32)
                    dump("d_sz0", sz, F32)
                if samp:
                    xe3 = xe.ap[:, 0:160].rearrange("p (s t) -> p s t", t=10)
                    A("dve", lambda e, xe3=xe3, j=j: e.tensor_copy(
                        out=xe3[:, :, 0:2], in_=hist_s[:, j, :].rearrange("p (s r) -> p s r", r=2)),
                      reads=HAS.tok, writes=xe.tok)
                    A("dve", lambda e, xe3=xe3, xa=xa, pc=pc: e.tensor_tensor(
                        out=xe3[:, :, 2:10], in0=psf(pc)[:, :128].rearrange("p (s t) -> p s t", t=8),
                        in1=xa.ap[:, :128].rearrange("p (s t) -> p s t", t=8), op=ALU.mult),
                      reads=PT(pc) + xa.tok, writes=xe.tok)
                    acc3 = s_acc.ap[:, :128].rearrange("p (s t) -> p s t", t=8)
                    win = [xe3[:, :, d:d + 8] for d in range(3)]
                    accv = acc3
                else:
                    A("dve", lambda e, xe=xe, j=j: e.tensor_copy(out=xe.ap[:, 0:2], in_=HISTAv.ap[:, j * 2:j * 2 + 2]),
                      reads=HISTAv.tok, writes=xe.tok)
                    A("dve", lambda e, xe=xe, xa=xa, pc=pc: e.tensor_tensor(
                        out=xe.ap[:, 2:2 + T], in0=psf(pc)[:, :T], in1=xa.ap[:, :T], op=ALU.mult),
                      reads=PT(pc) + xa.tok, writes=xe.tok)
                    win = [xe.ap[:, d:d + T] for d in range(3)]
                    accv = s_acc.ap[:, :T]
                A("dve", lambda e, sz=sz, pb=pb: e.tensor_tensor(out=s_tt.ap[:, :T], in0=psf(pb)[:, :T], in1=sz.ap[:, :T], op=ALU.mult),
                  reads=PT(pb) + sz.tok, writes=s_tt.tok)
                cw = [PAR[:, PC_CONVA + d * 16 + j:PC_CONVA + d * 16 + j + 1] for d in range(3)]
                A("dve", lambda e, accv=accv, win=win, cw=cw: e.tensor_scalar(
                    out=accv, in0=win[0], scalar1=cw[0], scalar2=None, op0=ALU.mult),
                  reads=xe.tok + CT, writes=s_acc.tok)
                for d in (1, 2):
                    A("dve", lambda e, accv=accv, win=win, cw=cw, d=d: e.scalar_tensor_tensor(
                        out=accv, in0=win[d], scalar=cw[d], in1=accv, op0=ALU.mult, op1=ALU.add),
                      reads=xe.tok + CT + s_acc.tok, writes=s_acc.tok)
                A("dve", lambda e, j=j: e.tensor_tensor(out=yT[:, j, :], in0=s_tt.ap[:, :T], in1=s_acc.ap[:, :T], op=ALU.mult),
                  reads=s_tt.tok + s_acc.tok, writes=ctok(4, j))
                if samp:
                    A("act", lambda e, xe3=xe3, j=j: e.copy(
                        out=nhist_s[:, j, :].rearrange("p (s r) -> p s r", r=2), in_=xe3[:, :, 8:10]),
                      reads=xe.tok, writes=NHA.tok)
                else:
                    A("act", lambda e, xe=xe, j=j: e.copy(out=HISTAv.ap[:, j * 2:j * 2 + 2], in_=xe.ap[:, T:T + 2]),
                      reads=xe.tok, writes=HISTAv.tok)

            if samp or last_p:
                nr = 32 if samp else 2
                src3 = nhist_s if samp else HISTAv.ap.rearrange("p (c r) -> p c r", r=2)
                srct = NHA.tok if samp else HISTAv.tok
                stg = scr(0, 2048, F32)
                for g in range(4):
                    pi = newps()
                    for c4 in range(4):
                        c = g * 4 + c4
                        A("pe", lambda e, c=c, c4=c4, pi=pi, src3=src3, nr=nr: e.matmul(
                            psf(pi)[:nr, c4 * 128:(c4 + 1) * 128], lhsT=src3[:, c, :], rhs=ident_f, start=True, stop=True),
                          reads=srct + CT, writes=PT(pi))
                    A("act", lambda e, g=g, pi=pi: e.copy(out=stg.ap[:nr, g * 512:(g + 1) * 512], in_=psf(pi)[:nr, :]),
                      reads=PT(pi), writes=stg.tok)
                p.dma("sp", cas if samp else cap, stg.ap[:nr, :], reads=stg.tok, is_out=True)

            def outproj(srcT, srcbuf):
                pss = [[newps(), newps()] for _ in range(NB)]
                for it in range(8):
                    w = wget()
                    wv = w.ap.rearrange("p (k n) -> p k n", k=2)
                    for k2 in range(2):
                        kc = it * 2 + k2
                        for b in range(NB):
                            for hf in range(2):
                                A("pe", lambda e, wv=wv, k2=k2, kc=kc, b=b, hf=hf, pi=pss[b][hf]: e.matmul(
                                    psf(pi)[:, :512], lhsT=srcT[:, kc, b * 128:(b + 1) * 128],
                                    rhs=wv[:, k2, hf * 512:(hf + 1) * 512], start=(kc == 0), stop=(kc == 15)),
                                  reads=w.tok + ctok(srcbuf, kc), writes=PT(pss[b][hf]))
                for b in range(NB):
                    for hf in range(2):
                        A("dve", lambda e, b=b, hf=hf, pi=pss[b][hf]: e.tensor_tensor(
                            out=Xt[:, b, hf * 512:(hf + 1) * 512], in0=psf(pi)[:, :512],
                            in1=Xt[:, b, hf * 512:(hf + 1) * 512], op=ALU.add),
                          reads=PT(pss[b][hf]) + X.tok, writes=X.tok)

            print("mark L0 outproj", len(p.ops))
            if ti == 0 and not samp:
                dump("d_yT", B5, BF16)
            outproj(yT, 4)
            if ti == 0 and not samp:
                dump("d_x1", X, F32)
            print("mark L1 start", len(p.ops))

            load_nw(1)
            norm_to_hT()
            xmT, xcT, vT, qT, kT = fm(B1), fm(B2), fm(B3), fm(B4), fm(B5)
            ogT, szT = xmT, qT

            if samp:
                s_rows_b = Bview(2, 2048, F32, off=1024)
                HBS = view(o_B[0] + 2048, 768, F32)
                NHB = view(o_B[0] + 2816, 768, F32)
                histb_s = HBS.ap.rearrange("p (c r) -> p c r", r=48)
                nhistb_s = NHB.ap.rearrange("p (c r) -> p c r", r=48)
                A("pool", lambda e: e.memset(s_rows_b.ap[:64, :], 0.0), writes=s_rows_b.tok)
                p.dma("sp", s_rows_b.ap[:48, :], scb, writes=s_rows_b.tok)
                for g in range(2):
                    pi = newps()
                    for c8 in range(8):
                        c = g * 8 + c8
                        A("pe", lambda e, c=c, c8=c8, pi=pi: e.transpose(
                            out=psf(pi)[:, c8 * 64:(c8 + 1) * 64], in_=s_rows_b.ap[:64, c * 128:(c + 1) * 128], identity=ident_f[:64, :64]),
                          reads=s_rows_b.tok + CT, writes=PT(pi))
                    A("dve", lambda e, g=g, pi=pi: e.tensor_copy(
                        out=histb_s[:, g * 8:(g + 1) * 8, :], in_=psf(pi)[:, 0:512].rearrange("p (c r) -> p c r", r=64)[:, :, 0:48]),
                      reads=PT(pi), writes=HBS.tok)

            for it in range(8):
                w = wget()
                wv = w.ap.rearrange("p (o k n) -> p o k n", o=2, k=8)
                for o2 in range(2):
                    oc = it * 2 + o2
                    pi = newps()
                    for k in range(8):
                        A("pe", lambda e, wv=wv, o2=o2, k=k, pi=pi: e.matmul(
                            psf(pi)[:, :T], lhsT=wv[:, o2, k, :], rhs=hT[:, k, :], start=(k == 0), stop=(k == 7)),
                          reads=w.tok + HT.tok, writes=PT(pi))
                    xe = s_xe[oc % 2]
                    if samp:
                        xe3 = xe.ap[:, 0:176].rearrange("p (s t) -> p s t", t=11)
                        A("dve", lambda e, xe3=xe3, oc=oc: e.tensor_copy(
                            out=xe3[:, :, 0:3], in_=histb_s[:, oc, :].rearrange("p (s r) -> p s r", r=3)),
                          reads=HBS.tok, writes=xe.tok)
                        A("act", lambda e, xe3=xe3, pi=pi: e.copy(
                            out=xe3[:, :, 3:11], in_=psf(pi)[:, :128].rearrange("p (s t) -> p s t", t=8)),
                          reads=PT(pi), writes=xe.tok)
                        A("act", lambda e, oc=oc, pi=pi: e.copy(out=xmT[:, oc, :], in_=psf(pi)[:, :T]),
                          reads=PT(pi), writes=B1.tok)
                        win = [xe3[:, :, d:d + 8] for d in range(4)]
                        accv = s_acc.ap[:, :128].rearrange("p (s t) -> p s t", t=8)
                    else:
                        A("dve", lambda e, xe=xe, oc=oc: e.tensor_copy(out=xe.ap[:, 0:3], in_=HISTBv.ap[:, oc * 3:oc * 3 + 3]),
                          reads=HISTBv.tok, writes=xe.tok)
                        A("act", lambda e, xe=xe, pi=pi: e.copy(out=xe.ap[:, 3:3 + T], in_=psf(pi)[:, :T]),
                          reads=PT(pi), writes=xe.tok)
                        A("act", lambda e, oc=oc, pi=pi: e.copy(out=xmT[:, oc, :], in_=psf(pi)[:, :T]),
                          reads=PT(pi), writes=B1.tok)
                        win = [xe.ap[:, d:d + T] for d in range(4)]
                        accv = s_acc.ap[:, :T]
                    cw = [PAR[:, PC_CONVB + d * 16 + oc:PC_CONVB + d * 16 + oc + 1] for d in range(4)]
                    A("act", lambda e, accv=accv, win=win, cw=cw: e.activation(out=accv, in_=win[0], func=AF.Copy, scale=cw[0]),
                      reads=xe.tok + CT, writes=s_acc.tok)
                    for d in (1, 2, 3):
                        A("dve", lambda e, accv=accv, win=win, cw=cw, d=d: e.scalar_tensor_tensor(
                            out=accv, in0=win[d], scalar=cw[d], in1=accv, op0=ALU.mult, op1=ALU.add),
                          reads=xe.tok + CT + s_acc.tok, writes=s_acc.tok)
                    A("act", lambda e, oc=oc: e.activation(out=xcT[:, oc, :], in_=s_acc.ap[:, :T], func=AF.Silu,
                                                           bias=PAR[:, PC_CB + oc:PC_CB + oc + 1]),
                      reads=s_acc.tok + CT, writes=B2.tok)
                    if samp:
                        A("act", lambda e, xe3=xe3, oc=oc: e.copy(
                            out=nhistb_s[:, oc, :].rearrange("p (s r) -> p s r", r=3), in_=xe3[:, :, 8:11]),
                          reads=xe.tok, writes=NHB.tok)
                    else:
                        A("act", lambda e, xe=xe, oc=oc: e.copy(out=HISTBv.ap[:, oc * 3:oc * 3 + 3], in_=xe.ap[:, T:T + 3]),
                          reads=xe.tok, writes=HISTBv.tok)

            if samp or last_p:
                nr = 48 if samp else 3
                src3 = nhistb_s if samp else HISTBv.ap.rearrange("p (c r) -> p c r", r=3)
                srct = NHB.tok if samp else HISTBv.tok
                stg = scr(0, 2048, F32)
                for g in range(4):
                    pi = newps()
                    for c4 in range(4):
                        c = g * 4 + c4
                        A("pe", lambda e, c=c, c4=c4, pi=pi, src3=src3, nr=nr: e.matmul(
                            psf(pi)[:nr, c4 * 128:(c4 + 1) * 128], lhsT=src3[:, c, :], rhs=ident_f, start=True, stop=True),
                          reads=srct + CT, writes=PT(pi))
                    A("act", lambda e, g=g, pi=pi: e.copy(out=stg.ap[:nr, g * 512:(g + 1) * 512], in_=psf(pi)[:nr, :]),
                      reads=PT(pi), writes=stg.tok)
                p.dma("sp", cbs if samp else cbp, stg.ap[:nr, :], reads=stg.tok, is_out=True)

            print("mark vqk", len(p.ops))
            cnt_ev = [0]
            for (dst, dv_, src, srct) in [(vT, B3, xmT, B1.tok), (qT, B4, xcT, B2.tok), (kT, B5, xcT, B2.tok)]:
                for oc in range(16):
                    w = wget()
                    wv = w.ap.rearrange("p (k n) -> p k n", k=16)
                    pi = newps()
                    for k in range(16):
                        A("pe", lambda e, wv=wv, k=k, pi=pi, src=src: e.matmul(
                            psf(pi)[:, :T], lhsT=wv[:, k, :], rhs=src[:, k, :], start=(k == 0), stop=(k == 15)),
                          reads=w.tok + srct, writes=PT(pi))
                    if cnt_ev[0] % 2 == 0:
                        A("act", lambda e, dst=dst, oc=oc, pi=pi: e.copy(out=dst[:, oc, :], in_=psf(pi)[:, :T]),
                          reads=PT(pi), writes=dv_.tok)
                    else:
                        A("dve", lambda e, dst=dst, oc=oc, pi=pi: e.tensor_copy(out=dst[:, oc, :], in_=psf(pi)[:, :T]),
                          reads=PT(pi), writes=dv_.tok)
                    cnt_ev[0] += 1

            print("mark gates", len(p.ops))
            pg = newps()
            gsrc = [(qT, B4.tok)] * 16 + [(kT, B5.tok)] * 16 + [(vT, B3.tok)] * 16
            for kc in range(48):
                A("pe", lambda e, kc=kc, pg=pg: e.matmul(
                    psf(pg)[:8, :T], lhsT=wif_bf[:, kc, :], rhs=gsrc[kc][0][:, kc % 16, :], start=(kc == 0), stop=(kc == 47)),
                  reads=CBT + gsrc[kc][1], writes=PT(pg))
            GSB = scr(3072, 512, parts=8)
            A("act", lambda e: e.activation(out=GSB.ap[:8, :T], in_=psf(pg)[:8, :T], func=AF.Identity,
                                            bias=PAR[:8, PC_BIF:PC_BIF + 1]),
              reads=PT(pg) + CT, writes=GSB.tok)
            pf = newps()
            A("pe", lambda e: e.matmul(psf(pf)[:4, :T], lhsT=PAR[:8, PC_SEL:PC_SEL + 4], rhs=GSB.ap[:8, :T], start=True, stop=True),
              reads=GSB.tok + CT, writes=PT(pf))
            G = [scr(i * 512, 512, parts=4) for i in range(6)]
            g_ = [g.ap[:4, :T] for g in G]
            CC = CCv.ap[:4, :T]
            NEGA = NEGAv.ap[:4, :T]
            rmask = PAR[:4, (PC_RMASK_S if samp else PC_RMASK):(PC_RMASK_S if samp else PC_RMASK) + T]
            amask = PAR[:4, (PC_AMASK_S if samp else PC_AMASK):(PC_AMASK_S if samp else PC_AMASK) + T]
            A("act", lambda e: e.copy(out=g_[2], in_=psf(pf)[:4, :T]), reads=PT(pf), writes=G[2].tok)
            A("dve", lambda e: e.scalar_tensor_tensor(out=g_[0], in0=g_[2], scalar=-1.0, in1=g_[2], op0=ALU.mult, op1=ALU.max),
              reads=G[2].tok, writes=G[0].tok)
            A("act", lambda e: e.activation(out=g_[1], in_=g_[0], func=AF.Exp, scale=-1.0), reads=G[0].tok, writes=G[1].tok)
            A("act", lambda e: e.activation(out=g_[1], in_=g_[1], func=AF.Ln, bias=1.0), reads=G[1].tok, writes=G[1].tok)
            A("dve", lambda e: e.tensor_scalar_min(out=g_[0], in0=g_[2], scalar1=0.0), reads=G[2].tok, writes=G[0].tok)
            A("dve", lambda e: e.tensor_sub(out=g_[0], in0=g_[0], in1=g_[1]), reads=G[0].tok + G[1].tok, writes=G[0].tok)
            A("dve", lambda e: e.tensor_tensor_scan(out=g_[5], data0=rmask, data1=g_[0], initial=0.0, op0=ALU.mult, op1=ALU.add),
              reads=G[0].tok + CT, writes=G[5].tok)
            A("dve", lambda e: e.tensor_sub(out=CC, in0=GSB.ap[:4, :T], in1=g_[5]), reads=GSB.tok + G[5].tok, writes=CCv.tok)
            A("dve", lambda e: e.tensor_tensor_scan(out=g_[1], data0=amask, data1=CC, initial=0.0, op0=ALU.add, op1=ALU.max),
              reads=CCv.tok + CT, writes=G[1].tok)
            MALL = MALLv.ap
            MT = scr(3584 + 64, 32, parts=4)
            if samp:
                p.dma("sp", MALL[:4, 0:16], sm, writes=MALLv.tok)
            else:
                for c in range(NCH):
                    le = c * L + L - 1
                    A("dve", lambda e, c=c, le=le: e.tensor_tensor(out=MT.ap[:4, 0:1], in0=g_[1][:, le:le + 1], in1=MALL[:4, c:c + 1], op=ALU.max),
                      reads=G[1].tok + MALLv.tok, writes=MT.tok)
                    A("dve", lambda e, c=c, le=le: e.tensor_tensor(out=MALL[:4, c + 1:c + 2], in0=MT.ap[:4, 0:1], in1=g_[5][:, le:le + 1], op=ALU.add),
                      reads=MT.tok + G[5].tok, writes=MALLv.tok)

            def v3(ap):
                return ap.rearrange("p (c l) -> p c l", l=L)

            mprev_bc = MALL[:4, 0:NCH].unsqueeze(2).to_broadcast([4, NCH, L])
            A("dve", lambda e: e.tensor_tensor(out=v3(g_[2]), in0=v3(g_[1]), in1=mprev_bc, op=ALU.max),
              reads=G[1].tok + MALLv.tok, writes=G[2].tok)
            A("dve", lambda e: e.tensor_scalar(out=NEGA, in0=g_[2], scalar1=-1.0, scalar2=None, op0=ALU.mult),
              reads=G[2].tok, writes=NEGAv.tok)
            A("dve", lambda e: e.tensor_tensor(out=g_[3], in0=g_[5], in1=g_[2], op=ALU.add), reads=G[5].tok + G[2].tok, writes=G[3].tok)
            A("act", lambda e: e.activation(out=g_[3], in_=g_[3], func=AF.Exp, scale=-1.0), reads=G[3].tok, writes=G[3].tok)
            A("dve", lambda e: e.tensor_tensor(out=v3(g_[4]), in0=mprev_bc, in1=v3(g_[2]), op=ALU.subtract),
              reads=G[2].tok + MALLv.tok, writes=G[4].tok)
            A("act", lambda e: e.activation(out=g_[4], in_=g_[4], func=AF.Exp), reads=G[4].tok, writes=G[4].tok)
            alast_bc = v3(g_[2])[:, :, L - 1:L].to_broadcast([4, NCH, L])
            A("dve", lambda e: e.tensor_tensor(out=v3(g_[0]), in0=v3(CC), in1=alast_bc, op=ALU.subtract),
              reads=CCv.tok + G[2].tok, writes=G[0].tok)
            A("act", lambda e: e.activation(out=g_[0], in_=g_[0], func=AF.Exp, bias=math.log(KSCALE)), reads=G[0].tok, writes=G[0].tok)
            if samp:
                A("dve", lambda e: e.tensor_tensor(out=MT.ap[:4, 0:16], in0=v3(g_[5])[:, :, L - 1], in1=v3(g_[2])[:, :, L - 1], op=ALU.add),
                  reads=G[5].tok + G[2].tok, writes=MT.tok)
                p.dma("sp", ms_o, MT.ap[:4, 0:16], reads=MT.tok, is_out=True)
            else:
                if last_p:
                    p.dma("sp", mp_o, MALL[:4, NCH:NCH + 1], reads=MALLv.tok, is_out=True)
            pq = newps()
            for c in range(NCH):
                for qi, (gap_, gtk_) in enumerate(((g_[0], G[0].tok), (g_[4], G[4].tok), (g_[3], G[3].tok), (CC, CCv.tok))):
                    A("pe", lambda e, c=c, qi=qi, gap_=gap_: e.matmul(
                        psf(pq)[:L, (c * 4 + qi) * 4:(c * 4 + qi) * 4 + 4], lhsT=gap_[:, c * L:(c + 1) * L],
                        rhs=PAR[:4, PC_EYE4:PC_EYE4 + 4], start=True, stop=True),
                      reads=gtk_ + CT, writes=PT(pq))
            COLQv = scr(2560, 256, parts=64)
            COLQ = COLQv.ap
            A("act", lambda e: e.copy(out=COLQ[:L, :NCH * 16], in_=psf(pq)[:L, :NCH * 16]), reads=PT(pq), writes=COLQv.tok)
            DECD = scr(3584, 64, parts=4)
            A("dve", lambda e: e.tensor_tensor(
                out=DECD.ap[:4, :NCH * 4].rearrange("p (c h) -> p c h", h=4),
                in0=v3(g_[4])[:, :, L - 1:L].to_broadcast([4, NCH, 4]),
                in1=PAR[:4, PC_EYE4:PC_EYE4 + 4].unsqueeze(1).to_broadcast([4, NCH, 4]), op=ALU.mult),
              reads=G[4].tok + CT, writes=DECD.tok)
            pd = newps()
            A("pe", lambda e: e.matmul(psf(pd)[:, :NCH * 4], lhsT=PAR[:4, PC_ONES:PC_ONES + 128], rhs=DECD.ap[:4, :NCH * 4],
                                       start=True, stop=True),
              reads=DECD.tok + CT, writes=PT(pd))
            DECB = DECBv.ap
            A("act", lambda e: e.copy(out=DECB[:, :NCH * 4], in_=psf(pd)[:, :NCH * 4]), reads=PT(pd), writes=DECBv.tok)
            if not samp:
                A("dve", lambda e: e.tensor_copy(out=MALL[:4, 0:1], in_=MALL[:4, NCH:NCH + 1]), reads=MALLv.tok, writes=MALLv.tok)

            if samp:
                NROWS = Bview(3, 256, F32, off=1024)
                NALL = view(o_B[0] + 3584, 256, F32)
                NBFA = view(o_B[0] + 3840, 128, BF16)
                nrows3 = NROWS.ap.rearrange("p (a d) -> p a d", a=2)
                p.dma("sp", nrows3, sn.rearrange("(a p) d -> p a d", p=128), writes=NROWS.tok)
                pi = newps()
                for a in range(2):
                    A("pe", lambda e, a=a, pi=pi: e.transpose(out=psf(pi)[:, a * 128:(a + 1) * 128], in_=nrows3[:, a, :], identity=ident_f),
                      reads=NROWS.tok + CT, writes=PT(pi))
                A("dve", lambda e, pi=pi: e.tensor_copy(out=NALL.ap, in_=psf(pi)[:, :256]), reads=PT(pi), writes=NALL.tok)
                A("act", lambda e: e.copy(out=NBFA.ap, in_=NALL.ap), reads=NALL.tok, writes=NBFA.tok)

            LMv = scr(512, 4 * T, parts=64)
            for h_ in range(4):
                pl = newps()
                A("pe", lambda e, pl=pl, h_=h_: e.matmul(psf(pl)[:L, :T], lhsT=PAR[:4, PC_EH + h_ * 64:PC_EH + h_ * 64 + L], rhs=NEGA,
                                                        start=True, stop=True),
                  reads=NEGAv.tok + CT, writes=PT(pl))
                A("dve", lambda e, pl=pl, h_=h_: e.tensor_tensor(
                    out=LMv.ap[:L, h_ * T:(h_ + 1) * T].rearrange("p (c l) -> p c l", l=L),
                    in0=psf(pl)[:L, :T].rearrange("p (c l) -> p c l", l=L),
                    in1=PAR[:L, PC_NEG:PC_NEG + L].unsqueeze(1).to_broadcast([L, NCH, L]), op=ALU.add),
                  reads=PT(pl) + CT, writes=LMv.tok)

            print("mark recurrence", len(p.ops))
            s_D = [scr(3712, 64, parts=64), scr(3776, 64, parts=64)]
            s_Sd = [scr(3840, 32, BF16, parts=64), scr(3872, 32, BF16, parts=64)]
            s_kw = [scr(3904, 256, BF16, parts=64), scr(4160, 256, BF16, parts=64)]
            s_vb = [scr(4416, 256, BF16, parts=64), scr(4672, 256, BF16, parts=64)]
            s_t1 = scr(4928, 512, parts=64)
            s_hh = [scr(5440, 512, parts=64), scr(5952, 512, parts=64)]
            s_hn = scr(6464, 1024, BF16, parts=64)
            s_sm = [scr(7488, 32, parts=64), scr(7520, 32, parts=64)]
            NIT = NCH * 4
            PF = 3

            def st_of(idx):
                c, h = idx // 4, idx % 4
                if samp:
                    Cst = CSTv[idx % 4]
                    return (Cst, NALL.ap[:, idx * 4:idx * 4 + 4], NALL.tok, NBFA.ap[:, idx * 4:idx * 4 + 4], NBFA.tok)
                return (CSTv[h], NSTh[h].ap, NSTh[h].tok, NBFh[h].ap, NBFh[h].tok)

            def c3(Cst):
                return Cst.ap.rearrange("p (k e) -> p k e", k=4)

            def load_C(idx):
                Cst = CSTv[idx % 4]
                p.dma("sp", c3(Cst), sC[idx].rearrange("(k p) e -> p k e", p=128), writes=Cst.tok)

            def castC(idx):
                Cst = st_of(idx)[0]
                Cbf = CBFv[idx % 2]
                A("act", lambda e, Cbf=Cbf, Cst=Cst: e.copy(out=Cbf.ap, in_=Cst.ap), reads=Cst.tok, writes=Cbf.tok)

            def front(idx):
                c, h = idx // 4, idx % 4
                t0 = c * L
                k2 = idx % 2
                pa, pbk, pvv = k2, 2, 3
                D, Sd, kw, vb = s_D[k2], s_Sd[k2], s_kw[k2], s_vb[k2]
                qs = [qT[:, h * 4 + dk, t0:t0 + L] for dk in range(4)]
                ks_ = [kT[:, h * 4 + dk, t0:t0 + L] for dk in range(4)]
                vs = [vT[:, h * 4 + dk, t0:t0 + L] for dk in range(4)]
                for dk in range(4):
                    A("pe", lambda e, dk=dk, pa=pa, ks_=ks_, qs=qs: e.matmul(
                        psf(pa)[:L, 0:L], lhsT=ks_[dk], rhs=qs[dk], start=(dk == 0), stop=(dk == 3)),
                      reads=B4.tok + B5.tok, writes=PT(pa))
                for dk in range(4):
                    A("pe", lambda e, dk=dk, pbk=pbk, ks_=ks_: e.matmul(psf(pbk)[:L, dk * 128:(dk + 1) * 128], lhsT=ks_[dk], rhs=ident_bf, start=True, stop=True),
                      reads=B5.tok + CBT, writes=PT(pbk))
                for dk in range(4):
                    A("pe", lambda e, dk=dk, pvv=pvv, vs=vs: e.matmul(psf(pvv)[:L, dk * 128:(dk + 1) * 128], lhsT=vs[dk], rhs=ident_bf, start=True, stop=True),
                      reads=B3.tok + CBT, writes=PT(pvv))
                ccol = (c * 4 + 3) * 4 + h
                A("act", lambda e, D=D, h=h, t0=t0, ccol=ccol: e.activation(
                    out=D.ap[:L, :L], in_=LMv.ap[:L, h * T + t0:h * T + t0 + L], func=AF.Exp, bias=COLQ[:L, ccol:ccol + 1]),
                  reads=LMv.tok + COLQv.tok, writes=D.tok)
                A("dve", lambda e, pa=pa, D=D, Sd=Sd: e.scalar_tensor_tensor(
                    out=Sd.ap[:L, :L], in0=psf(pa)[:L, 0:L], scalar=KSCALE, in1=D.ap[:L, :L], op0=ALU.mult, op1=ALU.mult),
                  reads=PT(pa) + D.tok, writes=Sd.tok)
                cq = (c * 4) * 4 + h
                A("act", lambda e, pbk=pbk, kw=kw, cq=cq: e.activation(
                    out=kw.ap[:L, :512], in_=psf(pbk)[:L, 0:512], func=AF.Copy, scale=COLQ[:L, cq:cq + 1]),
                  reads=PT(pbk) + COLQv.tok, writes=kw.tok)
                A("act", lambda e, pvv=pvv, vb=vb: e.copy(out=vb.ap[:L, :512], in_=psf(pvv)[:L, 0:512]), reads=PT(pvv), writes=vb.tok)

            def mid(idx, part):
                c, h = idx // 4, idx % 4
                t0 = c * L
                k2 = idx % 2
                Cst, nst, nstt, nbf, nbft = st_of(idx)
                Cbf = CBFv[k2]
                Cb3 = Cbf.ap.rearrange("p (k e) -> p k e", k=4)
                pa, phq, phs = k2, 4, 5
                Sd, vb, hh, sm_ = s_Sd[k2], s_vb[k2], s_hh[k2], s_sm[k2]
                qs = [qT[:, h * 4 + dk, t0:t0 + L] for dk in range(4)]
                s_ = sm_.ap
                if part == "b":
                    A("dve", lambda e, phs=phs, s_=s_, hh=hh: e.scalar_tensor_tensor(
                        out=hh.ap[:L, :512], in0=psf(phs)[:L, :512], scalar=s_[:L, 4:5], in1=s_t1.ap[:L, :512], op0=ALU.mult, op1=ALU.add),
                      reads=PT(phs) + sm_.tok + s_t1.tok, writes=hh.tok)
                    A("dve", lambda e, s_=s_, hh=hh: e.bn_stats(out=s_[:L, 8:14], in_=hh.ap[:L, :512]), reads=hh.tok, writes=sm_.tok)
                    A("dve", lambda e, s_=s_: e.bn_aggr(out=s_[:L, 14:16], in_=s_[:L, 8:14]), reads=sm_.tok, writes=sm_.tok)
                    A("act", lambda e, s_=s_: e.activation(out=s_[:L, 16:17], in_=s_[:L, 15:16], func=AF.Ln, bias=LN_EPS), reads=sm_.tok, writes=sm_.tok)
                    A("act", lambda e, s_=s_: e.activation(out=s_[:L, 17:18], in_=s_[:L, 16:17], func=AF.Exp, scale=-0.5), reads=sm_.tok, writes=sm_.tok)
                    return
                if part == "c":
                    A("dve", lambda e, s_=s_, hh=hh, h=h: e.tensor_scalar(
                        out=s_hn.ap[:L, h * 512:(h + 1) * 512], in0=hh.ap[:L, :512], scalar1=s_[:L, 14:15], scalar2=s_[:L, 17:18],
                        op0=ALU.subtract, op1=ALU.mult),
                      reads=hh.tok + sm_.tok, writes=s_hn.tok)
                    return
                for dk in range(4):
                    A("pe", lambda e, dk=dk, phq=phq, qs=qs, Cb3=Cb3: e.matmul(
                        psf(phq)[:L, :512], lhsT=qs[dk], rhs=Cb3[:, dk, :], start=(dk == 0), stop=(dk == 3)),
                      reads=B4.tok + Cbf.tok, writes=PT(phq))
                for dk in range(4):
                    A("pe", lambda e, dk=dk, pa=pa, qs=qs, nbf=nbf: e.matmul(
                        psf(pa)[:L, 128:129], lhsT=qs[dk], rhs=nbf[:, dk:dk + 1], start=(dk == 0), stop=(dk == 3)),
                      reads=B4.tok + nbft, writes=PT(pa))
                A("pe", lambda e, phs=phs, Sd=Sd, vb=vb: e.matmul(psf(phs)[:L, :512], lhsT=Sd.ap[:L, :L], rhs=vb.ap[:L, :512], start=True, stop=True),
                  reads=Sd.tok + vb.tok, writes=PT(phs))
                A("pe", lambda e, pa=pa, Sd=Sd: e.matmul(psf(pa)[:L, 129:130], lhsT=Sd.ap[:L, :L], rhs=ones_bf[:L, 0:1], start=True, stop=True),
                  reads=Sd.tok + CBT, writes=PT(pa))
                s_ = sm_.ap
                cq = (c * 4) * 4 + h
                wi = COLQ[:L, cq + 4:cq + 5]
                em = COLQ[:L, cq + 8:cq + 9]
                A("act", lambda e, pa=pa, s_=s_: e.copy(out=s_[:L, 0:2], in_=psf(pa)[:L, 128:130]), reads=PT(pa), writes=sm_.tok)
                A("dve", lambda e, s_=s_, wi=wi: e.scalar_tensor_tensor(out=s_[:L, 2:3], in0=s_[:L, 0:1], scalar=wi, in1=s_[:L, 1:2],
                                                                      op0=ALU.mult, op1=ALU.add),
                  reads=sm_.tok + COLQv.tok, writes=sm_.tok)
                A("dve", lambda e, s_=s_: e.scalar_tensor_tensor(out=s_[:L, 3:4], in0=s_[:L, 2:3], scalar=-1.0, in1=s_[:L, 2:3],
                                                                 op0=ALU.mult, op1=ALU.max),
                  reads=sm_.tok, writes=sm_.tok)
                A("dve", lambda e, s_=s_, em=em: e.tensor_tensor(out=s_[:L, 3:4], in0=s_[:L, 3:4], in1=em, op=ALU.max),
                  reads=sm_.tok + COLQv.tok, writes=sm_.tok)
                A("dve", lambda e, s_=s_: e.reciprocal(out=s_[:L, 4:5], in_=s_[:L, 3:4]), reads=sm_.tok, writes=sm_.tok)
                A("dve", lambda e, s_=s_, wi=wi: e.tensor_tensor(out=s_[:L, 5:6], in0=s_[:L, 4:5], in1=wi, op=ALU.mult),
                  reads=sm_.tok + COLQv.tok, writes=sm_.tok)
                A("act", lambda e, phq=phq, s_=s_: e.activation(out=s_t1.ap[:L, :512], in_=psf(phq)[:L, :512], func=AF.Copy, scale=s_[:L, 5:6]),
                  reads=PT(phq) + sm_.tok, writes=s_t1.tok)

            def back(idx, part):
                c, h = idx // 4, idx % 4
                k2 = idx % 2
                Cst, nst, nstt, nbf, nbft = st_of(idx)
                C3 = c3(Cst)
                kw, vb = s_kw[k2], s_vb[k2]
                pa = (idx + 1) % 2
                dec = DECB[:, idx:idx + 1]
                for dk in ((0, 1) if part == "a" else (2, 3)):
                    pk = 6 + dk % 2
                    A("pe", lambda e, dk=dk, pk=pk, kw=kw, vb=vb: e.matmul(
                        psf(pk)[:, :512], lhsT=kw.ap[:L, dk * 128:(dk + 1) * 128], rhs=vb.ap[:L, :512], start=True, stop=True),
                      reads=kw.tok + vb.tok, writes=PT(pk))
                    A("dve", lambda e, dk=dk, pk=pk, C3=C3, dec=dec: e.scalar_tensor_tensor(
                        out=C3[:, dk, :], in0=C3[:, dk, :], scalar=dec, in1=psf(pk)[:, :512], op0=ALU.mult, op1=ALU.add),
                      reads=PT(pk) + Cst.tok + DECBv.tok, writes=Cst.tok)
                if part == "a":
                    return
                for dk in range(4):
                    A("pe", lambda e, dk=dk, pa=pa, kw=kw: e.matmul(
                        psf(pa)[:, 132 + dk:133 + dk], lhsT=kw.ap[:L, dk * 128:(dk + 1) * 128], rhs=ones_bf[:L, 0:1], start=True, stop=True),
                      reads=kw.tok + CBT, writes=PT(pa))
                A("dve", lambda e, pa=pa, nst=nst, dec=dec: e.scalar_tensor_tensor(
                    out=nst, in0=nst, scalar=dec, in1=psf(pa)[:, 132:136], op0=ALU.mult, op1=ALU.add),
                  reads=PT(pa) + nstt + DECBv.tok, writes=nstt)
                if samp:
                    p.dma("pool", Cs[idx].rearrange("(k p) e -> p k e", p=128), C3, reads=Cst.tok, is_out=True)
                else:
                    A("pool", lambda e, nbf=nbf, nst=nst: e.tensor_copy(out=nbf, in_=nst), reads=nstt, writes=nbft)
                    if last_p and c == NCH - 1:
                        p.dma("sp", Cp[h].rearrange("(k p) e -> p k e", p=128), C3, reads=Cst.tok, is_out=True)

            def outstage(c):
                t0 = c * L
                po = 2
                if samp:
                    for fc in range(16):
                        A("pe", lambda e, fc=fc, po=po: e.matmul(psf(po)[:, fc * L:(fc + 1) * L], lhsT=s_hn.ap[:L, fc * 128:(fc + 1) * 128],
                                                                 rhs=ident_bf[:L, :L], start=True, stop=True),
                          reads=s_hn.tok + CBT, writes=PT(po))
                    A("act", lambda e, po=po, t0=t0: e.copy(out=ogT[:, :, t0:t0 + L], in_=psf(po)[:, :16 * L].rearrange("p (c l) -> p c l", l=L)),
                      reads=PT(po), writes=B1.tok)
                else:
                    for fc in range(16):
                        A("pe", lambda e, fc=fc, po=po: e.transpose(out=psh(po)[:, fc * L:(fc + 1) * L], in_=s_hn.ap[:L, fc * 128:(fc + 1) * 128],
                                                                    identity=ident_bf[:L, :L]),
                          reads=s_hn.tok + CBT, writes=PT(po))
                    A("act", lambda e, po=po, t0=t0: e.copy(out=ogT[:, :, t0:t0 + L], in_=psh(po)[:, :16 * L].rearrange("p (c l) -> p c l", l=L)),
                      reads=PT(po), writes=B1.tok)

            if samp:
                for i in range(min(PF, NIT)):
                    load_C(i)
            castC(0)
            if NIT > 1 and not samp:
                castC(1)
            front(0)
            for idx in range(NIT):
                if samp and idx + 1 < NIT:
                    castC(idx + 1)
                mid(idx, "a")
                if (not samp) and idx + 2 < NIT:
                    castC(idx + 2)
                if idx > 0 and idx % 4 == 0:
                    outstage(idx // 4 - 1)
                if idx > 0:
                    back(idx - 1, "a")
                mid(idx, "b")
                if idx > 0:
                    back(idx - 1, "b")
                if idx + 1 < NIT:
                    front(idx + 1)
                mid(idx, "c")
                if samp and idx + PF < NIT:
                    load_C(idx + PF)
            outstage(NCH - 1)
            back(NIT - 1, "a")
            back(NIT - 1, "b")

            if samp:
                pi = newps()
                for a in range(2):
                    A("pe", lambda e, a=a, pi=pi: e.transpose(out=psf(pi)[:, a * 128:(a + 1) * 128], in_=NALL.ap[:, a * 128:(a + 1) * 128], identity=ident_f),
                      reads=NALL.tok + CT, writes=PT(pi))
                A("act", lambda e, pi=pi: e.copy(out=NROWS.ap, in_=psf(pi)[:, :256]), reads=PT(pi), writes=NROWS.tok)
                p.dma("sp", ns_o.rearrange("(a p) d -> p a d", p=128), nrows3, reads=NROWS.tok, is_out=True)
            elif last_p:
                pi = newps()
                for h in range(4):
                    A("dve", lambda e, h=h: e.tensor_copy(out=NSTC.ap[:, h * 4:h * 4 + 4], in_=NSTh[h].ap), reads=NSTh[h].tok, writes=NSTC.tok)
                A("pe", lambda e, pi=pi: e.matmul(psf(pi)[:16, 0:128], lhsT=NSTC.ap, rhs=ident_f, start=True, stop=True), reads=NSTC.tok + CT, writes=PT(pi))
                NO = scr(0, 128, parts=16)
                A("act", lambda e, pi=pi: e.copy(out=NO.ap[:16, :], in_=psf(pi)[:16, 0:128]), reads=PT(pi), writes=NO.tok)
                p.dma("sp", np_o, NO.ap[:16, :], reads=NO.tok, is_out=True)

            print("mark z", len(p.ops))
            for it in range(8):
                w = wget()
                wv = w.ap.rearrange("p (o k n) -> p o k n", o=2, k=8)
                for o2 in range(2):
                    oc = it * 2 + o2
                    pi = newps()
                    for k in range(8):
                        A("pe", lambda e, wv=wv, o2=o2, k=k, pi=pi: e.matmul(
                            psf(pi)[:, :T], lhsT=wv[:, o2, k, :], rhs=hT[:, k, :], start=(k == 0), stop=(k == 7)),
                          reads=w.tok + HT.tok, writes=PT(pi))
                    A("act", lambda e, oc=oc, pi=pi: e.activation(out=szT[:, oc, :], in_=psf(pi)[:, :T], func=AF.Silu),
                      reads=PT(pi), writes=ctok(3, oc))
            s_o1 = [scr(0, 512), scr(512, 512)]
            for fc in range(16):
                o1 = s_o1[fc % 2]
                A("dve", lambda e, fc=fc, o1=o1: e.tensor_scalar(out=o1.ap[:, :T], in0=xcT[:, fc, :], scalar1=PAR[:, PC_SKIP + fc:PC_SKIP + fc + 1],
                                                                scalar2=None, op0=ALU.mult),
                  reads=ctok(1, fc) + CT, writes=o1.tok)
                A("dve", lambda e, fc=fc, o1=o1: e.scalar_tensor_tensor(out=o1.ap[:, :T], in0=ogT[:, fc, :], scalar=PAR[:, PC_ONORM + fc:PC_ONORM + fc + 1],
                                                                       in1=o1.ap[:, :T], op0=ALU.mult, op1=ALU.add),
                  reads=ctok(0, fc) + CT + o1.tok, writes=o1.tok)
                A("dve", lambda e, fc=fc, o1=o1: e.tensor_tensor(out=ogT[:, fc, :], in0=o1.ap[:, :T], in1=szT[:, fc, :], op=ALU.mult),
                  reads=o1.tok + ctok(3, fc), writes=ctok(0, fc))
            outproj(ogT, 0)

            print("mark final", len(p.ops))
            load_nw(2)
            for b in range(NB):
                k = b % 2
                rms_stats(b, k)
                ost = s_ost[k]
                A("dve", lambda e, b=b, k=k, ost=ost: e.scalar_tensor_tensor(
                    out=ost.ap, in0=Xt[:, b, :], scalar=s_ss[k].ap[:, 2:3], in1=s_nw.ap, op0=ALU.mult, op1=ALU.mult),
                  reads=X.tok + s_ss[k].tok + s_nw.tok, writes=ost.tok)
                if samp:
                    p.dma("sp", ys, ost.ap, reads=ost.tok, is_out=True)
                else:
                    r0 = ti * 512 + b * 128
                    p.dma("sp", yp[r0:r0 + 128, :], ost.ap, reads=ost.tok, is_out=True)

        for ti in range(n_ptiles):
            run_tile("p", ti)
        if do_sample:
            run_tile("s", 0)
        print("ops", len(p.ops), "arena scratch words", SCR_WORDS)
        p.emit(sems, dsems)
    return nc


def _host_prep(inp):
    f = np.float32
    a_w_in = np.asarray(inp["a_w_in"], f)[0]
    a_w_out = np.asarray(inp["a_w_out"], f)[0]
    b_w_in = np.asarray(inp["b_w_in"], f)[0]
    b_w_out = np.asarray(inp["b_w_out"], f)[0]
    items = []
    A5 = a_w_in.reshape(8, 128, 4, 16, 128)
    t = A5.transpose(3, 2, 1, 0, 4)
    for j in range(16):
        for half in range(2):
            blk = t[j, half * 2:half * 2 + 2]
            items.append(blk.transpose(1, 0, 2, 3).reshape(128, ITEM))
    def outw(w):
        W = w.reshape(8, 2, 128, 1024)
        return [W[i].transpose(1, 0, 2).reshape(128, ITEM) for i in range(8)]
    items += outw(a_w_out)
    def inw(wcols):
        W = wcols.reshape(8, 128, 8, 2, 128)
        return [W[:, :, i].transpose(1, 2, 0, 3).reshape(128, ITEM) for i in range(8)]
    items += inw(b_w_in[:, :2048])
    def sqw(w):
        W = w.reshape(16, 128, 16, 128)
        return [W[:, :, oc].transpose(1, 0, 2).reshape(128, ITEM) for oc in range(16)]
    items += sqw(np.asarray(inp["b_w_v"], f)[0])
    items += sqw(np.asarray(inp["b_w_q"], f)[0])
    items += sqw(np.asarray(inp["b_w_k"], f)[0])
    items += inw(b_w_in[:, 2048:])
    items += outw(b_w_out)
    assert len(items) == NITEMS
    wst = np.ascontiguousarray(np.stack(items, 0))

    par = np.zeros((128, NPAR), f)
    def fmaj(v):
        return np.asarray(v, f).reshape(16, 128).T
    ca = np.asarray(inp["a_conv_w"], f)[0]
    cb = np.asarray(inp["b_conv_w"], f)[0]
    for d in range(3):
        par[:, PC_CONVA + d * 16:PC_CONVA + (d + 1) * 16] = fmaj(ca[d])
    for d in range(4):
        par[:, PC_CONVB + d * 16:PC_CONVB + (d + 1) * 16] = fmaj(cb[d])
    par[:, PC_CB:PC_CB + 16] = fmaj(np.asarray(inp["b_conv_b"], f)[0])
    par[:, PC_SKIP:PC_SKIP + 16] = fmaj(np.asarray(inp["b_skip"], f)[0])
    par[:, PC_ONORM:PC_ONORM + 16] = fmaj(np.asarray(inp["b_onorm_w"], f)[0])
    par[:, PC_IDENT:PC_IDENT + 128] = np.eye(128, dtype=f)
    jj, ii = np.meshgrid(np.arange(64), np.arange(64), indexing="ij")
    par[:64, PC_NEG:PC_NEG + 64] = np.where(ii >= jj, 0.0, -30000.0).astype(f)
    for h in range(4):
        par[h, PC_EH + h * 64:PC_EH + (h + 1) * 64] = 1.0
    par[:4, PC_EYE4:PC_EYE4 + 4] = np.eye(4, dtype=f)
    par[:, PC_ONES:PC_ONES + 128] = 1.0
    for h in range(4):
        par[4 + h, PC_SEL + h] = 1.0
    par[:8, PC_BIF] = np.asarray(inp["b_b_if"], f)[0]
    tt = np.arange(512)
    par[:4, PC_RMASK:PC_RMASK + 512] = (tt % 64 != 0).astype(f)[None]
    par[:4, PC_AMASK:PC_AMASK + 512] = np.where(tt % 64 == 0, -1e30, 0.0).astype(f)[None]
    ts = np.arange(128)
    par[:4, PC_RMASK_S:PC_RMASK_S + 128] = (ts % 8 != 0).astype(f)[None]
    par[:4, PC_AMASK_S:PC_AMASK_S + 128] = np.where(ts % 8 == 0, -1e30, 0.0).astype(f)[None]

    cbf = np.zeros((128, NCB), f)
    cbf[:, CB_IDENT:CB_IDENT + 128] = np.eye(128, dtype=f)
    cbf[:, CB_ONES:CB_ONES + 8] = 1.0
    wif = np.asarray(inp["b_w_if"], f)[0]
    cbf[:, CB_WIF:CB_WIF + 384] = wif.reshape(48, 128, 8).transpose(1, 0, 2).reshape(128, 384)

    nw = np.asarray(inp["norm_w"], f)
    bc = np.stack([np.broadcast_to(nw[0], (128, DM)), np.broadcast_to(nw[1], (128, DM)),
                   np.broadcast_to(np.asarray(inp["final_norm_w"], f), (128, DM))], 0)
    bc = np.ascontiguousarray(bc)
    return wst, par, cbf, bc


_CACHE = {}


def kernel(**inp):
    f = np.float32
    wst, par, cbf, bc = _host_prep(inp)
    if "nc" not in _CACHE:
        _CACHE["nc"] = build_program()
    nc = _CACHE["nc"]
    xp = np.asarray(inp["x_prompt"], f)
    xs = np.asarray(inp["x_sample"], f)
    sca = np.asarray(inp["state_conv_a"], f)[0]
    scb = np.asarray(inp["state_conv_b"], f)[0]
    sC = np.asarray(inp["state_C"], f)[0]
    sn = np.asarray(inp["state_n"], f)[0]
    sm = np.asarray(inp["state_m"], f)[0]
    in_maps = []
    for c in range(NCORES):
        s0, s1 = 16 * c, 16 * c + 16
        in_maps.append({
            "xp": np.ascontiguousarray(xp[c]),
            "xs": np.ascontiguousarray(xs[s0:s1].reshape(128, DM)),
            "sca": np.ascontiguousarray(sca[s0:s1].reshape(32, AW)),
            "scb": np.ascontiguousarray(scb[s0:s1].reshape(48, AW)),
            "sC": np.ascontiguousarray(sC[s0:s1].reshape(64, 512, 512)),
            "sn": np.ascontiguousarray(sn[s0:s1].reshape(256, 128)),
            "sm": np.ascontiguousarray(sm[s0:s1].T),
            "wst": wst, "par": par, "cbf": cbf, "bc": bc,
        })
    res = run_bass_kernel_spmd(nc, in_maps, core_ids=list(range(NCORES)))
    R = res.results
    def cat(name, shp):
        return np.stack([np.asarray(R[c][name], f).reshape(shp) for c in range(NCORES)], 0)
    y_p = cat("yp", (2048, DM))
    y_s = cat("ys", (16, 8, DM)).reshape(128, 8, DM)
    ca_p = cat("cap", (2, AW))[None]
    ca_s = cat("cas", (16, 2, AW)).reshape(128, 2, AW)[None]
    cb_p = cat("cbp", (3, AW))[None]
    cb_s = cat("cbs", (16, 3, AW)).reshape(128, 3, AW)[None]
    C_p = cat("Cp", (4, 512, 512))[None]
    C_s = cat("Cs", (16, 4, 512, 512)).reshape(128, 4, 512, 512)[None]
    n_p = cat("np", (4, 512))[None]
    n_s = cat("ns", (16, 4, 512)).reshape(128, 4, 512)[None]
    m_p = cat("mp", (4,))[None]
    m_s = np.stack([np.asarray(R[c]["ms"], f).reshape(4, 16).T for c in range(NCORES)], 0).reshape(128, 4)[None]
    return (y_p, y_s, ca_p, ca_s, cb_p, cb_s, C_p, C_s, n_p, n_s, m_p, m_s)
```

```python
import math
import contextlib
import numpy as np
import concourse.bass as bass
import concourse.mybir as mybir
from concourse.bass_utils import run_bass_kernel_spmd

F32 = mybir.dt.float32
BF16 = mybir.dt.bfloat16
AF = mybir.ActivationFunctionType
ALU = mybir.AluOpType

NCORES = 8
DM = 1024
AW = 2048
NH = 4
DK = 512
RMS_EPS = 1e-6
LN_EPS = 1e-5
KSCALE = DK ** -0.5
NITEMS = 112
NSLOTS = 4
ITEM = 2048

PC_CONVA, PC_CONVB, PC_CB, PC_SKIP, PC_ONORM = 0, 48, 112, 128, 144
PC_IDENT = 160
PC_NEG = 288
PC_EH = 352
PC_EYE4 = 608
PC_ONES = 612
PC_SEL = 740
PC_BIF = 744
PC_RMASK = 745
PC_AMASK = 1257
PC_RMASK_S = 1769
PC_AMASK_S = 1897
NPAR = 2025
CB_IDENT, CB_ONES, CB_WIF = 0, 128, 136
NCB = 520


class Op:
    __slots__ = ("eng", "fn", "deps", "needs_inc", "val", "is_dma", "sem", "dma_val")

    def __init__(self, eng, fn, is_dma=False):
        self.eng = eng
        self.fn = fn
        self.deps = []
        self.needs_inc = False
        self.val = None
        self.is_dma = is_dma
        self.sem = None
        self.dma_val = None


class Prog:
    ENGS = ("pe", "act", "dve", "pool", "sp")

    def __init__(self, nc, n_dma_sems=32):
        self.nc = nc
        self.eng = {"pe": nc.tensor, "act": nc.scalar, "dve": nc.vector, "pool": nc.gpsimd, "sp": nc.sync}
        self.ops = []
        self.last_w = {}
        self.readers = {}
        self.n_dma_sems = n_dma_sems
        self.dma_rr = 0
        self.dma_rr_sw = 0
        self.dma_sem_last = [None] * n_dma_sems
        self.dma_sem_count = [0] * n_dma_sems
        self.out_dmas = []

    def _add_dep(self, op, d):
        if d is None or d is op:
            return
        if d.eng == op.eng and op.eng == "pe" and not d.is_dma and not op.is_dma:
            return
        op.deps.append(d)
        if not d.is_dma:
            d.needs_inc = True

    def op(self, eng, fn, reads=(), writes=(), is_dma=False, is_out=False):
        o = Op(eng, fn, is_dma)
        for t in reads:
            self._add_dep(o, self.last_w.get(t))
        for t in writes:
            self._add_dep(o, self.last_w.get(t))
            for r in self.readers.get(t, ()):
                if r.eng == eng and not r.is_dma and not is_dma:
                    continue
                self._add_dep(o, r)
        for t in reads:
            self.readers.setdefault(t, []).append(o)
        for t in writes:
            self.last_w[t] = o
            self.readers[t] = []
        if is_dma:
            if eng == "pool":
                s = self.dma_rr_sw
                self.dma_rr_sw = (self.dma_rr_sw + 1) % 8
            else:
                s = 8 + self.dma_rr
                self.dma_rr = (self.dma_rr + 1) % (self.n_dma_sems - 8)
            prev = self.dma_sem_last[s]
            if prev is not None:
                o.deps.append(prev)
            self.dma_sem_count[s] += 16
            o.sem = s
            o.dma_val = self.dma_sem_count[s]
            self.dma_sem_last[s] = o
            if is_out:
                self.out_dmas.append(o)
        self.ops.append(o)
        return o

    def dma(self, eng, out, in_, reads=(), writes=(), is_out=False):
        return self.op(eng, lambda e: e.dma_start(out=out, in_=in_), reads, writes, is_dma=True, is_out=is_out)

    def emit(self, sems, dma_sems):
        cnt = {e: 0 for e in self.ENGS}
        for o in self.ops:
            if o.needs_inc and not o.is_dma:
                cnt[o.eng] += 1
                o.val = cnt[o.eng]
        waited = {e: {} for e in self.ENGS}
        import os
        maxops = int(os.environ.get("MK_MAXOPS", "0"))
        if maxops:
            self.ops = self.ops[:maxops]
            self.out_dmas = [o for o in self.out_dmas if o in set(self.ops)]
        for o in self.ops:
            e = self.eng[o.eng]
            w = waited[o.eng]
            need = {}
            for d in o.deps:
                if d.is_dma:
                    key, v = ("d", d.sem), d.dma_val
                else:
                    key, v = ("e", d.eng), d.val
                if need.get(key, 0) < v:
                    need[key] = v
            for key, v in need.items():
                if w.get(key, 0) >= v:
                    continue
                w[key] = v
                e.wait_ge(dma_sems[key[1]] if key[0] == "d" else sems[key[1]], v)
            inst = o.fn(e)
            if o.is_dma:
                inst.then_inc(dma_sems[o.sem], 16)
            elif o.needs_inc:
                inst.then_inc(sems[o.eng], 1)
        e = self.eng["sp"]
        fin = {}
        for o in self.out_dmas:
            fin[o.sem] = max(fin.get(o.sem, 0), o.dma_val)
        for s, v in fin.items():
            e.wait_ge(dma_sems[s], v)


def build_program(n_ptiles=4, do_sample=True, dbg=False):
    nc = bass.Bass("TRN2", target_bir_lowering=False)

    def din(name, shape):
        return nc.dram_tensor(name, shape, F32, kind="ExternalInput").ap()

    def dout(name, shape):
        return nc.dram_tensor(name, shape, F32, kind="ExternalOutput").ap()

    xp = din("xp", [2048, DM])
    xs = din("xs", [128, DM])
    sca = din("sca", [32, AW])
    scb = din("scb", [48, AW])
    sC = din("sC", [64, 512, 512])
    sn = din("sn", [256, 128])
    sm = din("sm", [4, 16])
    wst = din("wst", [NITEMS, 128, ITEM])
    par_d = din("par", [128, NPAR])
    cbf_d = din("cbf", [128, NCB])
    bc_d = din("bc", [3, 128, DM])

    yp = dout("yp", [2048, DM])
    ys = dout("ys", [128, DM])
    cap = dout("cap", [2, AW])
    cas = dout("cas", [32, AW])
    cbp = dout("cbp", [3, AW])
    cbs = dout("cbs", [48, AW])
    Cp = dout("Cp", [4, 512, 512])
    Cs = dout("Cs", [64, 512, 512])
    np_o = dout("np", [16, 128])
    ns_o = dout("ns", [256, 128])
    mp_o = dout("mp", [4, 1])
    ms_o = dout("ms", [4, 16])

    es = contextlib.ExitStack()
    with es:
        AR_WORDS = 52600
        arena = es.enter_context(nc.sbuf_tensor("arena", [128, AR_WORDS], F32))
        psb = [es.enter_context(nc.psum_tensor(f"psb{i}", [128, 512], F32)) for i in range(8)]
        sems = {e: es.enter_context(nc.semaphore(f"sem_{e}")) for e in Prog.ENGS}
        dsems = [es.enter_context(nc.semaphore(f"dsem{i}")) for i in range(32)]
        p = Prog(nc)
        A = p.op

        cur = [0]
        PAGE = 32

        class V:
            __slots__ = ("ap", "tok")

            def __init__(self, ap, tok):
                self.ap = ap
                self.tok = tok

        def alloc(words):
            words = (words + PAGE - 1) // PAGE * PAGE
            o = cur[0]
            cur[0] += words
            assert cur[0] <= AR_WORDS, f"arena overflow {cur[0]}"
            return o

        def view(off, words, dtype=F32, parts=128):
            ap = arena[:parts, off:off + words]
            if dtype != F32:
                ap = ap.bitcast(dtype)
            toks = [("a", pg) for pg in range(off // PAGE, (off + words - 1) // PAGE + 1)]
            return V(ap, toks)

        o_w = alloc(NSLOTS * 1024)
        wslot = [view(o_w + i * 1024, 1024, BF16) for i in range(NSLOTS)]
        o_x = alloc(4096)
        Xv = view(o_x, 4096)
        o_ht = alloc(2048)
        HTv = view(o_ht, 2048, BF16)
        o_B = [alloc(4096) for _ in range(5)]
        o_cst = alloc(4 * 2048)
        CSTv = [view(o_cst + i * 2048, 2048) for i in range(4)]
        o_cbf = alloc(2 * 1024)
        CBFv = [view(o_cbf + i * 1024, 1024, BF16) for i in range(2)]
        o_par = alloc(NPAR)
        PARv = view(o_par, NPAR)
        PAR = PARv.ap
        o_cb = alloc(NCB // 2)
        CBv = view(o_cb, NCB // 2, BF16)
        CB = CBv.ap
        o_hist = alloc(416)
        HISTAv = view(o_hist, 32)
        HISTBv = view(o_hist + 32, 48)
        MALLv = view(o_hist + 96, 24, parts=4)
        NSTh = [view(o_hist + 128 + h * 32, 4) for h in range(4)]
        NBFh = [view(o_hist + 256 + h * 32, 2, BF16) for h in range(4)]
        NSTC = view(o_hist + 384, 16)
        o_g = alloc(512 * 2 + 192 + 64)
        CCv = view(o_g, 512, parts=4)
        NEGAv = view(o_g + 512, 512, parts=4)
        COLQv = view(o_g + 1024, 192, parts=64)
        DECBv = view(o_g + 1216, 64)
        o_scr = cur[0]
        SCR_WORDS = AR_WORDS - o_scr

        def scr(off, words, dtype=F32, parts=128):
            assert off + words <= SCR_WORDS, f"scratch overflow {off + words} > {SCR_WORDS}"
            return view(o_scr + off, words, dtype, parts)

        def dump(name, v, dtype):
            if not dbg:
                return
            shp = list(v.ap.shape)
            d = nc.dram_tensor(name, shp, dtype, kind="ExternalOutput").ap()
            p.dma("sp", d, v.ap, reads=v.tok, is_out=True)

        psrr = [0]

        def newps():
            i = psrr[0]
            psrr[0] = (i + 1) % 8
            return i

        def PT(i):
            return [("ps", i)]

        def psf(i):
            return psb[i][:]

        def psh(i):
            return psb[i][:].bitcast(BF16)

        wg = [0]
        wissued = [0]
        total_items = NITEMS * (n_ptiles + (1 if do_sample else 0))

        wbf = nc.dram_tensor("wbf_cache", [NITEMS, 128, ITEM], BF16, kind=("ExternalOutput" if dbg else "Internal")).ap()
        use_cache = total_items > NITEMS

        def w_issue_upto(g):
            while wissued[0] <= g and wissued[0] < total_items:
                gi = wissued[0]
                s = gi % NSLOTS
                if gi < NITEMS or not use_cache:
                    p.dma("pool", wslot[s].ap, wst[gi % NITEMS], writes=wslot[s].tok)
                else:
                    wq = "sp" if gi >= NITEMS * n_ptiles else "pool"
                    p.dma(wq, wslot[s].ap, wbf[gi % NITEMS], reads=[("wd", gi % NITEMS)], writes=wslot[s].tok)
                wissued[0] += 1

        def wget():
            g = wg[0]
            wg[0] += 1
            w_issue_upto(g + NSLOTS - 1)
            if use_cache and g < NITEMS:
                p.dma("sp", wbf[g], wslot[g % NSLOTS].ap, reads=wslot[g % NSLOTS].tok, writes=[("wd", g)])
            return wslot[g % NSLOTS]

        p.dma("sp", PAR, par_d, writes=PARv.tok)
        p.dma("pool", CB, cbf_d, writes=CBv.tok)
        ident_bf = CB[:, CB_IDENT:CB_IDENT + 128]
        ones_bf = CB[:, CB_ONES:CB_ONES + 8]
        wif_bf = CB[:, CB_WIF:CB_WIF + 384].rearrange("p (k g) -> p k g", g=8)
        ident_f = PAR[:, PC_IDENT:PC_IDENT + 128]
        CT = PARv.tok
        CBT = CBv.tok

        for h in range(4):
            A("pool", lambda e, h=h: e.memset(CSTv[h].ap, 0.0), writes=CSTv[h].tok)
        hist_all = view(o_hist, 416)
        A("pool", lambda e: e.memset(hist_all.ap, 0.0), writes=hist_all.tok)

        def run_tile(kind, ti):
            samp = kind == "s"
            T = 128 if samp else 512
            NB = T // 128
            L = 8 if samp else 64
            NCH = T // L
            HBA, HBB = 2, 3
            last_p = (not samp) and ti == n_ptiles - 1

            def Bview(i, words=None, dtype=BF16, off=0):
                return view(o_B[i] + off, words if words is not None else 16 * T // 2, dtype)

            def ctok(i, fc):
                cw = T // 2
                o0 = o_B[i] + fc * cw
                return [("a", pg) for pg in range(o0 // PAGE, (o0 + cw - 1) // PAGE + 1)]

            def fm(v):
                return v.ap.rearrange("p (c t) -> p c t", t=T)

            B1, B2, B3, B4, B5 = [Bview(i) for i in range(5)]
            HT = view(o_ht, 8 * T // 2, BF16)
            hT = HT.ap.rearrange("p (c t) -> p c t", t=T)
            X = view(o_x, NB * 1024)
            Xt = X.ap.rearrange("p (b f) -> p b f", f=DM)

            s_junk = scr(0, 512, BF16)
            s_hb = [scr(512, 512, BF16), scr(1024, 512, BF16)]
            s_ss = [scr(1536, 4), scr(1568, 4), scr(6784, 4), scr(6816, 4)]
            s_nw = scr(1600, 1024)
            s_xe = [scr(2624, 520), scr(3168, 520)]
            s_xa = [scr(3712, 512), scr(4224, 512)]
            s_sz = [scr(4736, 512), scr(5248, 512)]
            s_tt = scr(5760, 512)
            s_acc = scr(6272, 512)
            s_ost = [scr(2624, 1024), scr(3648, 1024)]

            if samp:
                p.dma("sp", Xt, xs.rearrange("(b p) f -> p b f", p=128), writes=X.tok)
            else:
                p.dma("sp", Xt, xp[ti * 512:(ti + 1) * 512, :].rearrange("(b p) f -> p b f", p=128), writes=X.tok)

            def load_nw(i):
                p.dma("sp", s_nw.ap, bc_d[i], writes=s_nw.tok)

            def rms_stats(b, k):
                ss = s_ss[k]
                A("act", lambda e: e.activation(out=s_junk.ap, in_=Xt[:, b, :], func=AF.Square, accum_out=ss.ap[:, 0:1]),
                  reads=X.tok, writes=s_junk.tok + ss.tok)
                A("act", lambda e: e.activation(out=ss.ap[:, 1:2], in_=ss.ap[:, 0:1], func=AF.Ln, scale=1.0 / DM, bias=RMS_EPS),
                  reads=ss.tok, writes=ss.tok)
                A("act", lambda e: e.activation(out=ss.ap[:, 2:3], in_=ss.ap[:, 1:2], func=AF.Exp, scale=-0.5),
                  reads=ss.tok, writes=ss.tok)

            def norm_to_hT():
                for b in range(NB):
                    rms_stats(b, b)
                for b in range(NB):
                    k = b % 2
                    hb = s_hb[k]
                    A("dve", lambda e, b=b, k=k, hb=hb: e.scalar_tensor_tensor(
                        out=hb.ap, in0=Xt[:, b, :], scalar=s_ss[b].ap[:, 2:3], in1=s_nw.ap, op0=ALU.mult, op1=ALU.mult),
                      reads=X.tok + s_ss[b].tok + s_nw.tok, writes=hb.tok)
                    pi = newps()
                    for c in range(8):
                        A("pe", lambda e, c=c, pi=pi, hb=hb: e.transpose(
                            out=psh(pi)[:, c * 128:(c + 1) * 128], in_=hb.ap[:, c * 128:(c + 1) * 128], identity=ident_bf),
                          reads=hb.tok + CBT, writes=PT(pi))
                    A("act", lambda e, b=b, pi=pi: e.copy(
                        out=hT[:, :, b * 128:(b + 1) * 128], in_=psh(pi)[:, 0:1024].rearrange("p (c t) -> p c t", t=128)),
                      reads=PT(pi), writes=HT.tok)

            print("mark L0 start", len(p.ops))
            load_nw(0)
            norm_to_hT()
            if ti == 0 and not samp:
                dump("d_hT", HT, BF16)

            yT = fm(B5)
            if samp:
                s_rows = Bview(1, 2048, F32, off=1024)
                HAS = Bview(0, 512, F32, off=1024)
                hist_s = HAS.ap.rearrange("p (c r) -> p c r", r=32)
                p.dma("sp", s_rows.ap[:32, :], sca, writes=s_rows.tok)
                for g in range(4):
                    pi = newps()
                    for c4 in range(4):
                        c = g * 4 + c4
                        A("pe", lambda e, c=c, c4=c4, pi=pi: e.transpose(
                            out=psf(pi)[:, c4 * 32:(c4 + 1) * 32], in_=s_rows.ap[:32, c * 128:(c + 1) * 128], identity=ident_f[:32, :32]),
                          reads=s_rows.tok + CT, writes=PT(pi))
                    A("dve", lambda e, g=g, pi=pi: e.tensor_copy(
                        out=hist_s[:, g * 4:(g + 1) * 4, :], in_=psf(pi)[:, 0:128].rearrange("p (c r) -> p c r", r=32)),
                      reads=PT(pi), writes=HAS.tok)
                NHA = view(o_B[0] + 1024 + 512, 512, F32)
                nhist_s = NHA.ap.rearrange("p (c r) -> p c r", r=32)

            for j in range(16):
                pis = [newps() for _ in range(4)]
                for bi in range(4):
                    bl = bi % 2
                    if bl == 0:
                        wt = wget()
                        wv = wt.ap.rearrange("p (b k n) -> p b k n", b=2, k=8)
                    for k in range(8):
                        A("pe", lambda e, wv=wv, bl=bl, k=k, pi=pis[bi]: e.matmul(
                            psf(pi)[:, :T], lhsT=wv[:, bl, k, :], rhs=hT[:, k, :], start=(k == 0), stop=(k == 7)),
                          reads=wt.tok + HT.tok, writes=PT(pis[bi]))
                pb, pc, pxa, pz = pis
                k2 = j % 2
                xe, xa, sz = s_xe[k2], s_xa[k2], s_sz[k2]
                A("act", lambda e, xa=xa, pxa=pxa: e.copy(out=xa.ap[:, :T], in_=psf(pxa)[:, :T]), reads=PT(pxa), writes=xa.tok)
                A("act", lambda e, sz=sz, pz=pz: e.activation(out=sz.ap[:, :T], in_=psf(pz)[:, :T], func=AF.Silu),
                  reads=PT(pz), writes=sz.tok)
                if ti == 0 and not samp and j == 0:
                    dump("d_xa0", xa, F32)
                    dump("d_sz0", sz, F32)
                if samp:
                    xe3 = xe.ap[:, 0:160].rearrange("p (s t) -> p s t", t=10)
                    A("dve", lambda e, xe3=xe3, j=j: e.tensor_copy(
                        out=xe3[:, :, 0:2], in_=hist_s[:, j, :].rearrange("p (s r) -> p s r", r=2)),
                      reads=HAS.tok, writes=xe.tok)
                    A("dve", lambda e, xe3=xe3, xa=xa, pc=pc: e.tensor_tensor(
                        out=xe3[:, :, 2:10], in0=psf(pc)[:, :128].rearrange("p (s t) -> p s t", t=8),
                        in1=xa.ap[:, :128].rearrange("p (s t) -> p s t", t=8), op=ALU.mult),
                      reads=PT(pc) + xa.tok, writes=xe.tok)
                    acc3 = s_acc.ap[:, :128].rearrange("p (s t) -> p s t", t=8)
                    win = [xe3[:, :, d:d + 8] for d in range(3)]
                    accv = acc3
                else:
                    A("dve", lambda e, xe=xe, j=j: e.tensor_copy(out=xe.ap[:, 0:2], in_=HISTAv.ap[:, j * 2:j * 2 + 2]),
                      reads=HISTAv.tok, writes=xe.tok)
                    A("dve", lambda e, xe=xe, xa=xa, pc=pc: e.tensor_tensor(
                        out=xe.ap[:, 2:2 + T], in0=psf(pc)[:, :T], in1=xa.ap[:, :T], op=ALU.mult),
                      reads=PT(pc) + xa.tok, writes=xe.tok)
                    win = [xe.ap[:, d:d + T] for d in range(3)]
                    accv = s_acc.ap[:, :T]
                A("dve", lambda e, sz=sz, pb=pb: e.tensor_tensor(out=s_tt.ap[:, :T], in0=psf(pb)[:, :T], in1=sz.ap[:, :T], op=ALU.mult),
                  reads=PT(pb) + sz.tok, writes=s_tt.tok)
                cw = [PAR[:, PC_CONVA + d * 16 + j:PC_CONVA + d * 16 + j + 1] for d in range(3)]
                A("dve", lambda e, accv=accv, win=win, cw=cw: e.tensor_scalar(
                    out=accv, in0=win[0], scalar1=cw[0], scalar2=None, op0=ALU.mult),
                  reads=xe.tok + CT, writes=s_acc.tok)
                for d in (1, 2):
                    A("dve", lambda e, accv=accv, win=win, cw=cw, d=d: e.scalar_tensor_tensor(
                        out=accv, in0=win[d], scalar=cw[d], in1=accv, op0=ALU.mult, op1=ALU.add),
                      reads=xe.tok + CT + s_acc.tok, writes=s_acc.tok)
                A("dve", lambda e, j=j: e.tensor_tensor(out=yT[:, j, :], in0=s_tt.ap[:, :T], in1=s_acc.ap[:, :T], op=ALU.mult),
                  reads=s_tt.tok + s_acc.tok, writes=ctok(4, j))
                if samp:
                    A("act", lambda e, xe3=xe3, j=j: e.copy(
                        out=nhist_s[:, j, :].rearrange("p (s r) -> p s r", r=2), in_=xe3[:, :, 8:10]),
                      reads=xe.tok, writes=NHA.tok)
                else:
                    A("act", lambda e, xe=xe, j=j: e.copy(out=HISTAv.ap[:, j * 2:j * 2 + 2], in_=xe.ap[:, T:T + 2]),
                      reads=xe.tok, writes=HISTAv.tok)

            if samp or last_p:
                nr = 32 if samp else 2
                src3 = nhist_s if samp else HISTAv.ap.rearrange("p (c r) -> p c r", r=2)
                srct = NHA.tok if samp else HISTAv.tok
                stg = scr(0, 2048, F32)
                for g in range(4):
                    pi = newps()
                    for c4 in range(4):
                        c = g * 4 + c4
                        A("pe", lambda e, c=c, c4=c4, pi=pi, src3=src3, nr=nr: e.matmul(
                            psf(pi)[:nr, c4 * 128:(c4 + 1) * 128], lhsT=src3[:, c, :], rhs=ident_f, start=True, stop=True),
                          reads=srct + CT, writes=PT(pi))
                    A("act", lambda e, g=g, pi=pi: e.copy(out=stg.ap[:nr, g * 512:(g + 1) * 512], in_=psf(pi)[:nr, :]),
                      reads=PT(pi), writes=stg.tok)
                p.dma("sp", cas if samp else cap, stg.ap[:nr, :], reads=stg.tok, is_out=True)

            def outproj(srcT, srcbuf):
                pss = [[newps(), newps()] for _ in range(NB)]
                for it in range(8):
                    w = wget()
                    wv = w.ap.rearrange("p (k n) -> p k n", k=2)
                    for k2 in range(2):
                        kc = it * 2 + k2
                        for b in range(NB):
                            for hf in range(2):
                                A("pe", lambda e, wv=wv, k2=k2, kc=kc, b=b, hf=hf, pi=pss[b][hf]: e.matmul(
                                    psf(pi)[:, :512], lhsT=srcT[:, kc, b * 128:(b + 1) * 128],
                                    rhs=wv[:, k2, hf * 512:(hf + 1) * 512], start=(kc == 0), stop=(kc == 15)),
                                  reads=w.tok + ctok(srcbuf, kc), writes=PT(pss[b][hf]))
                for b in range(NB):
                    for hf in range(2):
                        A("dve", lambda e, b=b, hf=hf, pi=pss[b][hf]: e.tensor_tensor(
                            out=Xt[:, b, hf * 512:(hf + 1) * 512], in0=psf(pi)[:, :512],
                            in1=Xt[:, b, hf * 512:(hf + 1) * 512], op=ALU.add),
                          reads=PT(pss[b][hf]) + X.tok, writes=X.tok)

            print("mark L0 outproj", len(p.ops))
            if ti == 0 and not samp:
                dump("d_yT", B5, BF16)
            outproj(yT, 4)
            if ti == 0 and not samp:
                dump("d_x1", X, F32)
            print("mark L1 start", len(p.ops))

            load_nw(1)
            norm_to_hT()
            xmT, xcT, vT, qT, kT = fm(B1), fm(B2), fm(B3), fm(B4), fm(B5)
            ogT, szT = xmT, qT

            if samp:
                s_rows_b = Bview(2, 2048, F32, off=1024)
                HBS = view(o_B[0] + 2048, 768, F32)
                NHB = view(o_B[0] + 2816, 768, F32)
                histb_s = HBS.ap.rearrange("p (c r) -> p c r", r=48)
                nhistb_s = NHB.ap.rearrange("p (c r) -> p c r", r=48)
                A("pool", lambda e: e.memset(s_rows_b.ap[:64, :], 0.0), writes=s_rows_b.tok)
                p.dma("sp", s_rows_b.ap[:48, :], scb, writes=s_rows_b.tok)
                for g in range(2):
                    pi = newps()
                    for c8 in range(8):
                        c = g * 8 + c8
                        A("pe", lambda e, c=c, c8=c8, pi=pi: e.transpose(
                            out=psf(pi)[:, c8 * 64:(c8 + 1) * 64], in_=s_rows_b.ap[:64, c * 128:(c + 1) * 128], identity=ident_f[:64, :64]),
                          reads=s_rows_b.tok + CT, writes=PT(pi))
                    A("dve", lambda e, g=g, pi=pi: e.tensor_copy(
                        out=histb_s[:, g * 8:(g + 1) * 8, :], in_=psf(pi)[:, 0:512].rearrange("p (c r) -> p c r", r=64)[:, :, 0:48]),
                      reads=PT(pi), writes=HBS.tok)

            for it in range(8):
                w = wget()
                wv = w.ap.rearrange("p (o k n) -> p o k n", o=2, k=8)
                for o2 in range(2):
                    oc = it * 2 + o2
                    pi = newps()
                    for k in range(8):
                        A("pe", lambda e, wv=wv, o2=o2, k=k, pi=pi: e.matmul(
                            psf(pi)[:, :T], lhsT=wv[:, o2, k, :], rhs=hT[:, k, :], start=(k == 0), stop=(k == 7)),
                          reads=w.tok + HT.tok, writes=PT(pi))
                    xe = s_xe[oc % 2]
                    if samp:
                        xe3 = xe.ap[:, 0:176].rearrange("p (s t) -> p s t", t=11)
                        A("dve", lambda e, xe3=xe3, oc=oc: e.tensor_copy(
                            out=xe3[:, :, 0:3], in_=histb_s[:, oc, :].rearrange("p (s r) -> p s r", r=3)),
                          reads=HBS.tok, writes=xe.tok)
                        A("act", lambda e, xe3=xe3, pi=pi: e.copy(
                            out=xe3[:, :, 3:11], in_=psf(pi)[:, :128].rearrange("p (s t) -> p s t", t=8)),
                          reads=PT(pi), writes=xe.tok)
                        A("act", lambda e, oc=oc, pi=pi: e.copy(out=xmT[:, oc, :], in_=psf(pi)[:, :T]),
                          reads=PT(pi), writes=B1.tok)
                        win = [xe3[:, :, d:d + 8] for d in range(4)]
                        accv = s_acc.ap[:, :128].rearrange("p (s t) -> p s t", t=8)
                    else:
                        A("dve", lambda e, xe=xe, oc=oc: e.tensor_copy(out=xe.ap[:, 0:3], in_=HISTBv.ap[:, oc * 3:oc * 3 + 3]),
                          reads=HISTBv.tok, writes=xe.tok)
                        A("act", lambda e, xe=xe, pi=pi: e.copy(out=xe.ap[:, 3:3 + T], in_=psf(pi)[:, :T]),
                          reads=PT(pi), writes=xe.tok)
                        A("act", lambda e, oc=oc, pi=pi: e.copy(out=xmT[:, oc, :], in_=psf(pi)[:, :T]),
                          reads=PT(pi), writes=B1.tok)
                        win = [xe.ap[:, d:d + T] for d in range(4)]
                        accv = s_acc.ap[:, :T]
                    cw = [PAR[:, PC_CONVB + d * 16 + oc:PC_CONVB + d * 16 + oc + 1] for d in range(4)]
                    A("act", lambda e, accv=accv, win=win, cw=cw: e.activation(out=accv, in_=win[0], func=AF.Copy, scale=cw[0]),
                      reads=xe.tok + CT, writes=s_acc.tok)
                    for d in (1, 2, 3):
                        A("dve", lambda e, accv=accv, win=win, cw=cw, d=d: e.scalar_tensor_tensor(
                            out=accv, in0=win[d], scalar=cw[d], in1=accv, op0=ALU.mult, op1=ALU.add),
                          reads=xe.tok + CT + s_acc.tok, writes=s_acc.tok)
                    A("act", lambda e, oc=oc: e.activation(out=xcT[:, oc, :], in_=s_acc.ap[:, :T], func=AF.Silu,
                                                           bias=PAR[:, PC_CB + oc:PC_CB + oc + 1]),
                      reads=s_acc.tok + CT, writes=B2.tok)
                    if samp:
                        A("act", lambda e, xe3=xe3, oc=oc: e.copy(
                            out=nhistb_s[:, oc, :].rearrange("p (s r) -> p s r", r=3), in_=xe3[:, :, 8:11]),
                          reads=xe.tok, writes=NHB.tok)
                    else:
                        A("act", lambda e, xe=xe, oc=oc: e.copy(out=HISTBv.ap[:, oc * 3:oc * 3 + 3], in_=xe.ap[:, T:T + 3]),
                          reads=xe.tok, writes=HISTBv.tok)

            if samp or last_p:
                nr = 48 if samp else 3
                src3 = nhistb_s if samp else HISTBv.ap.rearrange("p (c r) -> p c r", r=3)
                srct = NHB.tok if samp else HISTBv.tok
                stg = scr(0, 2048, F32)
                for g in range(4):
                    pi = newps()
                    for c4 in range(4):
                        c = g * 4 + c4
                        A("pe", lambda e, c=c, c4=c4, pi=pi, src3=src3, nr=nr: e.matmul(
                            psf(pi)[:nr, c4 * 128:(c4 + 1) * 128], lhsT=src3[:, c, :], rhs=ident_f, start=True, stop=True),
                          reads=srct + CT, writes=PT(pi))
                    A("act", lambda e, g=g, pi=pi: e.copy(out=stg.ap[:nr, g * 512:(g + 1) * 512], in_=psf(pi)[:nr, :]),
                      reads=PT(pi), writes=stg.tok)
                p.dma("sp", cbs if samp else cbp, stg.ap[:nr, :], reads=stg.tok, is_out=True)

            print("mark vqk", len(p.ops))
            cnt_ev = [0]
            for (dst, dv_, src, srct) in [(vT, B3, xmT, B1.tok), (qT, B4, xcT, B2.tok), (kT, B5, xcT, B2.tok)]:
                for oc in range(16):
                    w = wget()
                    wv = w.ap.rearrange("p (k n) -> p k n", k=16)
                    pi = newps()
                    for k in range(16):
                        A("pe", lambda e, wv=wv, k=k, pi=pi, src=src: e.matmul(
                            psf(pi)[:, :T], lhsT=wv[:, k, :], rhs=src[:, k, :], start=(k == 0), stop=(k == 15)),
                          reads=w.tok + srct, writes=PT(pi))
                    if cnt_ev[0] % 2 == 0:
                        A("act", lambda e, dst=dst, oc=oc, pi=pi: e.copy(out=dst[:, oc, :], in_=psf(pi)[:, :T]),
                          reads=PT(pi), writes=dv_.tok)
                    else:
                        A("dve", lambda e, dst=dst, oc=oc, pi=pi: e.tensor_copy(out=dst[:, oc, :], in_=psf(pi)[:, :T]),
                          reads=PT(pi), writes=dv_.tok)
                    cnt_ev[0] += 1

            print("mark gates", len(p.ops))
            pg = newps()
            gsrc = [(qT, B4.tok)] * 16 + [(kT, B5.tok)] * 16 + [(vT, B3.tok)] * 16
            for kc in range(48):
                A("pe", lambda e, kc=kc, pg=pg: e.matmul(
                    psf(pg)[:8, :T], lhsT=wif_bf[:, kc, :], rhs=gsrc[kc][0][:, kc % 16, :], start=(kc == 0), stop=(kc == 47)),
                  reads=CBT + gsrc[kc][1], writes=PT(pg))
            GSB = scr(3072, 512, parts=8)
            A("act", lambda e: e.activation(out=GSB.ap[:8, :T], in_=psf(pg)[:8, :T], func=AF.Identity,
                                            bias=PAR[:8, PC_BIF:PC_BIF + 1]),
              reads=PT(pg) + CT, writes=GSB.tok)
            pf = newps()
            A("pe", lambda e: e.matmul(psf(pf)[:4, :T], lhsT=PAR[:8, PC_SEL:PC_SEL + 4], rhs=GSB.ap[:8, :T], start=True, stop=True),
              reads=GSB.tok + CT, writes=PT(pf))
            G = [scr(i * 512, 512, parts=4) for i in range(6)]
            g_ = [g.ap[:4, :T] for g in G]
            CC = CCv.ap[:4, :T]
            NEGA = NEGAv.ap[:4, :T]
            rmask = PAR[:4, (PC_RMASK_S if samp else PC_RMASK):(PC_RMASK_S if samp else PC_RMASK) + T]
            amask = PAR[:4, (PC_AMASK_S if samp else PC_AMASK):(PC_AMASK_S if samp else PC_AMASK) + T]
            A("act", lambda e: e.copy(out=g_[2], in_=psf(pf)[:4, :T]), reads=PT(pf), writes=G[2].tok)
            A("dve", lambda e: e.scalar_tensor_tensor(out=g_[0], in0=g_[2], scalar=-1.0, in1=g_[2], op0=ALU.mult, op1=ALU.max),
              reads=G[2].tok, writes=G[0].tok)
            A("act", lambda e: e.activation(out=g_[1], in_=g_[0], func=AF.Exp, scale=-1.0), reads=G[0].tok, writes=G[1].tok)
            A("act", lambda e: e.activation(out=g_[1], in_=g_[1], func=AF.Ln, bias=1.0), reads=G[1].tok, writes=G[1].tok)
            A("dve", lambda e: e.tensor_scalar_min(out=g_[0], in0=g_[2], scalar1=0.0), reads=G[2].tok, writes=G[0].tok)
            A("dve", lambda e: e.tensor_sub(out=g_[0], in0=g_[0], in1=g_[1]), reads=G[0].tok + G[1].tok, writes=G[0].tok)
            A("dve", lambda e: e.tensor_tensor_scan(out=g_[5], data0=rmask, data1=g_[0], initial=0.0, op0=ALU.mult, op1=ALU.add),
              reads=G[0].tok + CT, writes=G[5].tok)
            A("dve", lambda e: e.tensor_sub(out=CC, in0=GSB.ap[:4, :T], in1=g_[5]), reads=GSB.tok + G[5].tok, writes=CCv.tok)
            A("dve", lambda e: e.tensor_tensor_scan(out=g_[1], data0=amask, data1=CC, initial=0.0, op0=ALU.add, op1=ALU.max),
              reads=CCv.tok + CT, writes=G[1].tok)
            MALL = MALLv.ap
            MT = scr(3584 + 64, 32, parts=4)
            if samp:
                p.dma("sp", MALL[:4, 0:16], sm, writes=MALLv.tok)
            else:
                for c in range(NCH):
                    le = c * L + L - 1
                    A("dve", lambda e, c=c, le=le: e.tensor_tensor(out=MT.ap[:4, 0:1], in0=g_[1][:, le:le + 1], in1=MALL[:4, c:c + 1], op=ALU.max),
                      reads=G[1].tok + MALLv.tok, writes=MT.tok)
                    A("dve", lambda e, c=c, le=le: e.tensor_tensor(out=MALL[:4, c + 1:c + 2], in0=MT.ap[:4, 0:1], in1=g_[5][:, le:le + 1], op=ALU.add),
                      reads=MT.tok + G[5].tok, writes=MALLv.tok)

            def v3(ap):
                return ap.rearrange("p (c l) -> p c l", l=L)

            mprev_bc = MALL[:4, 0:NCH].unsqueeze(2).to_broadcast([4, NCH, L])
            A("dve", lambda e: e.tensor_tensor(out=v3(g_[2]), in0=v3(g_[1]), in1=mprev_bc, op=ALU.max),
              reads=G[1].tok + MALLv.tok, writes=G[2].tok)
            A("dve", lambda e: e.tensor_scalar(out=NEGA, in0=g_[2], scalar1=-1.0, scalar2=None, op0=ALU.mult),
              reads=G[2].tok, writes=NEGAv.tok)
            A("dve", lambda e: e.tensor_tensor(out=g_[3], in0=g_[5], in1=g_[2], op=ALU.add), reads=G[5].tok + G[2].tok, writes=G[3].tok)
            A("act", lambda e: e.activation(out=g_[3], in_=g_[3], func=AF.Exp, scale=-1.0), reads=G[3].tok, writes=G[3].tok)
            A("dve", lambda e: e.tensor_tensor(out=v3(g_[4]), in0=mprev_bc, in1=v3(g_[2]), op=ALU.subtract),
              reads=G[2].tok + MALLv.tok, writes=G[4].tok)
            A("act", lambda e: e.activation(out=g_[4], in_=g_[4], func=AF.Exp), reads=G[4].tok, writes=G[4].tok)
            alast_bc = v3(g_[2])[:, :, L - 1:L].to_broadcast([4, NCH, L])
            A("dve", lambda e: e.tensor_tensor(out=v3(g_[0]), in0=v3(CC), in1=alast_bc, op=ALU.subtract),
              reads=CCv.tok + G[2].tok, writes=G[0].tok)
            A("act", lambda e: e.activation(out=g_[0], in_=g_[0], func=AF.Exp, bias=math.log(KSCALE)), reads=G[0].tok, writes=G[0].tok)
            if samp:
                A("dve", lambda e: e.tensor_tensor(out=MT.ap[:4, 0:16], in0=v3(g_[5])[:, :, L - 1], in1=v3(g_[2])[:, :, L - 1], op=ALU.add),
                  reads=G[5].tok + G[2].tok, writes=MT.tok)
                p.dma("sp", ms_o, MT.ap[:4, 0:16], reads=MT.tok, is_out=True)
            else:
                if last_p:
                    p.dma("sp", mp_o, MALL[:4, NCH:NCH + 1], reads=MALLv.tok, is_out=True)
            pq = newps()
            for c in range(NCH):
                for qi, (gap_, gtk_) in enumerate(((g_[0], G[0].tok), (g_[4], G[4].tok), (g_[3], G[3].tok), (CC, CCv.tok))):
                    A("pe", lambda e, c=c, qi=qi, gap_=gap_: e.matmul(
                        psf(pq)[:L, (c * 4 + qi) * 4:(c * 4 + qi) * 4 + 4], lhsT=gap_[:, c * L:(c + 1) * L],
                        rhs=PAR[:4, PC_EYE4:PC_EYE4 + 4], start=True, stop=True),
                      reads=gtk_ + CT, writes=PT(pq))
            COLQv = scr(2560, 256, parts=64)
            COLQ = COLQv.ap
            A("act", lambda e: e.copy(out=COLQ[:L, :NCH * 16], in_=psf(pq)[:L, :NCH * 16]), reads=PT(pq), writes=COLQv.tok)
            DECD = scr(3584, 64, parts=4)
            A("dve", lambda e: e.tensor_tensor(
                out=DECD.ap[:4, :NCH * 4].rearrange("p (c h) -> p c h", h=4),
                in0=v3(g_[4])[:, :, L - 1:L].to_broadcast([4, NCH, 4]),
                in1=PAR[:4, PC_EYE4:PC_EYE4 + 4].unsqueeze(1).to_broadcast([4, NCH, 4]), op=ALU.mult),
              reads=G[4].tok + CT, writes=DECD.tok)
            pd = newps()
            A("pe", lambda e: e.matmul(psf(pd)[:, :NCH * 4], lhsT=PAR[:4, PC_ONES:PC_ONES + 128], rhs=DECD.ap[:4, :NCH * 4],
                                       start=True, stop=True),
              reads=DECD.tok + CT, writes=PT(pd))
            DECB = DECBv.ap
            A("act", lambda e: e.copy(out=DECB[:, :NCH * 4], in_=psf(pd)[:, :NCH * 4]), reads=PT(pd), writes=DECBv.tok)
            if not samp:
                A("dve", lambda e: e.tensor_copy(out=MALL[:4, 0:1], in_=MALL[:4, NCH:NCH + 1]), reads=MALLv.tok, writes=MALLv.tok)

            if samp:
                NROWS = Bview(3, 256, F32, off=1024)
                NALL = view(o_B[0] + 3584, 256, F32)
                NBFA = view(o_B[0] + 3840, 128, BF16)
                nrows3 = NROWS.ap.rearrange("p (a d) -> p a d", a=2)
                p.dma("sp", nrows3, sn.rearrange("(a p) d -> p a d", p=128), writes=NROWS.tok)
                pi = newps()
                for a in range(2):
                    A("pe", lambda e, a=a, pi=pi: e.transpose(out=psf(pi)[:, a * 128:(a + 1) * 128], in_=nrows3[:, a, :], identity=ident_f),
                      reads=NROWS.tok + CT, writes=PT(pi))
                A("dve", lambda e, pi=pi: e.tensor_copy(out=NALL.ap, in_=psf(pi)[:, :256]), reads=PT(pi), writes=NALL.tok)
                A("act", lambda e: e.copy(out=NBFA.ap, in_=NALL.ap), reads=NALL.tok, writes=NBFA.tok)

            LMv = scr(512, 4 * T, parts=64)
            for h_ in range(4):
                pl = newps()
                A("pe", lambda e, pl=pl, h_=h_: e.matmul(psf(pl)[:L, :T], lhsT=PAR[:4, PC_EH + h_ * 64:PC_EH + h_ * 64 + L], rhs=NEGA,
                                                        start=True, stop=True),
                  reads=NEGAv.tok + CT, writes=PT(pl))
                A("dve", lambda e, pl=pl, h_=h_: e.tensor_tensor(
                    out=LMv.ap[:L, h_ * T:(h_ + 1) * T].rearrange("p (c l) -> p c l", l=L),
                    in0=psf(pl)[:L, :T].rearrange("p (c l) -> p c l", l=L),
                    in1=PAR[:L, PC_NEG:PC_NEG + L].unsqueeze(1).to_broadcast([L, NCH, L]), op=ALU.add),
                  reads=PT(pl) + CT, writes=LMv.tok)

            print("mark recurrence", len(p.ops))
            s_D = [scr(3712, 64, parts=64), scr(3776, 64, parts=64)]
            s_Sd = [scr(3840, 32, BF16, parts=64), scr(3872, 32, BF16, parts=64)]
            s_kw = [scr(3904, 256, BF16, parts=64), scr(4160, 256, BF16, parts=64)]
            s_vb = [scr(4416, 256, BF16, parts=64), scr(4672, 256, BF16, parts=64)]
            s_t1 = scr(4928, 512, parts=64)
            s_hh = [scr(5440, 512, parts=64), scr(5952, 512, parts=64)]
            s_hn = scr(6464, 1024, BF16, parts=64)
            s_sm = [scr(7488, 32, parts=64), scr(7520, 32, parts=64)]
            NIT = NCH * 4
            PF = 3

            def st_of(idx):
                c, h = idx // 4, idx % 4
                if samp:
                    Cst = CSTv[idx % 4]
                    return (Cst, NALL.ap[:, idx * 4:idx * 4 + 4], NALL.tok, NBFA.ap[:, idx * 4:idx * 4 + 4], NBFA.tok)
                return (CSTv[h], NSTh[h].ap, NSTh[h].tok, NBFh[h].ap, NBFh[h].tok)

            def c3(Cst):
                return Cst.ap.rearrange("p (k e) -> p k e", k=4)

            def load_C(idx):
                Cst = CSTv[idx % 4]
                p.dma("sp", c3(Cst), sC[idx].rearrange("(k p) e -> p k e", p=128), writes=Cst.tok)

            def castC(idx):
                Cst = st_of(idx)[0]
                Cbf = CBFv[idx % 2]
                A("act", lambda e, Cbf=Cbf, Cst=Cst: e.copy(out=Cbf.ap, in_=Cst.ap), reads=Cst.tok, writes=Cbf.tok)

            def front(idx):
                c, h = idx // 4, idx % 4
                t0 = c * L
                k2 = idx % 2
                pa, pbk, pvv = k2, 2, 3
                D, Sd, kw, vb = s_D[k2], s_Sd[k2], s_kw[k2], s_vb[k2]
                qs = [qT[:, h * 4 + dk, t0:t0 + L] for dk in range(4)]
                ks_ = [kT[:, h * 4 + dk, t0:t0 + L] for dk in range(4)]
                vs = [vT[:, h * 4 + dk, t0:t0 + L] for dk in range(4)]
                for dk in range(4):
                    A("pe", lambda e, dk=dk, pa=pa, ks_=ks_, qs=qs: e.matmul(
                        psf(pa)[:L, 0:L], lhsT=ks_[dk], rhs=qs[dk], start=(dk == 0), stop=(dk == 3)),
                      reads=B4.tok + B5.tok, writes=PT(pa))
                for dk in range(4):
                    A("pe", lambda e, dk=dk, pbk=pbk, ks_=ks_: e.matmul(psf(pbk)[:L, dk * 128:(dk + 1) * 128], lhsT=ks_[dk], rhs=ident_bf, start=True, stop=True),
                      reads=B5.tok + CBT, writes=PT(pbk))
                for dk in range(4):
                    A("pe", lambda e, dk=dk, pvv=pvv, vs=vs: e.matmul(psf(pvv)[:L, dk * 128:(dk + 1) * 128], lhsT=vs[dk], rhs=ident_bf, start=True, stop=True),
                      reads=B3.tok + CBT, writes=PT(pvv))
                ccol = (c * 4 + 3) * 4 + h
                A("act", lambda e, D=D, h=h, t0=t0, ccol=ccol: e.activation(
                    out=D.ap[:L, :L], in_=LMv.ap[:L, h * T + t0:h * T + t0 + L], func=AF.Exp, bias=COLQ[:L, ccol:ccol + 1]),
                  reads=LMv.tok + COLQv.tok, writes=D.tok)
                A("dve", lambda e, pa=pa, D=D, Sd=Sd: e.scalar_tensor_tensor(
                    out=Sd.ap[:L, :L], in0=psf(pa)[:L, 0:L], scalar=KSCALE, in1=D.ap[:L, :L], op0=ALU.mult, op1=ALU.mult),
                  reads=PT(pa) + D.tok, writes=Sd.tok)
                cq = (c * 4) * 4 + h
                A("act", lambda e, pbk=pbk, kw=kw, cq=cq: e.activation(
                    out=kw.ap[:L, :512], in_=psf(pbk)[:L, 0:512], func=AF.Copy, scale=COLQ[:L, cq:cq + 1]),
                  reads=PT(pbk) + COLQv.tok, writes=kw.tok)
                A("act", lambda e, pvv=pvv, vb=vb: e.copy(out=vb.ap[:L, :512], in_=psf(pvv)[:L, 0:512]), reads=PT(pvv), writes=vb.tok)

            def mid(idx, part):
                c, h = idx // 4, idx % 4
                t0 = c * L
                k2 = idx % 2
                Cst, nst, nstt, nbf, nbft = st_of(idx)
                Cbf = CBFv[k2]
                Cb3 = Cbf.ap.rearrange("p (k e) -> p k e", k=4)
                pa, phq, phs = k2, 4, 5
                Sd, vb, hh, sm_ = s_Sd[k2], s_vb[k2], s_hh[k2], s_sm[k2]
                qs = [qT[:, h * 4 + dk, t0:t0 + L] for dk in range(4)]
                s_ = sm_.ap
                if part == "b":
                    A("dve", lambda e, phs=phs, s_=s_, hh=hh: e.scalar_tensor_tensor(
                        out=hh.ap[:L, :512], in0=psf(phs)[:L, :512], scalar=s_[:L, 4:5], in1=s_t1.ap[:L, :512], op0=ALU.mult, op1=ALU.add),
                      reads=PT(phs) + sm_.tok + s_t1.tok, writes=hh.tok)
                    A("dve", lambda e, s_=s_, hh=hh: e.bn_stats(out=s_[:L, 8:14], in_=hh.ap[:L, :512]), reads=hh.tok, writes=sm_.tok)
                    A("dve", lambda e, s_=s_: e.bn_aggr(out=s_[:L, 14:16], in_=s_[:L, 8:14]), reads=sm_.tok, writes=sm_.tok)
                    A("act", lambda e, s_=s_: e.activation(out=s_[:L, 16:17], in_=s_[:L, 15:16], func=AF.Ln, bias=LN_EPS), reads=sm_.tok, writes=sm_.tok)
                    A("act", lambda e, s_=s_: e.activation(out=s_[:L, 17:18], in_=s_[:L, 16:17], func=AF.Exp, scale=-0.5), reads=sm_.tok, writes=sm_.tok)
                    return
                if part == "c":
                    A("dve", lambda e, s_=s_, hh=hh, h=h: e.tensor_scalar(
                        out=s_hn.ap[:L, h * 512:(h + 1) * 512], in0=hh.ap[:L, :512], scalar1=s_[:L, 14:15], scalar2=s_[:L, 17:18],
                        op0=ALU.subtract, op1=ALU.mult),
                      reads=hh.tok + sm_.tok, writes=s_hn.tok)
                    return
                for dk in range(4):
                    A("pe", lambda e, dk=dk, phq=phq, qs=qs, Cb3=Cb3: e.matmul(
                        psf(phq)[:L, :512], lhsT=qs[dk], rhs=Cb3[:, dk, :], start=(dk == 0), stop=(dk == 3)),
                      reads=B4.tok + Cbf.tok, writes=PT(phq))
                for dk in range(4):
                    A("pe", lambda e, dk=dk, pa=pa, qs=qs, nbf=nbf: e.matmul(
                        psf(pa)[:L, 128:129], lhsT=qs[dk], rhs=nbf[:, dk:dk + 1], start=(dk == 0), stop=(dk == 3)),
                      reads=B4.tok + nbft, writes=PT(pa))
                A("pe", lambda e, phs=phs, Sd=Sd, vb=vb: e.matmul(psf(phs)[:L, :512], lhsT=Sd.ap[:L, :L], rhs=vb.ap[:L, :512], start=True, stop=True),
                  reads=Sd.tok + vb.tok, writes=PT(phs))
                A("pe", lambda e, pa=pa, Sd=Sd: e.matmul(psf(pa)[:L, 129:130], lhsT=Sd.ap[:L, :L], rhs=ones_bf[:L, 0:1], start=True, stop=True),
                  reads=Sd.tok + CBT, writes=PT(pa))
                s_ = sm_.ap
                cq = (c * 4) * 4 + h
                wi = COLQ[:L, cq + 4:cq + 5]
                em = COLQ[:L, cq + 8:cq + 9]
                A("act", lambda e, pa=pa, s_=s_: e.copy(out=s_[:L, 0:2], in_=psf(pa)[:L, 128:130]), reads=PT(pa), writes=sm_.tok)
                A("dve", lambda e, s_=s_, wi=wi: e.scalar_tensor_tensor(out=s_[:L, 2:3], in0=s_[:L, 0:1], scalar=wi, in1=s_[:L, 1:2],
                                                                      op0=ALU.mult, op1=ALU.add),
                  reads=sm_.tok + COLQv.tok, writes=sm_.tok)
                A("dve", lambda e, s_=s_: e.scalar_tensor_tensor(out=s_[:L, 3:4], in0=s_[:L, 2:3], scalar=-1.0, in1=s_[:L, 2:3],
                                                                 op0=ALU.mult, op1=ALU.max),
                  reads=sm_.tok, writes=sm_.tok)
                A("dve", lambda e, s_=s_, em=em: e.tensor_tensor(out=s_[:L, 3:4], in0=s_[:L, 3:4], in1=em, op=ALU.max),
                  reads=sm_.tok + COLQv.tok, writes=sm_.tok)
                A("dve", lambda e, s_=s_: e.reciprocal(out=s_[:L, 4:5], in_=s_[:L, 3:4]), reads=sm_.tok, writes=sm_.tok)
                A("dve", lambda e, s_=s_, wi=wi: e.tensor_tensor(out=s_[:L, 5:6], in0=s_[:L, 4:5], in1=wi, op=ALU.mult),
                  reads=sm_.tok + COLQv.tok, writes=sm_.tok)
                A("act", lambda e, phq=phq, s_=s_: e.activation(out=s_t1.ap[:L, :512], in_=psf(phq)[:L, :512], func=AF.Copy, scale=s_[:L, 5:6]),
                  reads=PT(phq) + sm_.tok, writes=s_t1.tok)

            def back(idx, part):
                c, h = idx // 4, idx % 4
                k2 = idx % 2
                Cst, nst, nstt, nbf, nbft = st_of(idx)
                C3 = c3(Cst)
                kw, vb = s_kw[k2], s_vb[k2]
                pa = (idx + 1) % 2
                dec = DECB[:, idx:idx + 1]
                for dk in ((0, 1) if part == "a" else (2, 3)):
                    pk = 6 + dk % 2
                    A("pe", lambda e, dk=dk, pk=pk, kw=kw, vb=vb: e.matmul(
                        psf(pk)[:, :512], lhsT=kw.ap[:L, dk * 128:(dk + 1) * 128], rhs=vb.ap[:L, :512], start=True, stop=True),
                      reads=kw.tok + vb.tok, writes=PT(pk))
                    A("dve", lambda e, dk=dk, pk=pk, C3=C3, dec=dec: e.scalar_tensor_tensor(
                        out=C3[:, dk, :], in0=C3[:, dk, :], scalar=dec, in1=psf(pk)[:, :512], op0=ALU.mult, op1=ALU.add),
                      reads=PT(pk) + Cst.tok + DECBv.tok, writes=Cst.tok)
                if part == "a":
                    return
                for dk in range(4):
                    A("pe", lambda e, dk=dk, pa=pa, kw=kw: e.matmul(
                        psf(pa)[:, 132 + dk:133 + dk], lhsT=kw.ap[:L, dk * 128:(dk + 1) * 128], rhs=ones_bf[:L, 0:1], start=True, stop=True),
                      reads=kw.tok + CBT, writes=PT(pa))
                A("dve", lambda e, pa=pa, nst=nst, dec=dec: e.scalar_tensor_tensor(
                    out=nst, in0=nst, scalar=dec, in1=psf(pa)[:, 132:136], op0=ALU.mult, op1=ALU.add),
                  reads=PT(pa) + nstt + DECBv.tok, writes=nstt)
                if samp:
                    p.dma("pool", Cs[idx].rearrange("(k p) e -> p k e", p=128), C3, reads=Cst.tok, is_out=True)
                else:
                    A("pool", lambda e, nbf=nbf, nst=nst: e.tensor_copy(out=nbf, in_=nst), reads=nstt, writes=nbft)
                    if last_p and c == NCH - 1:
                        p.dma("sp", Cp[h].rearrange("(k p) e -> p k e", p=128), C3, reads=Cst.tok, is_out=True)

            def outstage(c):
                t0 = c * L
                po = 2
                if samp:
                    for fc in range(16):
                        A("pe", lambda e, fc=fc, po=po: e.matmul(psf(po)[:, fc * L:(fc + 1) * L], lhsT=s_hn.ap[:L, fc * 128:(fc + 1) * 128],
                                                                 rhs=ident_bf[:L, :L], start=True, stop=True),
                          reads=s_hn.tok + CBT, writes=PT(po))
                    A("act", lambda e, po=po, t0=t0: e.copy(out=ogT[:, :, t0:t0 + L], in_=psf(po)[:, :16 * L].rearrange("p (c l) -> p c l", l=L)),
                      reads=PT(po), writes=B1.tok)
                else:
                    for fc in range(16):
                        A("pe", lambda e, fc=fc, po=po: e.transpose(out=psh(po)[:, fc * L:(fc + 1) * L], in_=s_hn.ap[:L, fc * 128:(fc + 1) * 128],
                                                                    identity=ident_bf[:L, :L]),
                          reads=s_hn.tok + CBT, writes=PT(po))
                    A("act", lambda e, po=po, t0=t0: e.copy(out=ogT[:, :, t0:t0 + L], in_=psh(po)[:, :16 * L].rearrange("p (c l) -> p c l", l=L)),
                      reads=PT(po), writes=B1.tok)

            if samp:
                for i in range(min(PF, NIT)):
                    load_C(i)
            castC(0)
            if NIT > 1 and not samp:
                castC(1)
            front(0)
            for idx in range(NIT):
                mid(idx, "a")
                if (not samp) and idx + 2 < NIT:
                    castC(idx + 2)
                if samp and idx + 1 < NIT:
                    castC(idx + 1)
                if idx > 0 and idx % 4 == 0:
                    outstage(idx // 4 - 1)
                if idx > 0:
                    back(idx - 1, "a")
                mid(idx, "b")
                if idx > 0:
                    back(idx - 1, "b")
                if idx + 1 < NIT:
                    front(idx + 1)
                mid(idx, "c")
                if samp and idx + PF < NIT:
                    load_C(idx + PF)
            outstage(NCH - 1)
            back(NIT - 1, "a")
            back(NIT - 1, "b")

            if samp:
                pi = newps()
                for a in range(2):
                    A("pe", lambda e, a=a, pi=pi: e.transpose(out=psf(pi)[:, a * 128:(a + 1) * 128], in_=NALL.ap[:, a * 128:(a + 1) * 128], identity=ident_f),
                      reads=NALL.tok + CT, writes=PT(pi))
                A("act", lambda e, pi=pi: e.copy(out=NROWS.ap, in_=psf(pi)[:, :256]), reads=PT(pi), writes=NROWS.tok)
                p.dma("sp", ns_o.rearrange("(a p) d -> p a d", p=128), nrows3, reads=NROWS.tok, is_out=True)
            elif last_p:
                pi = newps()
                for h in range(4):
                    A("dve", lambda e, h=h: e.tensor_copy(out=NSTC.ap[:, h * 4:h * 4 + 4], in_=NSTh[h].ap), reads=NSTh[h].tok, writes=NSTC.tok)
                A("pe", lambda e, pi=pi: e.matmul(psf(pi)[:16, 0:128], lhsT=NSTC.ap, rhs=ident_f, start=True, stop=True), reads=NSTC.tok + CT, writes=PT(pi))
                NO = scr(0, 128, parts=16)
                A("act", lambda e, pi=pi: e.copy(out=NO.ap[:16, :], in_=psf(pi)[:16, 0:128]), reads=PT(pi), writes=NO.tok)
                p.dma("sp", np_o, NO.ap[:16, :], reads=NO.tok, is_out=True)

            print("mark z", len(p.ops))
            for it in range(8):
                w = wget()
                wv = w.ap.rearrange("p (o k n) -> p o k n", o=2, k=8)
                for o2 in range(2):
                    oc = it * 2 + o2
                    pi = newps()
                    for k in range(8):
                        A("pe", lambda e, wv=wv, o2=o2, k=k, pi=pi: e.matmul(
                            psf(pi)[:, :T], lhsT=wv[:, o2, k, :], rhs=hT[:, k, :], start=(k == 0), stop=(k == 7)),
                          reads=w.tok + HT.tok, writes=PT(pi))
                    A("act", lambda e, oc=oc, pi=pi: e.activation(out=szT[:, oc, :], in_=psf(pi)[:, :T], func=AF.Silu),
                      reads=PT(pi), writes=ctok(3, oc))
            s_o1 = [scr(0, 512), scr(512, 512)]
            for fc in range(16):
                o1 = s_o1[fc % 2]
                A("dve", lambda e, fc=fc, o1=o1: e.tensor_scalar(out=o1.ap[:, :T], in0=xcT[:, fc, :], scalar1=PAR[:, PC_SKIP + fc:PC_SKIP + fc + 1],
                                                                scalar2=None, op0=ALU.mult),
                  reads=ctok(1, fc) + CT, writes=o1.tok)
                A("dve", lambda e, fc=fc, o1=o1: e.scalar_tensor_tensor(out=o1.ap[:, :T], in0=ogT[:, fc, :], scalar=PAR[:, PC_ONORM + fc:PC_ONORM + fc + 1],
                                                                       in1=o1.ap[:, :T], op0=ALU.mult, op1=ALU.add),
                  reads=ctok(0, fc) + CT + o1.tok, writes=o1.tok)
                A("dve", lambda e, fc=fc, o1=o1: e.tensor_tensor(out=ogT[:, fc, :], in0=o1.ap[:, :T], in1=szT[:, fc, :], op=ALU.mult),
                  reads=o1.tok + ctok(3, fc), writes=ctok(0, fc))
            outproj(ogT, 0)

            print("mark final", len(p.ops))
            load_nw(2)
            for b in range(NB):
                k = b % 2
                rms_stats(b, k)
                ost = s_ost[k]
                A("dve", lambda e, b=b, k=k, ost=ost: e.scalar_tensor_tensor(
                    out=ost.ap, in0=Xt[:, b, :], scalar=s_ss[k].ap[:, 2:3], in1=s_nw.ap, op0=ALU.mult, op1=ALU.mult),
                  reads=X.tok + s_ss[k].tok + s_nw.tok, writes=ost.tok)
                if samp:
                    p.dma("sp", ys, ost.ap, reads=ost.tok, is_out=True)
                else:
                    r0 = ti * 512 + b * 128
                    p.dma("sp", yp[r0:r0 + 128, :], ost.ap, reads=ost.tok, is_out=True)

        for ti in range(n_ptiles):
            run_tile("p", ti)
        if do_sample:
            run_tile("s", 0)
        print("ops", len(p.ops), "arena scratch words", SCR_WORDS)
        p.emit(sems, dsems)
    return nc


def _host_prep(inp):
    f = np.float32
    a_w_in = np.asarray(inp["a_w_in"], f)[0]
    a_w_out = np.asarray(inp["a_w_out"], f)[0]
    b_w_in = np.asarray(inp["b_w_in"], f)[0]
    b_w_out = np.asarray(inp["b_w_out"], f)[0]
    items = []
    A5 = a_w_in.reshape(8, 128, 4, 16, 128)
    t = A5.transpose(3, 2, 1, 0, 4)
    for j in range(16):
        for half in range(2):
            blk = t[j, half * 2:half * 2 + 2]
            items.append(blk.transpose(1, 0, 2, 3).reshape(128, ITEM))
    def outw(w):
        W = w.reshape(8, 2, 128, 1024)
        return [W[i].transpose(1, 0, 2).reshape(128, ITEM) for i in range(8)]
    items += outw(a_w_out)
    def inw(wcols):
        W = wcols.reshape(8, 128, 8, 2, 128)
        return [W[:, :, i].transpose(1, 2, 0, 3).reshape(128, ITEM) for i in range(8)]
    items += inw(b_w_in[:, :2048])
    def sqw(w):
        W = w.reshape(16, 128, 16, 128)
        return [W[:, :, oc].transpose(1, 0, 2).reshape(128, ITEM) for oc in range(16)]
    items += sqw(np.asarray(inp["b_w_v"], f)[0])
    items += sqw(np.asarray(inp["b_w_q"], f)[0])
    items += sqw(np.asarray(inp["b_w_k"], f)[0])
    items += inw(b_w_in[:, 2048:])
    items += outw(b_w_out)
    assert len(items) == NITEMS
    wst = np.ascontiguousarray(np.stack(items, 0))

    par = np.zeros((128, NPAR), f)
    def fmaj(v):
        return np.asarray(v, f).reshape(16, 128).T
    ca = np.asarray(inp["a_conv_w"], f)[0]
    cb = np.asarray(inp["b_conv_w"], f)[0]
    for d in range(3):
        par[:, PC_CONVA + d * 16:PC_CONVA + (d + 1) * 16] = fmaj(ca[d])
    for d in range(4):
        par[:, PC_CONVB + d * 16:PC_CONVB + (d + 1) * 16] = fmaj(cb[d])
    par[:, PC_CB:PC_CB + 16] = fmaj(np.asarray(inp["b_conv_b"], f)[0])
    par[:, PC_SKIP:PC_SKIP + 16] = fmaj(np.asarray(inp["b_skip"], f)[0])
    par[:, PC_ONORM:PC_ONORM + 16] = fmaj(np.asarray(inp["b_onorm_w"], f)[0])
    par[:, PC_IDENT:PC_IDENT + 128] = np.eye(128, dtype=f)
    jj, ii = np.meshgrid(np.arange(64), np.arange(64), indexing="ij")
    par[:64, PC_NEG:PC_NEG + 64] = np.where(ii >= jj, 0.0, -30000.0).astype(f)
    for h in range(4):
        par[h, PC_EH + h * 64:PC_EH + (h + 1) * 64] = 1.0
    par[:4, PC_EYE4:PC_EYE4 + 4] = np.eye(4, dtype=f)
    par[:, PC_ONES:PC_ONES + 128] = 1.0
    for h in range(4):
        par[4 + h, PC_SEL + h] = 1.0
    par[:8, PC_BIF] = np.asarray(inp["b_b_if"], f)[0]
    tt = np.arange(512)
    par[:4, PC_RMASK:PC_RMASK + 512] = (tt % 64 != 0).astype(f)[None]
    par[:4, PC_AMASK:PC_AMASK + 512] = np.where(tt % 64 == 0, -1e30, 0.0).astype(f)[None]
    ts = np.arange(128)
    par[:4, PC_RMASK_S:PC_RMASK_S + 128] = (ts % 8 != 0).astype(f)[None]
    par[:4, PC_AMASK_S:PC_AMASK_S + 128] = np.where(ts % 8 == 0, -1e30, 0.0).astype(f)[None]

    cbf = np.zeros((128, NCB), f)
    cbf[:, CB_IDENT:CB_IDENT + 128] = np.eye(128, dtype=f)
    cbf[:, CB_ONES:CB_ONES + 8] = 1.0
    wif = np.asarray(inp["b_w_if"], f)[0]
    cbf[:, CB_WIF:CB_WIF + 384] = wif.reshape(48, 128, 8).transpose(1, 0, 2).reshape(128, 384)

    nw = np.asarray(inp["norm_w"], f)
    bc = np.stack([np.broadcast_to(nw[0], (128, DM)), np.broadcast_to(nw[1], (128, DM)),
                   np.broadcast_to(np.asarray(inp["final_norm_w"], f), (128, DM))], 0)
    bc = np.ascontiguousarray(bc)
    return wst, par, cbf, bc


_CACHE = {}


def kernel(**inp):
    f = np.float32
    wst, par, cbf, bc = _host_prep(inp)
    if "nc" not in _CACHE:
        _CACHE["nc"] = build_program()
    nc = _CACHE["nc"]
    xp = np.asarray(inp["x_prompt"], f)
    xs = np.asarray(inp["x_sample"], f)
    sca = np.asarray(inp["state_conv_a"], f)[0]
    scb = np.asarray(inp["state_conv_b"], f)[0]
    sC = np.asarray(inp["state_C"], f)[0]
    sn = np.asarray(inp["state_n"], f)[0]
    sm = np.asarray(inp["state_m"], f)[0]
    in_maps = []
    for c in range(NCORES):
        s0, s1 = 16 * c, 16 * c + 16
        in_maps.append({
            "xp": np.ascontiguousarray(xp[c]),
            "xs": np.ascontiguousarray(xs[s0:s1].reshape(128, DM)),
            "sca": np.ascontiguousarray(sca[s0:s1].reshape(32, AW)),
            "scb": np.ascontiguousarray(scb[s0:s1].reshape(48, AW)),
            "sC": np.ascontiguousarray(sC[s0:s1].reshape(64, 512, 512)),
            "sn": np.ascontiguousarray(sn[s0:s1].reshape(256, 128)),
            "sm": np.ascontiguousarray(sm[s0:s1].T),
            "wst": wst, "par": par, "cbf": cbf, "bc": bc,
        })
    res = run_bass_kernel_spmd(nc, in_maps, core_ids=list(range(NCORES)))
    R = res.results
    def cat(name, shp):
        return np.stack([np.asarray(R[c][name], f).reshape(shp) for c in range(NCORES)], 0)
    y_p = cat("yp", (2048, DM))
    y_s = cat("ys", (16, 8, DM)).reshape(128, 8, DM)
    ca_p = cat("cap", (2, AW))[None]
    ca_s = cat("cas", (16, 2, AW)).reshape(128, 2, AW)[None]
    cb_p = cat("cbp", (3, AW))[None]
    cb_s = cat("cbs", (16, 3, AW)).reshape(128, 3, AW)[None]
    C_p = cat("Cp", (4, 512, 512))[None]
    C_s = cat("Cs", (16, 4, 512, 512)).reshape(128, 4, 512, 512)[None]
    n_p = cat("np", (4, 512))[None]
    n_s = cat("ns", (16, 4, 512)).reshape(128, 4, 512)[None]
    m_p = cat("mp", (4,))[None]
    m_s = np.stack([np.asarray(R[c]["ms"], f).reshape(4, 16).T for c in range(NCORES)], 0).reshape(128, 4)[None]
    return (y_p, y_s, ca_p, ca_s, cb_p, cb_s, C_p, C_s, n_p, n_s, m_p, m_s)
```

```python
import math
import contextlib
import numpy as np
import concourse.bass as bass
import concourse.mybir as mybir
from concourse.bass_utils import run_bass_kernel_spmd

F32 = mybir.dt.float32
BF16 = mybir.dt.bfloat16
AF = mybir.ActivationFunctionType
ALU = mybir.AluOpType

NCORES = 8
DM = 1024
AW = 2048
NH = 4
DK = 512
RMS_EPS = 1e-6
LN_EPS = 1e-5
KSCALE = DK ** -0.5
NITEMS = 112
NSLOTS = 4
ITEM = 2048

PC_CONVA, PC_CONVB, PC_CB, PC_SKIP, PC_ONORM = 0, 48, 112, 128, 144
PC_IDENT = 160
PC_NEG = 288
PC_EH = 352
PC_EYE4 = 608
PC_ONES = 612
PC_SEL = 740
PC_BIF = 744
PC_RMASK = 745
PC_AMASK = 1257
PC_RMASK_S = 1769
PC_AMASK_S = 1897
NPAR = 2025
CB_IDENT, CB_ONES, CB_WIF = 0, 128, 136
NCB = 520


class Op:
    __slots__ = ("eng", "fn", "deps", "needs_inc", "val", "is_dma", "sem", "dma_val")

    def __init__(self, eng, fn, is_dma=False):
        self.eng = eng
        self.fn = fn
        self.deps = []
        self.needs_inc = False
        self.val = None
        self.is_dma = is_dma
        self.sem = None
        self.dma_val = None


class Prog:
    ENGS = ("pe", "act", "dve", "pool", "sp")

    def __init__(self, nc, n_dma_sems=32):
        self.nc = nc
        self.eng = {"pe": nc.tensor, "act": nc.scalar, "dve": nc.vector, "pool": nc.gpsimd, "sp": nc.sync}
        self.ops = []
        self.last_w = {}
        self.readers = {}
        self.n_dma_sems = n_dma_sems
        self.dma_rr = 0
        self.dma_rr_sw = 0
        self.dma_sem_last = [None] * n_dma_sems
        self.dma_sem_count = [0] * n_dma_sems
        self.out_dmas = []

    def _add_dep(self, op, d):
        if d is None or d is op:
            return
        if d.eng == op.eng and op.eng == "pe" and not d.is_dma and not op.is_dma:
            return
        op.deps.append(d)
        if not d.is_dma:
            d.needs_inc = True

    def op(self, eng, fn, reads=(), writes=(), is_dma=False, is_out=False):
        o = Op(eng, fn, is_dma)
        for t in reads:
            self._add_dep(o, self.last_w.get(t))
        for t in writes:
            self._add_dep(o, self.last_w.get(t))
            for r in self.readers.get(t, ()):
                if r.eng == eng and not r.is_dma and not is_dma:
                    continue
                self._add_dep(o, r)
        for t in reads:
            self.readers.setdefault(t, []).append(o)
        for t in writes:
            self.last_w[t] = o
            self.readers[t] = []
        if is_dma:
            if eng == "pool":
                s = self.dma_rr_sw
                self.dma_rr_sw = (self.dma_rr_sw + 1) % 8
            else:
                s = 8 + self.dma_rr
                self.dma_rr = (self.dma_rr + 1) % (self.n_dma_sems - 8)
            prev = self.dma_sem_last[s]
            if prev is not None:
                o.deps.append(prev)
            self.dma_sem_count[s] += 16
            o.sem = s
            o.dma_val = self.dma_sem_count[s]
            self.dma_sem_last[s] = o
            if is_out:
                self.out_dmas.append(o)
        self.ops.append(o)
        return o

    def dma(self, eng, out, in_, reads=(), writes=(), is_out=False):
        return self.op(eng, lambda e: e.dma_start(out=out, in_=in_), reads, writes, is_dma=True, is_out=is_out)

    def emit(self, sems, dma_sems):
        cnt = {e: 0 for e in self.ENGS}
        for o in self.ops:
            if o.needs_inc and not o.is_dma:
                cnt[o.eng] += 1
                o.val = cnt[o.eng]
        waited = {e: {} for e in self.ENGS}
        import os
        maxops = int(os.environ.get("MK_MAXOPS", "0"))
        if maxops:
            self.ops = self.ops[:maxops]
            self.out_dmas = [o for o in self.out_dmas if o in set(self.ops)]
        for o in self.ops:
            e = self.eng[o.eng]
            w = waited[o.eng]
            need = {}
            for d in o.deps:
                if d.is_dma:
                    key, v = ("d", d.sem), d.dma_val
                else:
                    key, v = ("e", d.eng), d.val
                if need.get(key, 0) < v:
                    need[key] = v
            for key, v in need.items():
                if w.get(key, 0) >= v:
                    continue
                w[key] = v
                e.wait_ge(dma_sems[key[1]] if key[0] == "d" else sems[key[1]], v)
            inst = o.fn(e)
            if o.is_dma:
                inst.then_inc(dma_sems[o.sem], 16)
            elif o.needs_inc:
                inst.then_inc(sems[o.eng], 1)
        e = self.eng["sp"]
        fin = {}
        for o in self.out_dmas:
            fin[o.sem] = max(fin.get(o.sem, 0), o.dma_val)
        for s, v in fin.items():
            e.wait_ge(dma_sems[s], v)


def build_program(n_ptiles=4, do_sample=True, dbg=False):
    nc = bass.Bass("TRN2", target_bir_lowering=False)

    def din(name, shape):
        return nc.dram_tensor(name, shape, F32, kind="ExternalInput").ap()

    def dout(name, shape):
        return nc.dram_tensor(name, shape, F32, kind="ExternalOutput").ap()

    xp = din("xp", [2048, DM])
    xs = din("xs", [128, DM])
    sca = din("sca", [32, AW])
    scb = din("scb", [48, AW])
    sC = din("sC", [64, 512, 512])
    sn = din("sn", [256, 128])
    sm = din("sm", [4, 16])
    wst = din("wst", [NITEMS, 128, ITEM])
    par_d = din("par", [128, NPAR])
    cbf_d = din("cbf", [128, NCB])
    bc_d = din("bc", [3, 128, DM])

    yp = dout("yp", [2048, DM])
    ys = dout("ys", [128, DM])
    cap = dout("cap", [2, AW])
    cas = dout("cas", [32, AW])
    cbp = dout("cbp", [3, AW])
    cbs = dout("cbs", [48, AW])
    Cp = dout("Cp", [4, 512, 512])
    Cs = dout("Cs", [64, 512, 512])
    np_o = dout("np", [16, 128])
    ns_o = dout("ns", [256, 128])
    mp_o = dout("mp", [4, 1])
    ms_o = dout("ms", [4, 16])

    es = contextlib.ExitStack()
    with es:
        AR_WORDS = 52600
        arena = es.enter_context(nc.sbuf_tensor("arena", [128, AR_WORDS], F32))
        psb = [es.enter_context(nc.psum_tensor(f"psb{i}", [128, 512], F32)) for i in range(8)]
        sems = {e: es.enter_context(nc.semaphore(f"sem_{e}")) for e in Prog.ENGS}
        dsems = [es.enter_context(nc.semaphore(f"dsem{i}")) for i in range(32)]
        p = Prog(nc)
        A = p.op

        cur = [0]
        PAGE = 32

        class V:
            __slots__ = ("ap", "tok")

            def __init__(self, ap, tok):
                self.ap = ap
                self.tok = tok

        def alloc(words):
            words = (words + PAGE - 1) // PAGE * PAGE
            o = cur[0]
            cur[0] += words
            assert cur[0] <= AR_WORDS, f"arena overflow {cur[0]}"
            return o

        def view(off, words, dtype=F32, parts=128):
            ap = arena[:parts, off:off + words]
            if dtype != F32:
                ap = ap.bitcast(dtype)
            toks = [("a", pg) for pg in range(off // PAGE, (off + words - 1) // PAGE + 1)]
            return V(ap, toks)

        o_w = alloc(NSLOTS * 1024)
        wslot = [view(o_w + i * 1024, 1024, BF16) for i in range(NSLOTS)]
        o_x = alloc(4096)
        Xv = view(o_x, 4096)
        o_ht = alloc(2048)
        HTv = view(o_ht, 2048, BF16)
        o_B = [alloc(4096) for _ in range(5)]
        o_cst = alloc(4 * 2048)
        CSTv = [view(o_cst + i * 2048, 2048) for i in range(4)]
        o_cbf = alloc(2 * 1024)
        CBFv = [view(o_cbf + i * 1024, 1024, BF16) for i in range(2)]
        o_par = alloc(NPAR)
        PARv = view(o_par, NPAR)
        PAR = PARv.ap
        o_cb = alloc(NCB // 2)
        CBv = view(o_cb, NCB // 2, BF16)
        CB = CBv.ap
        o_hist = alloc(416)
        HISTAv = view(o_hist, 32)
        HISTBv = view(o_hist + 32, 48)
        MALLv = view(o_hist + 96, 24, parts=4)
        NSTh = [view(o_hist + 128 + h * 32, 4) for h in range(4)]
        NBFh = [view(o_hist + 256 + h * 32, 2, BF16) for h in range(4)]
        NSTC = view(o_hist + 384, 16)
        o_g = alloc(512 * 2 + 192 + 64)
        CCv = view(o_g, 512, parts=4)
        NEGAv = view(o_g + 512, 512, parts=4)
        COLQv = view(o_g + 1024, 192, parts=64)
        DECBv = view(o_g + 1216, 64)
        o_scr = cur[0]
        SCR_WORDS = AR_WORDS - o_scr

        def scr(off, words, dtype=F32, parts=128):
            assert off + words <= SCR_WORDS, f"scratch overflow {off + words} > {SCR_WORDS}"
            return view(o_scr + off, words, dtype, parts)

        def dump(name, v, dtype):
            if not dbg:
                return
            shp = list(v.ap.shape)
            d = nc.dram_tensor(name, shp, dtype, kind="ExternalOutput").ap()
            p.dma("sp", d, v.ap, reads=v.tok, is_out=True)

        psrr = [0]

        def newps():
            i = psrr[0]
            psrr[0] = (i + 1) % 8
            return i

        def PT(i):
            return [("ps", i)]

        def psf(i):
            return psb[i][:]

        def psh(i):
            return psb[i][:].bitcast(BF16)

        wg = [0]
        wissued = [0]
        total_items = NITEMS * (n_ptiles + (1 if do_sample else 0))

        wbf = nc.dram_tensor("wbf_cache", [NITEMS, 128, ITEM], BF16, kind=("ExternalOutput" if dbg else "Internal")).ap()
        use_cache = total_items > NITEMS

        def w_issue_upto(g):
            while wissued[0] <= g and wissued[0] < total_items:
                gi = wissued[0]
                s = gi % NSLOTS
                if gi < NITEMS or not use_cache:
                    p.dma("pool", wslot[s].ap, wst[gi % NITEMS], writes=wslot[s].tok)
                else:
                    wq = "sp" if gi >= NITEMS * n_ptiles else "pool"
                    p.dma(wq, wslot[s].ap, wbf[gi % NITEMS], reads=[("wd", gi % NITEMS)], writes=wslot[s].tok)
                wissued[0] += 1

        def wget():
            g = wg[0]
            wg[0] += 1
            w_issue_upto(g + NSLOTS - 1)
            if use_cache and g < NITEMS:
                p.dma("sp", wbf[g], wslot[g % NSLOTS].ap, reads=wslot[g % NSLOTS].tok, writes=[("wd", g)])
            return wslot[g % NSLOTS]

        p.dma("sp", PAR, par_d, writes=PARv.tok)
        p.dma("pool", CB, cbf_d, writes=CBv.tok)
        ident_bf = CB[:, CB_IDENT:CB_IDENT + 128]
        ones_bf = CB[:, CB_ONES:CB_ONES + 8]
        wif_bf = CB[:, CB_WIF:CB_WIF + 384].rearrange("p (k g) -> p k g", g=8)
        ident_f = PAR[:, PC_IDENT:PC_IDENT + 128]
        CT = PARv.tok
        CBT = CBv.tok

        for h in range(4):
            A("pool", lambda e, h=h: e.memset(CSTv[h].ap, 0.0), writes=CSTv[h].tok)
        hist_all = view(o_hist, 416)
        A("pool", lambda e: e.memset(hist_all.ap, 0.0), writes=hist_all.tok)

        def run_tile(kind, ti):
            samp = kind == "s"
            T = 128 if samp else 512
            NB = T // 128
            L = 8 if samp else 64
            NCH = T // L
            HBA, HBB = 2, 3
            last_p = (not samp) and ti == n_ptiles - 1

            def Bview(i, words=None, dtype=BF16, off=0):
                return view(o_B[i] + off, words if words is not None else 16 * T // 2, dtype)

            def ctok(i, fc):
                cw = T // 2
                o0 = o_B[i] + fc * cw
                return [("a", pg) for pg in range(o0 // PAGE, (o0 + cw - 1) // PAGE + 1)]

            def fm(v):
                return v.ap.rearrange("p (c t) -> p c t", t=T)

            B1, B2, B3, B4, B5 = [Bview(i) for i in range(5)]
            HT = view(o_ht, 8 * T // 2, BF16)
            hT = HT.ap.rearrange("p (c t) -> p c t", t=T)
            X = view(o_x, NB * 1024)
            Xt = X.ap.rearrange("p (b f) -> p b f", f=DM)

            s_junk = scr(0, 512, BF16)
            s_hb = [scr(512, 512, BF16), scr(1024, 512, BF16)]
            s_ss = [scr(1536, 4), scr(1568, 4), scr(6784, 4), scr(6816, 4)]
            s_nw = scr(1600, 1024)
            s_xe = [scr(2624, 520), scr(3168, 520)]
            s_xa = [scr(3712, 512), scr(4224, 512)]
            s_sz = [scr(4736, 512), scr(5248, 512)]
            s_tt = scr(5760, 512)
            s_acc = scr(6272, 512)
            s_ost = [scr(2624, 1024), scr(3648, 1024)]

            def xtok(b):
                o0 = o_x + b * 1024
                return [("a", pg) for pg in range(o0 // PAGE, (o0 + 1023) // PAGE + 1)]

            if samp:
                p.dma("sp", Xt, xs.rearrange("(b p) f -> p b f", p=128), writes=X.tok)
            else:
                for b_ in range(NB):
                    r0_ = ti * 512 + b_ * 128
                    p.dma("sp", Xt[:, b_, :], xp[r0_:r0_ + 128, :], writes=xtok(b_))

            def load_nw(i):
                p.dma("sp", s_nw.ap, bc_d[i], writes=s_nw.tok)

            def rms_stats(b, k):
                ss = s_ss[k]
                A("act", lambda e: e.activation(out=s_junk.ap, in_=Xt[:, b, :], func=AF.Square, accum_out=ss.ap[:, 0:1]),
                  reads=xtok(b), writes=s_junk.tok + ss.tok)
                A("act", lambda e: e.activation(out=ss.ap[:, 1:2], in_=ss.ap[:, 0:1], func=AF.Ln, scale=1.0 / DM, bias=RMS_EPS),
                  reads=ss.tok, writes=ss.tok)
                A("act", lambda e: e.activation(out=ss.ap[:, 2:3], in_=ss.ap[:, 1:2], func=AF.Exp, scale=-0.5),
                  reads=ss.tok, writes=ss.tok)

            def norm_to_hT():
                for b in range(NB):
                    rms_stats(b, b)
                for b in range(NB):
                    k = b % 2
                    hb = s_hb[k]
                    A("dve", lambda e, b=b, k=k, hb=hb: e.scalar_tensor_tensor(
                        out=hb.ap, in0=Xt[:, b, :], scalar=s_ss[b].ap[:, 2:3], in1=s_nw.ap, op0=ALU.mult, op1=ALU.mult),
                      reads=xtok(b) + s_ss[b].tok + s_nw.tok, writes=hb.tok)
                    pi = newps()
                    for c in range(8):
                        A("pe", lambda e, c=c, pi=pi, hb=hb: e.transpose(
                            out=psh(pi)[:, c * 128:(c + 1) * 128], in_=hb.ap[:, c * 128:(c + 1) * 128], identity=ident_bf),
                          reads=hb.tok + CBT, writes=PT(pi))
                    A("act", lambda e, b=b, pi=pi: e.copy(
                        out=hT[:, :, b * 128:(b + 1) * 128], in_=psh(pi)[:, 0:1024].rearrange("p (c t) -> p c t", t=128)),
                      reads=PT(pi), writes=HT.tok)

            print("mark L0 start", len(p.ops))
            load_nw(0)
            norm_to_hT()
            if ti == 0 and not samp:
                dump("d_hT", HT, BF16)

            yT = fm(B5)
            if samp:
                s_rows = Bview(1, 2048, F32, off=1024)
                HAS = Bview(0, 512, F32, off=1024)
                hist_s = HAS.ap.rearrange("p (c r) -> p c r", r=32)
                p.dma("sp", s_rows.ap[:32, :], sca, writes=s_rows.tok)
                for g in range(4):
                    pi = newps()
                    for c4 in range(4):
                        c = g * 4 + c4
                        A("pe", lambda e, c=c, c4=c4, pi=pi: e.transpose(
                            out=psf(pi)[:, c4 * 32:(c4 + 1) * 32], in_=s_rows.ap[:32, c * 128:(c + 1) * 128], identity=ident_f[:32, :32]),
                          reads=s_rows.tok + CT, writes=PT(pi))
                    A("dve", lambda e, g=g, pi=pi: e.tensor_copy(
                        out=hist_s[:, g * 4:(g + 1) * 4, :], in_=psf(pi)[:, 0:128].rearrange("p (c r) -> p c r", r=32)),
                      reads=PT(pi), writes=HAS.tok)
                NHA = view(o_B[0] + 1024 + 512, 512, F32)
                nhist_s = NHA.ap.rearrange("p (c r) -> p c r", r=32)

            for j in range(16):
                pis = [newps() for _ in range(4)]
                for bi in range(4):
                    bl = bi % 2
                    if bl == 0:
                        wt = wget()
                        wv = wt.ap.rearrange("p (b k n) -> p b k n", b=2, k=8)
                    for k in range(8):
                        A("pe", lambda e, wv=wv, bl=bl, k=k, pi=pis[bi]: e.matmul(
                            psf(pi)[:, :T], lhsT=wv[:, bl, k, :], rhs=hT[:, k, :], start=(k == 0), stop=(k == 7)),
                          reads=wt.tok + HT.tok, writes=PT(pis[bi]))
                pb, pc, pxa, pz = pis
                k2 = j % 2
                xe, xa, sz = s_xe[k2], s_xa[k2], s_sz[k2]
                A("act", lambda e, xa=xa, pxa=pxa: e.copy(out=xa.ap[:, :T], in_=psf(pxa)[:, :T]), reads=PT(pxa), writes=xa.tok)
                A("act", lambda e, sz=sz, pz=pz: e.activation(out=sz.ap[:, :T], in_=psf(pz)[:, :T], func=AF.Silu),
                  reads=PT(pz), writes=sz.tok)
                if ti == 0 and not samp and j == 0:
                    dump("d_xa0", xa, F32)
                    dump("d_sz0", sz, F32)
                if samp:
                    xe3 = xe.ap[:, 0:160].rearrange("p (s t) -> p s t", t=10)
                    A("dve", lambda e, xe3=xe3, j=j: e.tensor_copy(
                        out=xe3[:, :, 0:2], in_=hist_s[:, j, :].rearrange("p (s r) -> p s r", r=2)),
                      reads=HAS.tok, writes=xe.tok)
                    A("dve", lambda e, xe3=xe3, xa=xa, pc=pc: e.tensor_tensor(
                        out=xe3[:, :, 2:10], in0=psf(pc)[:, :128].rearrange("p (s t) -> p s t", t=8),
                        in1=xa.ap[:, :128].rearrange("p (s t) -> p s t", t=8), op=ALU.mult),
                      reads=PT(pc) + xa.tok, writes=xe.tok)
                    acc3 = s_acc.ap[:, :128].rearrange("p (s t) -> p s t", t=8)
                    win = [xe3[:, :, d:d + 8] for d in range(3)]
                    accv = acc3
                else:
                    A("dve", lambda e, xe=xe, j=j: e.tensor_copy(out=xe.ap[:, 0:2], in_=HISTAv.ap[:, j * 2:j * 2 + 2]),
                      reads=HISTAv.tok, writes=xe.tok)
                    A("dve", lambda e, xe=xe, xa=xa, pc=pc: e.tensor_tensor(
                        out=xe.ap[:, 2:2 + T], in0=psf(pc)[:, :T], in1=xa.ap[:, :T], op=ALU.mult),
                      reads=PT(pc) + xa.tok, writes=xe.tok)
                    win = [xe.ap[:, d:d + T] for d in range(3)]
                    accv = s_acc.ap[:, :T]
                A("dve", lambda e, sz=sz, pb=pb: e.tensor_tensor(out=s_tt.ap[:, :T], in0=psf(pb)[:, :T], in1=sz.ap[:, :T], op=ALU.mult),
                  reads=PT(pb) + sz.tok, writes=s_tt.tok)
                cw = [PAR[:, PC_CONVA + d * 16 + j:PC_CONVA + d * 16 + j + 1] for d in range(3)]
                A("dve", lambda e, accv=accv, win=win, cw=cw: e.tensor_scalar(
                    out=accv, in0=win[0], scalar1=cw[0], scalar2=None, op0=ALU.mult),
                  reads=xe.tok + CT, writes=s_acc.tok)
                for d in (1, 2):
                    A("dve", lambda e, accv=accv, win=win, cw=cw, d=d: e.scalar_tensor_tensor(
                        out=accv, in0=win[d], scalar=cw[d], in1=accv, op0=ALU.mult, op1=ALU.add),
                      reads=xe.tok + CT + s_acc.tok, writes=s_acc.tok)
                A("dve", lambda e, j=j: e.tensor_tensor(out=yT[:, j, :], in0=s_tt.ap[:, :T], in1=s_acc.ap[:, :T], op=ALU.mult),
                  reads=s_tt.tok + s_acc.tok, writes=ctok(4, j))
                if samp:
                    A("act", lambda e, xe3=xe3, j=j: e.copy(
                        out=nhist_s[:, j, :].rearrange("p (s r) -> p s r", r=2), in_=xe3[:, :, 8:10]),
                      reads=xe.tok, writes=NHA.tok)
                else:
                    A("act", lambda e, xe=xe, j=j: e.copy(out=HISTAv.ap[:, j * 2:j * 2 + 2], in_=xe.ap[:, T:T + 2]),
                      reads=xe.tok, writes=HISTAv.tok)

            if samp or last_p:
                nr = 32 if samp else 2
                src3 = nhist_s if samp else HISTAv.ap.rearrange("p (c r) -> p c r", r=2)
                srct = NHA.tok if samp else HISTAv.tok
                stg = scr(0, 2048, F32)
                for g in range(4):
                    pi = newps()
                    for c4 in range(4):
                        c = g * 4 + c4
                        A("pe", lambda e, c=c, c4=c4, pi=pi, src3=src3, nr=nr: e.matmul(
                            psf(pi)[:nr, c4 * 128:(c4 + 1) * 128], lhsT=src3[:, c, :], rhs=ident_f, start=True, stop=True),
                          reads=srct + CT, writes=PT(pi))
                    A("act", lambda e, g=g, pi=pi: e.copy(out=stg.ap[:nr, g * 512:(g + 1) * 512], in_=psf(pi)[:nr, :]),
                      reads=PT(pi), writes=stg.tok)
                p.dma("sp", cas if samp else cap, stg.ap[:nr, :], reads=stg.tok, is_out=True)

            def outproj(srcT, srcbuf):
                pss = [[newps(), newps()] for _ in range(NB)]
                for it in range(8):
                    w = wget()
                    wv = w.ap.rearrange("p (k n) -> p k n", k=2)
                    for k2 in range(2):
                        kc = it * 2 + k2
                        for b in range(NB):
                            for hf in range(2):
                                A("pe", lambda e, wv=wv, k2=k2, kc=kc, b=b, hf=hf, pi=pss[b][hf]: e.matmul(
                                    psf(pi)[:, :512], lhsT=srcT[:, kc, b * 128:(b + 1) * 128],
                                    rhs=wv[:, k2, hf * 512:(hf + 1) * 512], start=(kc == 0), stop=(kc == 15)),
                                  reads=w.tok + ctok(srcbuf, kc), writes=PT(pss[b][hf]))
                for b in range(NB):
                    for hf in range(2):
                        A("dve", lambda e, b=b, hf=hf, pi=pss[b][hf]: e.tensor_tensor(
                            out=Xt[:, b, hf * 512:(hf + 1) * 512], in0=psf(pi)[:, :512],
                            in1=Xt[:, b, hf * 512:(hf + 1) * 512], op=ALU.add),
                          reads=PT(pss[b][hf]) + X.tok, writes=X.tok)

            print("mark L0 outproj", len(p.ops))
            if ti == 0 and not samp:
                dump("d_yT", B5, BF16)
            outproj(yT, 4)
            if ti == 0 and not samp:
                dump("d_x1", X, F32)
            print("mark L1 start", len(p.ops))

            load_nw(1)
            norm_to_hT()
            xmT, xcT, vT, qT, kT = fm(B1), fm(B2), fm(B3), fm(B4), fm(B5)
            ogT, szT = xmT, qT

            if samp:
                s_rows_b = Bview(2, 2048, F32, off=1024)
                HBS = view(o_B[0] + 2048, 768, F32)
                NHB = view(o_B[0] + 2816, 768, F32)
                histb_s = HBS.ap.rearrange("p (c r) -> p c r", r=48)
                nhistb_s = NHB.ap.rearrange("p (c r) -> p c r", r=48)
                A("pool", lambda e: e.memset(s_rows_b.ap[:64, :], 0.0), writes=s_rows_b.tok)
                p.dma("sp", s_rows_b.ap[:48, :], scb, writes=s_rows_b.tok)
                for g in range(2):
                    pi = newps()
                    for c8 in range(8):
                        c = g * 8 + c8
                        A("pe", lambda e, c=c, c8=c8, pi=pi: e.transpose(
                            out=psf(pi)[:, c8 * 64:(c8 + 1) * 64], in_=s_rows_b.ap[:64, c * 128:(c + 1) * 128], identity=ident_f[:64, :64]),
                          reads=s_rows_b.tok + CT, writes=PT(pi))
                    A("dve", lambda e, g=g, pi=pi: e.tensor_copy(
                        out=histb_s[:, g * 8:(g + 1) * 8, :], in_=psf(pi)[:, 0:512].rearrange("p (c r) -> p c r", r=64)[:, :, 0:48]),
                      reads=PT(pi), writes=HBS.tok)

            for it in range(8):
                w = wget()
                wv = w.ap.rearrange("p (o k n) -> p o k n", o=2, k=8)
                for o2 in range(2):
                    oc = it * 2 + o2
                    pi = newps()
                    for k in range(8):
                        A("pe", lambda e, wv=wv, o2=o2, k=k, pi=pi: e.matmul(
                            psf(pi)[:, :T], lhsT=wv[:, o2, k, :], rhs=hT[:, k, :], start=(k == 0), stop=(k == 7)),
                          reads=w.tok + HT.tok, writes=PT(pi))
                    xe = s_xe[oc % 2]
                    if samp:
                        xe3 = xe.ap[:, 0:176].rearrange("p (s t) -> p s t", t=11)
                        A("dve", lambda e, xe3=xe3, oc=oc: e.tensor_copy(
                            out=xe3[:, :, 0:3], in_=histb_s[:, oc, :].rearrange("p (s r) -> p s r", r=3)),
                          reads=HBS.tok, writes=xe.tok)
                        A("act", lambda e, xe3=xe3, pi=pi: e.copy(
                            out=xe3[:, :, 3:11], in_=psf(pi)[:, :128].rearrange("p (s t) -> p s t", t=8)),
                          reads=PT(pi), writes=xe.tok)
                        A("act", lambda e, oc=oc, pi=pi: e.copy(out=xmT[:, oc, :], in_=psf(pi)[:, :T]),
                          reads=PT(pi), writes=B1.tok)
                        win = [xe3[:, :, d:d + 8] for d in range(4)]
                        accv = s_acc.ap[:, :128].rearrange("p (s t) -> p s t", t=8)
                    else:
                        A("dve", lambda e, xe=xe, oc=oc: e.tensor_copy(out=xe.ap[:, 0:3], in_=HISTBv.ap[:, oc * 3:oc * 3 + 3]),
                          reads=HISTBv.tok, writes=xe.tok)
                        A("act", lambda e, xe=xe, pi=pi: e.copy(out=xe.ap[:, 3:3 + T], in_=psf(pi)[:, :T]),
                          reads=PT(pi), writes=xe.tok)
                        A("act", lambda e, oc=oc, pi=pi: e.copy(out=xmT[:, oc, :], in_=psf(pi)[:, :T]),
                          reads=PT(pi), writes=B1.tok)
                        win = [xe.ap[:, d:d + T] for d in range(4)]
                        accv = s_acc.ap[:, :T]
                    cw = [PAR[:, PC_CONVB + d * 16 + oc:PC_CONVB + d * 16 + oc + 1] for d in range(4)]
                    A("act", lambda e, accv=accv, win=win, cw=cw: e.activation(out=accv, in_=win[0], func=AF.Copy, scale=cw[0]),
                      reads=xe.tok + CT, writes=s_acc.tok)
                    for d in (1, 2, 3):
                        A("dve", lambda e, accv=accv, win=win, cw=cw, d=d: e.scalar_tensor_tensor(
                            out=accv, in0=win[d], scalar=cw[d], in1=accv, op0=ALU.mult, op1=ALU.add),
                          reads=xe.tok + CT + s_acc.tok, writes=s_acc.tok)
                    A("act", lambda e, oc=oc: e.activation(out=xcT[:, oc, :], in_=s_acc.ap[:, :T], func=AF.Silu,
                                                           bias=PAR[:, PC_CB + oc:PC_CB + oc + 1]),
                      reads=s_acc.tok + CT, writes=B2.tok)
                    if samp:
                        A("act", lambda e, xe3=xe3, oc=oc: e.copy(
                            out=nhistb_s[:, oc, :].rearrange("p (s r) -> p s r", r=3), in_=xe3[:, :, 8:11]),
                          reads=xe.tok, writes=NHB.tok)
                    else:
                        A("act", lambda e, xe=xe, oc=oc: e.copy(out=HISTBv.ap[:, oc * 3:oc * 3 + 3], in_=xe.ap[:, T:T + 3]),
                          reads=xe.tok, writes=HISTBv.tok)

            if samp or last_p:
                nr = 48 if samp else 3
                src3 = nhistb_s if samp else HISTBv.ap.rearrange("p (c r) -> p c r", r=3)
                srct = NHB.tok if samp else HISTBv.tok
                stg = scr(0, 2048, F32)
                for g in range(4):
                    pi = newps()
                    for c4 in range(4):
                        c = g * 4 + c4
                        A("pe", lambda e, c=c, c4=c4, pi=pi, src3=src3, nr=nr: e.matmul(
                            psf(pi)[:nr, c4 * 128:(c4 + 1) * 128], lhsT=src3[:, c, :], rhs=ident_f, start=True, stop=True),
                          reads=srct + CT, writes=PT(pi))
                    A("act", lambda e, g=g, pi=pi: e.copy(out=stg.ap[:nr, g * 512:(g + 1) * 512], in_=psf(pi)[:nr, :]),
                      reads=PT(pi), writes=stg.tok)
                p.dma("sp", cbs if samp else cbp, stg.ap[:nr, :], reads=stg.tok, is_out=True)

            print("mark vqk", len(p.ops))
            cnt_ev = [0]
            for (dst, dv_, src, srct) in [(vT, B3, xmT, B1.tok), (qT, B4, xcT, B2.tok), (kT, B5, xcT, B2.tok)]:
                for oc in range(16):
                    w = wget()
                    wv = w.ap.rearrange("p (k n) -> p k n", k=16)
                    pi = newps()
                    for k in range(16):
                        A("pe", lambda e, wv=wv, k=k, pi=pi, src=src: e.matmul(
                            psf(pi)[:, :T], lhsT=wv[:, k, :], rhs=src[:, k, :], start=(k == 0), stop=(k == 15)),
                          reads=w.tok + srct, writes=PT(pi))
                    if cnt_ev[0] % 2 == 0:
                        A("act", lambda e, dst=dst, oc=oc, pi=pi: e.copy(out=dst[:, oc, :], in_=psf(pi)[:, :T]),
                          reads=PT(pi), writes=dv_.tok)
                    else:
                        A("dve", lambda e, dst=dst, oc=oc, pi=pi: e.tensor_copy(out=dst[:, oc, :], in_=psf(pi)[:, :T]),
                          reads=PT(pi), writes=dv_.tok)
                    cnt_ev[0] += 1

            print("mark gates", len(p.ops))
            pg = newps()
            gsrc = [(qT, B4.tok)] * 16 + [(kT, B5.tok)] * 16 + [(vT, B3.tok)] * 16
            for kc in range(48):
                A("pe", lambda e, kc=kc, pg=pg: e.matmul(
                    psf(pg)[:8, :T], lhsT=wif_bf[:, kc, :], rhs=gsrc[kc][0][:, kc % 16, :], start=(kc == 0), stop=(kc == 47)),
                  reads=CBT + gsrc[kc][1], writes=PT(pg))
            GSB = scr(3072, 512, parts=8)
            A("act", lambda e: e.activation(out=GSB.ap[:8, :T], in_=psf(pg)[:8, :T], func=AF.Identity,
                                            bias=PAR[:8, PC_BIF:PC_BIF + 1]),
              reads=PT(pg) + CT, writes=GSB.tok)
            pf = newps()
            A("pe", lambda e: e.matmul(psf(pf)[:4, :T], lhsT=PAR[:8, PC_SEL:PC_SEL + 4], rhs=GSB.ap[:8, :T], start=True, stop=True),
              reads=GSB.tok + CT, writes=PT(pf))
            G = [scr(i * 512, 512, parts=4) for i in range(6)]
            g_ = [g.ap[:4, :T] for g in G]
            CC = CCv.ap[:4, :T]
            NEGA = NEGAv.ap[:4, :T]
            rmask = PAR[:4, (PC_RMASK_S if samp else PC_RMASK):(PC_RMASK_S if samp else PC_RMASK) + T]
            amask = PAR[:4, (PC_AMASK_S if samp else PC_AMASK):(PC_AMASK_S if samp else PC_AMASK) + T]
            A("act", lambda e: e.copy(out=g_[2], in_=psf(pf)[:4, :T]), reads=PT(pf), writes=G[2].tok)
            A("dve", lambda e: e.scalar_tensor_tensor(out=g_[0], in0=g_[2], scalar=-1.0, in1=g_[2], op0=ALU.mult, op1=ALU.max),
              reads=G[2].tok, writes=G[0].tok)
            A("act", lambda e: e.activation(out=g_[1], in_=g_[0], func=AF.Exp, scale=-1.0), reads=G[0].tok, writes=G[1].tok)
            A("act", lambda e: e.activation(out=g_[1], in_=g_[1], func=AF.Ln, bias=1.0), reads=G[1].tok, writes=G[1].tok)
            A("dve", lambda e: e.tensor_scalar_min(out=g_[0], in0=g_[2], scalar1=0.0), reads=G[2].tok, writes=G[0].tok)
            A("dve", lambda e: e.tensor_sub(out=g_[0], in0=g_[0], in1=g_[1]), reads=G[0].tok + G[1].tok, writes=G[0].tok)
            A("dve", lambda e: e.tensor_tensor_scan(out=g_[5], data0=rmask, data1=g_[0], initial=0.0, op0=ALU.mult, op1=ALU.add),
              reads=G[0].tok + CT, writes=G[5].tok)
            A("dve", lambda e: e.tensor_sub(out=CC, in0=GSB.ap[:4, :T], in1=g_[5]), reads=GSB.tok + G[5].tok, writes=CCv.tok)
            A("dve", lambda e: e.tensor_tensor_scan(out=g_[1], data0=amask, data1=CC, initial=0.0, op0=ALU.add, op1=ALU.max),
              reads=CCv.tok + CT, writes=G[1].tok)
            MALL = MALLv.ap
            MT = scr(3584 + 64, 32, parts=4)
            if samp:
                p.dma("sp", MALL[:4, 0:16], sm, writes=MALLv.tok)
            else:
                for c in range(NCH):
                    le = c * L + L - 1
                    A("dve", lambda e, c=c, le=le: e.tensor_tensor(out=MT.ap[:4, 0:1], in0=g_[1][:, le:le + 1], in1=MALL[:4, c:c + 1], op=ALU.max),
                      reads=G[1].tok + MALLv.tok, writes=MT.tok)
                    A("dve", lambda e, c=c, le=le: e.tensor_tensor(out=MALL[:4, c + 1:c + 2], in0=MT.ap[:4, 0:1], in1=g_[5][:, le:le + 1], op=ALU.add),
                      reads=MT.tok + G[5].tok, writes=MALLv.tok)

            def v3(ap):
                return ap.rearrange("p (c l) -> p c l", l=L)

            mprev_bc = MALL[:4, 0:NCH].unsqueeze(2).to_broadcast([4, NCH, L])
            A("dve", lambda e: e.tensor_tensor(out=v3(g_[2]), in0=v3(g_[1]), in1=mprev_bc, op=ALU.max),
              reads=G[1].tok + MALLv.tok, writes=G[2].tok)
            A("dve", lambda e: e.tensor_scalar(out=NEGA, in0=g_[2], scalar1=-1.0, scalar2=None, op0=ALU.mult),
              reads=G[2].tok, writes=NEGAv.tok)
            A("dve", lambda e: e.tensor_tensor(out=g_[3], in0=g_[5], in1=g_[2], op=ALU.add), reads=G[5].tok + G[2].tok, writes=G[3].tok)
            A("act", lambda e: e.activation(out=g_[3], in_=g_[3], func=AF.Exp, scale=-1.0), reads=G[3].tok, writes=G[3].tok)
            A("dve", lambda e: e.tensor_tensor(out=v3(g_[4]), in0=mprev_bc, in1=v3(g_[2]), op=ALU.subtract),
              reads=G[2].tok + MALLv.tok, writes=G[4].tok)
            A("act", lambda e: e.activation(out=g_[4], in_=g_[4], func=AF.Exp), reads=G[4].tok, writes=G[4].tok)
            alast_bc = v3(g_[2])[:, :, L - 1:L].to_broadcast([4, NCH, L])
            A("dve", lambda e: e.tensor_tensor(out=v3(g_[0]), in0=v3(CC), in1=alast_bc, op=ALU.subtract),
              reads=CCv.tok + G[2].tok, writes=G[0].tok)
            A("act", lambda e: e.activation(out=g_[0], in_=g_[0], func=AF.Exp, bias=math.log(KSCALE)), reads=G[0].tok, writes=G[0].tok)
            if samp:
                A("dve", lambda e: e.tensor_tensor(out=MT.ap[:4, 0:16], in0=v3(g_[5])[:, :, L - 1], in1=v3(g_[2])[:, :, L - 1], op=ALU.add),
                  reads=G[5].tok + G[2].tok, writes=MT.tok)
                p.dma("sp", ms_o, MT.ap[:4, 0:16], reads=MT.tok, is_out=True)
            else:
                if last_p:
                    p.dma("sp", mp_o, MALL[:4, NCH:NCH + 1], reads=MALLv.tok, is_out=True)
            pq = newps()
            for c in range(NCH):
                for qi, (gap_, gtk_) in enumerate(((g_[0], G[0].tok), (g_[4], G[4].tok), (g_[3], G[3].tok), (CC, CCv.tok))):
                    A("pe", lambda e, c=c, qi=qi, gap_=gap_: e.matmul(
                        psf(pq)[:L, (c * 4 + qi) * 4:(c * 4 + qi) * 4 + 4], lhsT=gap_[:, c * L:(c + 1) * L],
                        rhs=PAR[:4, PC_EYE4:PC_EYE4 + 4], start=True, stop=True),
                      reads=gtk_ + CT, writes=PT(pq))
            COLQv = scr(2560, 256, parts=64)
            COLQ = COLQv.ap
            A("act", lambda e: e.copy(out=COLQ[:L, :NCH * 16], in_=psf(pq)[:L, :NCH * 16]), reads=PT(pq), writes=COLQv.tok)
            DECD = scr(3584, 64, parts=4)
            A("dve", lambda e: e.tensor_tensor(
                out=DECD.ap[:4, :NCH * 4].rearrange("p (c h) -> p c h", h=4),
                in0=v3(g_[4])[:, :, L - 1:L].to_broadcast([4, NCH, 4]),
                in1=PAR[:4, PC_EYE4:PC_EYE4 + 4].unsqueeze(1).to_broadcast([4, NCH, 4]), op=ALU.mult),
              reads=G[4].tok + CT, writes=DECD.tok)
            pd = newps()
            A("pe", lambda e: e.matmul(psf(pd)[:, :NCH * 4], lhsT=PAR[:4, PC_ONES:PC_ONES + 128], rhs=DECD.ap[:4, :NCH * 4],
                                       start=True, stop=True),
              reads=DECD.tok + CT, writes=PT(pd))
            DECB = DECBv.ap
            A("act", lambda e: e.copy(out=DECB[:, :NCH * 4], in_=psf(pd)[:, :NCH * 4]), reads=PT(pd), writes=DECBv.tok)
            if not samp:
                A("dve", lambda e: e.tensor_copy(out=MALL[:4, 0:1], in_=MALL[:4, NCH:NCH + 1]), reads=MALLv.tok, writes=MALLv.tok)

            if samp:
                NROWS = Bview(3, 256, F32, off=1024)
                NALL = view(o_B[0] + 3584, 256, F32)
                NBFA = view(o_B[0] + 3840, 128, BF16)
                nrows3 = NROWS.ap.rearrange("p (a d) -> p a d", a=2)
                p.dma("sp", nrows3, sn.rearrange("(a p) d -> p a d", p=128), writes=NROWS.tok)
                pi = newps()
                for a in range(2):
                    A("pe", lambda e, a=a, pi=pi: e.transpose(out=psf(pi)[:, a * 128:(a + 1) * 128], in_=nrows3[:, a, :], identity=ident_f),
                      reads=NROWS.tok + CT, writes=PT(pi))
                A("dve", lambda e, pi=pi: e.tensor_copy(out=NALL.ap, in_=psf(pi)[:, :256]), reads=PT(pi), writes=NALL.tok)
                A("act", lambda e: e.copy(out=NBFA.ap, in_=NALL.ap), reads=NALL.tok, writes=NBFA.tok)

            LMv = scr(512, 4 * T, parts=64)
            for h_ in range(4):
                pl = newps()
                A("pe", lambda e, pl=pl, h_=h_: e.matmul(psf(pl)[:L, :T], lhsT=PAR[:4, PC_EH + h_ * 64:PC_EH + h_ * 64 + L], rhs=NEGA,
                                                        start=True, stop=True),
                  reads=NEGAv.tok + CT, writes=PT(pl))
                A("dve", lambda e, pl=pl, h_=h_: e.tensor_tensor(
                    out=LMv.ap[:L, h_ * T:(h_ + 1) * T].rearrange("p (c l) -> p c l", l=L),
                    in0=psf(pl)[:L, :T].rearrange("p (c l) -> p c l", l=L),
                    in1=PAR[:L, PC_NEG:PC_NEG + L].unsqueeze(1).to_broadcast([L, NCH, L]), op=ALU.add),
                  reads=PT(pl) + CT, writes=LMv.tok)

            print("mark recurrence", len(p.ops))
            s_D = [scr(3712, 64, parts=64), scr(3776, 64, parts=64)]
            s_Sd = [scr(3840, 32, BF16, parts=64), scr(3872, 32, BF16, parts=64)]
            s_kw = [scr(3904, 256, BF16, parts=64), scr(4160, 256, BF16, parts=64)]
            s_vb = [scr(4416, 256, BF16, parts=64), scr(4672, 256, BF16, parts=64)]
            s_t1 = scr(4928, 512, parts=64)
            s_hh = [scr(5440, 512, parts=64), scr(5952, 512, parts=64)]
            s_hn = scr(6464, 1024, BF16, parts=64)
            s_sm = [scr(7488, 32, parts=64), scr(7520, 32, parts=64)]
            NIT = NCH * 4
            PF = 3

            def st_of(idx):
                c, h = idx // 4, idx % 4
                if samp:
                    Cst = CSTv[idx % 4]
                    return (Cst, NALL.ap[:, idx * 4:idx * 4 + 4], NALL.tok, NBFA.ap[:, idx * 4:idx * 4 + 4], NBFA.tok)
                return (CSTv[h], NSTh[h].ap, NSTh[h].tok, NBFh[h].ap, NBFh[h].tok)

            def c3(Cst):
                return Cst.ap.rearrange("p (k e) -> p k e", k=4)

            def load_C(idx):
                Cst = CSTv[idx % 4]
                p.dma("sp", c3(Cst), sC[idx].rearrange("(k p) e -> p k e", p=128), writes=Cst.tok)

            def castC(idx):
                Cst = st_of(idx)[0]
                Cbf = CBFv[idx % 2]
                A("act", lambda e, Cbf=Cbf, Cst=Cst: e.copy(out=Cbf.ap, in_=Cst.ap), reads=Cst.tok, writes=Cbf.tok)

            def front(idx):
                c, h = idx // 4, idx % 4
                t0 = c * L
                k2 = idx % 2
                pa, pbk, pvv = k2, 2, 3
                D, Sd, kw, vb = s_D[k2], s_Sd[k2], s_kw[k2], s_vb[k2]
                qs = [qT[:, h * 4 + dk, t0:t0 + L] for dk in range(4)]
                ks_ = [kT[:, h * 4 + dk, t0:t0 + L] for dk in range(4)]
                vs = [vT[:, h * 4 + dk, t0:t0 + L] for dk in range(4)]
                for dk in range(4):
                    A("pe", lambda e, dk=dk, pa=pa, ks_=ks_, qs=qs: e.matmul(
                        psf(pa)[:L, 0:L], lhsT=ks_[dk], rhs=qs[dk], start=(dk == 0), stop=(dk == 3)),
                      reads=B4.tok + B5.tok, writes=PT(pa))
                for dk in range(4):
                    A("pe", lambda e, dk=dk, pbk=pbk, ks_=ks_: e.matmul(psf(pbk)[:L, dk * 128:(dk + 1) * 128], lhsT=ks_[dk], rhs=ident_bf, start=True, stop=True),
                      reads=B5.tok + CBT, writes=PT(pbk))
                for dk in range(4):
                    A("pe", lambda e, dk=dk, pvv=pvv, vs=vs: e.matmul(psf(pvv)[:L, dk * 128:(dk + 1) * 128], lhsT=vs[dk], rhs=ident_bf, start=True, stop=True),
                      reads=B3.tok + CBT, writes=PT(pvv))
                ccol = (c * 4 + 3) * 4 + h
                A("act", lambda e, D=D, h=h, t0=t0, ccol=ccol: e.activation(
                    out=D.ap[:L, :L], in_=LMv.ap[:L, h * T + t0:h * T + t0 + L], func=AF.Exp, bias=COLQ[:L, ccol:ccol + 1]),
                  reads=LMv.tok + COLQv.tok, writes=D.tok)
                A("dve", lambda e, pa=pa, D=D, Sd=Sd: e.scalar_tensor_tensor(
                    out=Sd.ap[:L, :L], in0=psf(pa)[:L, 0:L], scalar=KSCALE, in1=D.ap[:L, :L], op0=ALU.mult, op1=ALU.mult),
                  reads=PT(pa) + D.tok, writes=Sd.tok)
                cq = (c * 4) * 4 + h
                A("act", lambda e, pbk=pbk, kw=kw, cq=cq: e.activation(
                    out=kw.ap[:L, :512], in_=psf(pbk)[:L, 0:512], func=AF.Copy, scale=COLQ[:L, cq:cq + 1]),
                  reads=PT(pbk) + COLQv.tok, writes=kw.tok)
                A("act", lambda e, pvv=pvv, vb=vb: e.copy(out=vb.ap[:L, :512], in_=psf(pvv)[:L, 0:512]), reads=PT(pvv), writes=vb.tok)

            def mid(idx, part):
                c, h = idx // 4, idx % 4
                t0 = c * L
                k2 = idx % 2
                Cst, nst, nstt, nbf, nbft = st_of(idx)
                Cbf = CBFv[k2]
                Cb3 = Cbf.ap.rearrange("p (k e) -> p k e", k=4)
                pa, phq, phs = k2, 4, 5
                Sd, vb, hh, sm_ = s_Sd[k2], s_vb[k2], s_hh[k2], s_sm[k2]
                qs = [qT[:, h * 4 + dk, t0:t0 + L] for dk in range(4)]
                s_ = sm_.ap
                if part == "b":
                    cq_ = (c * 4) * 4 + h
                    wi_ = COLQ[:L, cq_ + 4:cq_ + 5]
                    em_ = COLQ[:L, cq_ + 8:cq_ + 9]
                    A("dve", lambda e, phs=phs, hh=hh: e.tensor_tensor(
                        out=hh.ap[:L, :512], in0=psf(phs)[:L, :512], in1=s_t1.ap[:L, :512], op=ALU.add),
                      reads=PT(phs) + s_t1.tok, writes=hh.tok)
                    A("dve", lambda e, s_=s_, hh=hh: e.bn_stats(out=s_[:L, 8:14], in_=hh.ap[:L, :512]), reads=hh.tok, writes=sm_.tok)
                    A("dve", lambda e, s_=s_: e.bn_aggr(out=s_[:L, 14:16], in_=s_[:L, 8:14]), reads=sm_.tok, writes=sm_.tok)
                    A("dve", lambda e, s_=s_, wi_=wi_: e.scalar_tensor_tensor(out=s_[:L, 2:3], in0=s_[:L, 0:1], scalar=wi_, in1=s_[:L, 1:2],
                                                                            op0=ALU.mult, op1=ALU.add),
                      reads=sm_.tok + COLQv.tok, writes=sm_.tok)
                    A("dve", lambda e, s_=s_: e.scalar_tensor_tensor(out=s_[:L, 3:4], in0=s_[:L, 2:3], scalar=-1.0, in1=s_[:L, 2:3],
                                                                     op0=ALU.mult, op1=ALU.max),
                      reads=sm_.tok, writes=sm_.tok)
                    A("dve", lambda e, s_=s_, em_=em_: e.tensor_tensor(out=s_[:L, 3:4], in0=s_[:L, 3:4], in1=em_, op=ALU.max),
                      reads=sm_.tok + COLQv.tok, writes=sm_.tok)
                    A("dve", lambda e, s_=s_: e.scalar_tensor_tensor(out=s_[:L, 6:7], in0=s_[:L, 3:4], scalar=LN_EPS, in1=s_[:L, 3:4],
                                                                     op0=ALU.mult, op1=ALU.mult),
                      reads=sm_.tok, writes=sm_.tok)
                    A("act", lambda e, s_=s_: e.activation(out=s_[:L, 16:17], in_=s_[:L, 15:16], func=AF.Ln, bias=s_[:L, 6:7]), reads=sm_.tok, writes=sm_.tok)
                    A("act", lambda e, s_=s_: e.activation(out=s_[:L, 17:18], in_=s_[:L, 16:17], func=AF.Exp, scale=-0.5), reads=sm_.tok, writes=sm_.tok)
                    return
                if part == "c":
                    A("dve", lambda e, s_=s_, hh=hh, h=h: e.tensor_scalar(
                        out=s_hn.ap[:L, h * 512:(h + 1) * 512], in0=hh.ap[:L, :512], scalar1=s_[:L, 14:15], scalar2=s_[:L, 17:18],
                        op0=ALU.subtract, op1=ALU.mult),
                      reads=hh.tok + sm_.tok, writes=s_hn.tok)
                    return
                for dk in range(4):
                    A("pe", lambda e, dk=dk, phq=phq, qs=qs, Cb3=Cb3: e.matmul(
                        psf(phq)[:L, :512], lhsT=qs[dk], rhs=Cb3[:, dk, :], start=(dk == 0), stop=(dk == 3)),
                      reads=B4.tok + Cbf.tok, writes=PT(phq))
                for dk in range(4):
                    A("pe", lambda e, dk=dk, pa=pa, qs=qs, nbf=nbf: e.matmul(
                        psf(pa)[:L, 128:129], lhsT=qs[dk], rhs=nbf[:, dk:dk + 1], start=(dk == 0), stop=(dk == 3)),
                      reads=B4.tok + nbft, writes=PT(pa))
                A("pe", lambda e, phs=phs, Sd=Sd, vb=vb: e.matmul(psf(phs)[:L, :512], lhsT=Sd.ap[:L, :L], rhs=vb.ap[:L, :512], start=True, stop=True),
                  reads=Sd.tok + vb.tok, writes=PT(phs))
                A("pe", lambda e, pa=pa, Sd=Sd: e.matmul(psf(pa)[:L, 129:130], lhsT=Sd.ap[:L, :L], rhs=ones_bf[:L, 0:1], start=True, stop=True),
                  reads=Sd.tok + CBT, writes=PT(pa))
                s_ = sm_.ap
                cq = (c * 4) * 4 + h
                wi = COLQ[:L, cq + 4:cq + 5]
                em = COLQ[:L, cq + 8:cq + 9]
                A("act", lambda e, phq=phq, wi=wi: e.activation(out=s_t1.ap[:L, :512], in_=psf(phq)[:L, :512], func=AF.Copy, scale=wi),
                  reads=PT(phq) + COLQv.tok, writes=s_t1.tok)
                A("act", lambda e, pa=pa, s_=s_: e.copy(out=s_[:L, 0:2], in_=psf(pa)[:L, 128:130]), reads=PT(pa), writes=sm_.tok)

            def back(idx, part):
                c, h = idx // 4, idx % 4
                k2 = idx % 2
                Cst, nst, nstt, nbf, nbft = st_of(idx)
                C3 = c3(Cst)
                kw, vb = s_kw[k2], s_vb[k2]
                pa = (idx + 1) % 2
                dec = DECB[:, idx:idx + 1]
                for dk in ((0, 1) if part == "a" else (2, 3)):
                    pk = 6 + dk % 2
                    A("pe", lambda e, dk=dk, pk=pk, kw=kw, vb=vb: e.matmul(
                        psf(pk)[:, :512], lhsT=kw.ap[:L, dk * 128:(dk + 1) * 128], rhs=vb.ap[:L, :512], start=True, stop=True),
                      reads=kw.tok + vb.tok, writes=PT(pk))
                    A("dve", lambda e, dk=dk, pk=pk, C3=C3, dec=dec: e.scalar_tensor_tensor(
                        out=C3[:, dk, :], in0=C3[:, dk, :], scalar=dec, in1=psf(pk)[:, :512], op0=ALU.mult, op1=ALU.add),
                      reads=PT(pk) + Cst.tok + DECBv.tok, writes=Cst.tok)
                if part == "a":
                    return
                for dk in range(4):
                    A("pe", lambda e, dk=dk, pa=pa, kw=kw: e.matmul(
                        psf(pa)[:, 132 + dk:133 + dk], lhsT=kw.ap[:L, dk * 128:(dk + 1) * 128], rhs=ones_bf[:L, 0:1], start=True, stop=True),
                      reads=kw.tok + CBT, writes=PT(pa))
                A("dve", lambda e, pa=pa, nst=nst, dec=dec: e.scalar_tensor_tensor(
                    out=nst, in0=nst, scalar=dec, in1=psf(pa)[:, 132:136], op0=ALU.mult, op1=ALU.add),
                  reads=PT(pa) + nstt + DECBv.tok, writes=nstt)
                if samp:
                    p.dma("pool", Cs[idx].rearrange("(k p) e -> p k e", p=128), C3, reads=Cst.tok, is_out=True)
                else:
                    A("pool", lambda e, nbf=nbf, nst=nst: e.tensor_copy(out=nbf, in_=nst), reads=nstt, writes=nbft)
                    if last_p and c == NCH - 1:
                        p.dma("sp", Cp[h].rearrange("(k p) e -> p k e", p=128), C3, reads=Cst.tok, is_out=True)

            def outstage(c):
                t0 = c * L
                po = 2
                if samp:
                    for fc in range(16):
                        A("pe", lambda e, fc=fc, po=po: e.matmul(psf(po)[:, fc * L:(fc + 1) * L], lhsT=s_hn.ap[:L, fc * 128:(fc + 1) * 128],
                                                                 rhs=ident_bf[:L, :L], start=True, stop=True),
                          reads=s_hn.tok + CBT, writes=PT(po))
                    A("act", lambda e, po=po, t0=t0: e.copy(out=ogT[:, :, t0:t0 + L], in_=psf(po)[:, :16 * L].rearrange("p (c l) -> p c l", l=L)),
                      reads=PT(po), writes=B1.tok)
                else:
                    for fc in range(16):
                        A("pe", lambda e, fc=fc, po=po: e.transpose(out=psh(po)[:, fc * L:(fc + 1) * L], in_=s_hn.ap[:L, fc * 128:(fc + 1) * 128],
                                                                    identity=ident_bf[:L, :L]),
                          reads=s_hn.tok + CBT, writes=PT(po))
                    A("act", lambda e, po=po, t0=t0: e.copy(out=ogT[:, :, t0:t0 + L], in_=psh(po)[:, :16 * L].rearrange("p (c l) -> p c l", l=L)),
                      reads=PT(po), writes=B1.tok)

            if samp:
                for i in range(min(PF, NIT)):
                    load_C(i)
            castC(0)
            if NIT > 1 and not samp:
                castC(1)
            front(0)
            for idx in range(NIT):
                mid(idx, "a")
                if (not samp) and idx + 2 < NIT:
                    castC(idx + 2)
                if samp and idx + 1 < NIT:
                    castC(idx + 1)
                if idx > 0 and idx % 4 == 0:
                    outstage(idx // 4 - 1)
                if idx > 0:
                    back(idx - 1, "a")
                mid(idx, "b")
                if idx > 0:
                    back(idx - 1, "b")
                if idx + 1 < NIT:
                    front(idx + 1)
                mid(idx, "c")
                if samp and idx + PF < NIT:
                    load_C(idx + PF)
            outstage(NCH - 1)
            back(NIT - 1, "a")
            back(NIT - 1, "b")

            if samp:
                pi = newps()
                for a in range(2):
                    A("pe", lambda e, a=a, pi=pi: e.transpose(out=psf(pi)[:, a * 128:(a + 1) * 128], in_=NALL.ap[:, a * 128:(a + 1) * 128], identity=ident_f),
                      reads=NALL.tok + CT, writes=PT(pi))
                A("act", lambda e, pi=pi: e.copy(out=NROWS.ap, in_=psf(pi)[:, :256]), reads=PT(pi), writes=NROWS.tok)
                p.dma("sp", ns_o.rearrange("(a p) d -> p a d", p=128), nrows3, reads=NROWS.tok, is_out=True)
            elif last_p:
                pi = newps()
                for h in range(4):
                    A("dve", lambda e, h=h: e.tensor_copy(out=NSTC.ap[:, h * 4:h * 4 + 4], in_=NSTh[h].ap), reads=NSTh[h].tok, writes=NSTC.tok)
                A("pe", lambda e, pi=pi: e.matmul(psf(pi)[:16, 0:128], lhsT=NSTC.ap, rhs=ident_f, start=True, stop=True), reads=NSTC.tok + CT, writes=PT(pi))
                NO = scr(0, 128, parts=16)
                A("act", lambda e, pi=pi: e.copy(out=NO.ap[:16, :], in_=psf(pi)[:16, 0:128]), reads=PT(pi), writes=NO.tok)
                p.dma("sp", np_o, NO.ap[:16, :], reads=NO.tok, is_out=True)

            print("mark z", len(p.ops))
            for it in range(8):
                w = wget()
                wv = w.ap.rearrange("p (o k n) -> p o k n", o=2, k=8)
                for o2 in range(2):
                    oc = it * 2 + o2
                    pi = newps()
                    for k in range(8):
                        A("pe", lambda e, wv=wv, o2=o2, k=k, pi=pi: e.matmul(
                            psf(pi)[:, :T], lhsT=wv[:, o2, k, :], rhs=hT[:, k, :], start=(k == 0), stop=(k == 7)),
                          reads=w.tok + HT.tok, writes=PT(pi))
                    A("act", lambda e, oc=oc, pi=pi: e.activation(out=szT[:, oc, :], in_=psf(pi)[:, :T], func=AF.Silu),
                      reads=PT(pi), writes=ctok(3, oc))
            s_o1 = [scr(0, 512), scr(512, 512)]
            for fc in range(16):
                o1 = s_o1[fc % 2]
                A("dve", lambda e, fc=fc, o1=o1: e.tensor_scalar(out=o1.ap[:, :T], in0=xcT[:, fc, :], scalar1=PAR[:, PC_SKIP + fc:PC_SKIP + fc + 1],
                                                                scalar2=None, op0=ALU.mult),
                  reads=ctok(1, fc) + CT, writes=o1.tok)
                A("dve", lambda e, fc=fc, o1=o1: e.scalar_tensor_tensor(out=o1.ap[:, :T], in0=ogT[:, fc, :], scalar=PAR[:, PC_ONORM + fc:PC_ONORM + fc + 1],
                                                                       in1=o1.ap[:, :T], op0=ALU.mult, op1=ALU.add),
                  reads=ctok(0, fc) + CT + o1.tok, writes=o1.tok)
                A("dve", lambda e, fc=fc, o1=o1: e.tensor_tensor(out=ogT[:, fc, :], in0=o1.ap[:, :T], in1=szT[:, fc, :], op=ALU.mult),
                  reads=o1.tok + ctok(3, fc), writes=ctok(0, fc))
            outproj(ogT, 0)

            print("mark final", len(p.ops))
            load_nw(2)
            for b in range(NB):
                k = b % 2
                rms_stats(b, k)
                ost = s_ost[k]
                A("dve", lambda e, b=b, k=k, ost=ost: e.scalar_tensor_tensor(
                    out=ost.ap, in0=Xt[:, b, :], scalar=s_ss[k].ap[:, 2:3], in1=s_nw.ap, op0=ALU.mult, op1=ALU.mult),
                  reads=X.tok + s_ss[k].tok + s_nw.tok, writes=ost.tok)
                if samp:
                    p.dma("sp", ys, ost.ap, reads=ost.tok, is_out=True)
                else:
                    r0 = ti * 512 + b * 128
                    p.dma("sp", yp[r0:r0 + 128, :], ost.ap, reads=ost.tok, is_out=True)

        for ti in range(n_ptiles):
            run_tile("p", ti)
        if do_sample:
            run_tile("s", 0)
        print("ops", len(p.ops), "arena scratch words", SCR_WORDS)
        p.emit(sems, dsems)
    return nc


def _host_prep(inp):
    f = np.float32
    a_w_in = np.asarray(inp["a_w_in"], f)[0]
    a_w_out = np.asarray(inp["a_w_out"], f)[0]
    b_w_in = np.asarray(inp["b_w_in"], f)[0]
    b_w_out = np.asarray(inp["b_w_out"], f)[0]
    items = []
    A5 = a_w_in.reshape(8, 128, 4, 16, 128)
    t = A5.transpose(3, 2, 1, 0, 4)
    for j in range(16):
        for half in range(2):
            blk = t[j, half * 2:half * 2 + 2]
            items.append(blk.transpose(1, 0, 2, 3).reshape(128, ITEM))
    def outw(w):
        W = w.reshape(8, 2, 128, 1024)
        return [W[i].transpose(1, 0, 2).reshape(128, ITEM) for i in range(8)]
    items += outw(a_w_out)
    def inw(wcols):
        W = wcols.reshape(8, 128, 8, 2, 128)
        return [W[:, :, i].transpose(1, 2, 0, 3).reshape(128, ITEM) for i in range(8)]
    items += inw(b_w_in[:, :2048])
    def sqw(w):
        W = w.reshape(16, 128, 16, 128)
        return [W[:, :, oc].transpose(1, 0, 2).reshape(128, ITEM) for oc in range(16)]
    items += sqw(np.asarray(inp["b_w_v"], f)[0])
    items += sqw(np.asarray(inp["b_w_q"], f)[0])
    items += sqw(np.asarray(inp["b_w_k"], f)[0])
    items += inw(b_w_in[:, 2048:])
    items += outw(b_w_out)
    assert len(items) == NITEMS
    wst = np.ascontiguousarray(np.stack(items, 0))

    par = np.zeros((128, NPAR), f)
    def fmaj(v):
        return np.asarray(v, f).reshape(16, 128).T
    ca = np.asarray(inp["a_conv_w"], f)[0]
    cb = np.asarray(inp["b_conv_w"], f)[0]
    for d in range(3):
        par[:, PC_CONVA + d * 16:PC_CONVA + (d + 1) * 16] = fmaj(ca[d])
    for d in range(4):
        par[:, PC_CONVB + d * 16:PC_CONVB + (d + 1) * 16] = fmaj(cb[d])
    par[:, PC_CB:PC_CB + 16] = fmaj(np.asarray(inp["b_conv_b"], f)[0])
    par[:, PC_SKIP:PC_SKIP + 16] = fmaj(np.asarray(inp["b_skip"], f)[0])
    par[:, PC_ONORM:PC_ONORM + 16] = fmaj(np.asarray(inp["b_onorm_w"], f)[0])
    par[:, PC_IDENT:PC_IDENT + 128] = np.eye(128, dtype=f)
    jj, ii = np.meshgrid(np.arange(64), np.arange(64), indexing="ij")
    par[:64, PC_NEG:PC_NEG + 64] = np.where(ii >= jj, 0.0, -30000.0).astype(f)
    for h in range(4):
        par[h, PC_EH + h * 64:PC_EH + (h + 1) * 64] = 1.0
    par[:4, PC_EYE4:PC_EYE4 + 4] = np.eye(4, dtype=f)
    par[:, PC_ONES:PC_ONES + 128] = 1.0
    for h in range(4):
        par[4 + h, PC_SEL + h] = 1.0
    par[:8, PC_BIF] = np.asarray(inp["b_b_if"], f)[0]
    tt = np.arange(512)
    par[:4, PC_RMASK:PC_RMASK + 512] = (tt % 64 != 0).astype(f)[None]
    par[:4, PC_AMASK:PC_AMASK + 512] = np.where(tt % 64 == 0, -1e30, 0.0).astype(f)[None]
    ts = np.arange(128)
    par[:4, PC_RMASK_S:PC_RMASK_S + 128] = (ts % 8 != 0).astype(f)[None]
    par[:4, PC_AMASK_S:PC_AMASK_S + 128] = np.where(ts % 8 == 0, -1e30, 0.0).astype(f)[None]

    cbf = np.zeros((128, NCB), f)
    cbf[:, CB_IDENT:CB_IDENT + 128] = np.eye(128, dtype=f)
    cbf[:, CB_ONES:CB_ONES + 8] = 1.0
    wif = np.asarray(inp["b_w_if"], f)[0]
    cbf[:, CB_WIF:CB_WIF + 384] = wif.reshape(48, 128, 8).transpose(1, 0, 2).reshape(128, 384)

    nw = np.asarray(inp["norm_w"], f)
    bc = np.stack([np.broadcast_to(nw[0], (128, DM)), np.broadcast_to(nw[1], (128, DM)),
                   np.broadcast_to(np.asarray(inp["final_norm_w"], f), (128, DM))], 0)
    bc = np.ascontiguousarray(bc)
    return wst, par, cbf, bc


_CACHE = {}


def kernel(**inp):
    f = np.float32
    wst, par, cbf, bc = _host_prep(inp)
    if "nc" not in _CACHE:
        _CACHE["nc"] = build_program()
    nc = _CACHE["nc"]
    xp = np.asarray(inp["x_prompt"], f)
    xs = np.asarray(inp["x_sample"], f)
    sca = np.asarray(inp["state_conv_a"], f)[0]
    scb = np.asarray(inp["state_conv_b"], f)[0]
    sC = np.asarray(inp["state_C"], f)[0]
    sn = np.asarray(inp["state_n"], f)[0]
    sm = np.asarray(inp["state_m"], f)[0]
    in_maps = []
    for c in range(NCORES):
        s0, s1 = 16 * c, 16 * c + 16
        in_maps.append({
            "xp": np.ascontiguousarray(xp[c]),
            "xs": np.ascontiguousarray(xs[s0:s1].reshape(128, DM)),
            "sca": np.ascontiguousarray(sca[s0:s1].reshape(32, AW)),
            "scb": np.ascontiguousarray(scb[s0:s1].reshape(48, AW)),
            "sC": np.ascontiguousarray(sC[s0:s1].reshape(64, 512, 512)),
            "sn": np.ascontiguousarray(sn[s0:s1].reshape(256, 128)),
            "sm": np.ascontiguousarray(sm[s0:s1].T),
            "wst": wst, "par": par, "cbf": cbf, "bc": bc,
        })
    res = run_bass_kernel_spmd(nc, in_maps, core_ids=list(range(NCORES)))
    R = res.results
    def cat(name, shp):
        return np.stack([np.asarray(R[c][name], f).reshape(shp) for c in range(NCORES)], 0)
    y_p = cat("yp", (2048, DM))
    y_s = cat("ys", (16, 8, DM)).reshape(128, 8, DM)
    ca_p = cat("cap", (2, AW))[None]
    ca_s = cat("cas", (16, 2, AW)).reshape(128, 2, AW)[None]
    cb_p = cat("cbp", (3, AW))[None]
    cb_s = cat("cbs", (16, 3, AW)).reshape(128, 3, AW)[None]
    C_p = cat("Cp", (4, 512, 512))[None]
    C_s = cat("Cs", (16, 4, 512, 512)).reshape(128, 4, 512, 512)[None]
    n_p = cat("np", (4, 512))[None]
    n_s = cat("ns", (16, 4, 512)).reshape(128, 4, 512)[None]
    m_p = cat("mp", (4,))[None]
    m_s = np.stack([np.asarray(R[c]["ms"], f).reshape(4, 16).T for c in range(NCORES)], 0).reshape(128, 4)[None]
    return (y_p, y_s, ca_p, ca_s, cb_p, cb_s, C_p, C_s, n_p, n_s, m_p, m_s)
```
